# Optimizing a Trainium2 kernel written in Bass

```python
import math
import jax, jax.numpy as jnp
from jax import lax
import numpy as np

D_MODEL = 1024
BATCH = 8
SEQ = 2048
DEPTH = 1
DEC_BATCH = 128
DEC_SEQ = 1
PAST_LEN = 16384
PAGE_SIZE = 128

GM_CHUNK = 128
GM_GROUPS = 4
GM_GROUP_DIM = 128
GM_WIDTH = GM_GROUPS * GM_GROUP_DIM
SSD_HEADS = 16
SSD_HEAD_DIM = 64
SSD_INNER = SSD_HEADS * SSD_HEAD_DIM
SSD_GROUPS = 2
SSD_STATE = 128
SSD_CONV = 4
SSD_CHUNK = 128
SSD_CONV_DIM = SSD_INNER + 2 * SSD_GROUPS * SSD_STATE
MEM_LEN = 256
XA_HEADS = 4
XA_HEAD_DIM = 128
XA_WIDTH = XA_HEADS * XA_HEAD_DIM
N_BRANCH = 3
D_FF = 4 * D_MODEL
EPS = 1e-6
IN_PROJ_DIM = 2 * GM_WIDTH + SSD_INNER + SSD_CONV_DIM + SSD_HEADS + XA_WIDTH + N_BRANCH * D_MODEL

kernel_name = "hybrid_gmlp_ssd_memxattn_decode_step"


def rmsnorm(x, w):
    xf = x.astype(jnp.float32)
    y = xf * lax.rsqrt(jnp.mean(xf * xf, axis=-1, keepdims=True) + EPS)
    return (y * w.astype(jnp.float32)).astype(x.dtype)


def layernorm(x, w, b):
    xf = x.astype(jnp.float32)
    mu = jnp.mean(xf, axis=-1, keepdims=True)
    xc = xf - mu
    y = xc * lax.rsqrt(jnp.mean(xc * xc, axis=-1, keepdims=True) + EPS)
    return (y * w.astype(jnp.float32) + b.astype(jnp.float32)).astype(x.dtype)


def pad_seq(t, mult):
    n_pad = (-t.shape[1]) % mult
    return jnp.pad(t, [(0, 0), (0, n_pad)] + [(0, 0)] * (t.ndim - 2))


def split_points():
    sizes = (2 * GM_WIDTH, SSD_INNER, SSD_CONV_DIM, SSD_HEADS, XA_WIDTH, N_BRANCH * D_MODEL)
    pts, acc = [], 0
    for s in sizes[:-1]:
        acc += s
        pts.append(acc)
    return pts


def gmlp_mix(uv, ln_w, ln_b, w_s, b_s):
    u, v = jnp.split(uv, 2, axis=-1)
    v = layernorm(v, ln_w, ln_b)
    b, L, _ = v.shape
    vc = pad_seq(v, GM_CHUNK)
    n_c = vc.shape[1] // GM_CHUNK
    vc = vc.reshape(b, n_c, GM_CHUNK, GM_GROUPS, GM_GROUP_DIM)
    causal = jnp.tril(jnp.ones((GM_CHUNK, GM_CHUNK), dtype=bool))
    ws = jnp.where(causal[None], w_s, 0).astype(v.dtype)
    mixed = jnp.einsum('gts,bcsgd->bctgd', ws, vc) + b_s.T.astype(v.dtype)[None, None, :, :, None]
    mixed = mixed.reshape(b, n_c * GM_CHUNK, GM_WIDTH)[:, :L]
    return u * mixed, v


def causal_conv(prev, u, w, bias):
    full = jnp.concatenate([prev.astype(u.dtype), u], axis=1)
    L = u.shape[1]
    out = full[:, 0:L] * w[0]
    for k in range(1, SSD_CONV):
        out = out + full[:, k:k + L] * w[k]
    return out + bias, full[:, L:]


def ssd_chunked(x, dt, A, bm, cm, h0):
    b, L = x.shape[:2]
    E = SSD_HEADS // SSD_GROUPS
    Q = SSD_CHUNK
    x, dt, bm, cm = (pad_seq(t, Q) for t in (x, dt, bm, cm))
    n_c = x.shape[1] // Q
    x = x.reshape(b, n_c, Q, SSD_GROUPS, E, SSD_HEAD_DIM)
    dt = dt.reshape(b, n_c, Q, SSD_GROUPS, E)
    bm = bm.reshape(b, n_c, Q, SSD_GROUPS, SSD_STATE)
    cm = cm.reshape(b, n_c, Q, SSD_GROUPS, SSD_STATE)
    acum = jnp.cumsum(dt * A.reshape(SSD_GROUPS, E), axis=2)
    xdt = x * dt[..., None]
    seg = acum[:, :, :, None] - acum[:, :, None, :]
    causal = jnp.tril(jnp.ones((Q, Q), dtype=bool))[:, :, None, None]
    decay = jnp.exp(jnp.where(causal, seg, -jnp.inf))
    cb = jnp.einsum('bcign,bcjgn->bcijg', cm, bm)
    y_diag = jnp.einsum('bcijg,bcijge,bcjgep->bcigep', cb, decay, xdt)
    decay_end = jnp.exp(acum[:, :, -1:] - acum)
    states = jnp.einsum('bcjgn,bcjge,bcjgep->bcgepn', bm, decay_end, xdt)
    chunk_decay = jnp.exp(acum[:, :, -1])

    def step(h, inp):
        s, d = inp
        return h * d[..., None, None] + s, h

    h_init = h0.reshape(b, SSD_GROUPS, E, SSD_HEAD_DIM, SSD_STATE)
    h_final, h_prev = lax.scan(step, h_init, (jnp.moveaxis(states, 1, 0), jnp.moveaxis(chunk_decay, 1, 0)))
    h_prev = jnp.moveaxis(h_prev, 0, 1)
    y_off = jnp.einsum('bcign,bcgepn,bcige->bcigep', cm, h_prev, jnp.exp(acum))
    y = (y_diag + y_off).reshape(b, n_c * Q, SSD_HEADS, SSD_HEAD_DIM)[:, :L]
    return y, h_final.reshape(b, SSD_HEADS, SSD_HEAD_DIM, SSD_STATE)


def mem_kv(mem, mem_norm_w, w_k, w_v):
    b = mem.shape[0]
    mn = rmsnorm(mem, mem_norm_w)
    k = (mn @ w_k).reshape(b, MEM_LEN, XA_HEADS, XA_HEAD_DIM)
    v = (mn @ w_v).reshape(b, MEM_LEN, XA_HEADS, XA_HEAD_DIM)
    return k, v


def mem_attend(q, k, v):
    s = jnp.einsum('blhd,bmhd->bhlm', q, k).astype(jnp.float32) * (XA_HEAD_DIM ** -0.5)
    p = jax.nn.softmax(s, axis=-1).astype(v.dtype)
    o = jnp.einsum('bhlm,bmhd->blhd', p, v)
    return o.reshape(q.shape[0], q.shape[1], XA_WIDTH)


def decoder_block(h, conv_prev, ssm_prev, mem_k, mem_v,
                  norm_mix_w, w_in, gm_ln_w, gm_ln_b, gm_ws, gm_bs,
                  conv_w, conv_b, dt_bias, a_log, d_skip, ssd_norm_w,
                  w_br_gm, w_br_ssd, w_br_xa, w_out, norm_ffn_w, w_up, w_down):
    b, L, _ = h.shape
    f32 = jnp.float32
    xn = rmsnorm(h, norm_mix_w)
    proj = xn @ w_in
    gm_uv, z, xbc, dt_raw, q, gate_raw = jnp.split(proj, split_points(), axis=-1)
    gm_out, gm_v = gmlp_mix(jax.nn.gelu(gm_uv, approximate=False), gm_ln_w, gm_ln_b, gm_ws, gm_bs)
    xbc, conv_new = causal_conv(conv_prev, xbc, conv_w, conv_b)
    xbc = jax.nn.silu(xbc)
    xs, bm, cm = jnp.split(xbc, [SSD_INNER, SSD_INNER + SSD_GROUPS * SSD_STATE], axis=-1)
    xs_h = xs.reshape(b, L, SSD_HEADS, SSD_HEAD_DIM).astype(f32)
    dt = jax.nn.softplus(dt_raw.astype(f32) + dt_bias.astype(f32))
    A = -jnp.exp(a_log.astype(f32))
    y, ssm_new = ssd_chunked(xs_h, dt, A,
                             bm.reshape(b, L, SSD_GROUPS, SSD_STATE).astype(f32),
                             cm.reshape(b, L, SSD_GROUPS, SSD_STATE).astype(f32),
                             ssm_prev.astype(f32))
    y = y + xs_h * d_skip.astype(f32)[:, None]
    y = y.reshape(b, L, SSD_INNER).astype(h.dtype) * jax.nn.silu(z)
    y = rmsnorm(y.reshape(b, L, SSD_GROUPS, SSD_INNER // SSD_GROUPS),
                ssd_norm_w.reshape(SSD_GROUPS, SSD_INNER // SSD_GROUPS)).reshape(b, L, SSD_INNER)
    xa = mem_attend(q.reshape(b, L, XA_HEADS, XA_HEAD_DIM), mem_k, mem_v)
    gates = jax.nn.sigmoid(gate_raw.astype(f32)).astype(h.dtype).reshape(b, L, N_BRANCH, D_MODEL)
    merged = (gates[:, :, 0] * (gm_out @ w_br_gm)
              + gates[:, :, 1] * (y @ w_br_ssd)
              + gates[:, :, 2] * (xa @ w_br_xa))
    h = h + merged @ w_out
    hn = rmsnorm(h, norm_ffn_w)
    h = h + jnp.square(jax.nn.relu(hn @ w_up)) @ w_down
    return h, conv_new, ssm_new, gm_v


def setup_inputs(seed: int = 0) -> dict:
    key = jax.random.key(seed)
    ks = jax.random.split(key, 40)
    f32 = jnp.float32

    def nrm(k, shape, scale):
        return jax.random.normal(k, shape, f32) * scale

    Ld = DEPTH
    dt0 = jnp.exp(jax.random.uniform(ks[12], (Ld, SSD_HEADS), f32, math.log(1e-3), math.log(1e-1)))
    return {
        "x_prompt": nrm(ks[0], (BATCH, SEQ, D_MODEL), 1.0),
        "x_sample": nrm(ks[1], (DEC_BATCH, DEC_SEQ, D_MODEL), 1.0),
        "mem_prompt": nrm(ks[2], (BATCH, MEM_LEN, D_MODEL), 1.0),
        "cache_mem_k": nrm(ks[3], (Ld, DEC_BATCH, MEM_LEN, XA_HEADS, XA_HEAD_DIM), 1.0),
        "cache_mem_v": nrm(ks[4], (Ld, DEC_BATCH, MEM_LEN, XA_HEADS, XA_HEAD_DIM), 1.0),
        "state_conv": nrm(ks[5], (Ld, DEC_BATCH, SSD_CONV - 1, SSD_CONV_DIM), 1.0),
        "state_ssm": nrm(ks[6], (Ld, DEC_BATCH, SSD_HEADS, SSD_HEAD_DIM, SSD_STATE), 0.1),
        "norm_mix_w": 1.0 + nrm(ks[7], (Ld, D_MODEL), 0.02),
        "w_in": nrm(ks[8], (Ld, D_MODEL, IN_PROJ_DIM), D_MODEL ** -0.5),
        "gm_ln_w": 1.0 + nrm(ks[9], (Ld, GM_WIDTH), 0.02),
        "gm_ln_b": nrm(ks[10], (Ld, GM_WIDTH), 0.02),
        "gm_ws": nrm(ks[11], (Ld, GM_GROUPS, GM_CHUNK, GM_CHUNK), 0.5 * GM_CHUNK ** -0.5),
        "gm_bs": 1.0 + nrm(ks[13], (Ld, GM_GROUPS, GM_CHUNK), 0.02),
        "conv_w": nrm(ks[14], (Ld, SSD_CONV, SSD_CONV_DIM), SSD_CONV ** -0.5),
        "conv_b": nrm(ks[15], (Ld, SSD_CONV_DIM), 0.02),
        "dt_bias": dt0 + jnp.log(-jnp.expm1(-dt0)),
        "a_log": jnp.log(jax.random.uniform(ks[16], (Ld, SSD_HEADS), f32, 1.0, 16.0)),
        "d_skip": 1.0 + nrm(ks[17], (Ld, SSD_HEADS), 0.02),
        "ssd_norm_w": 1.0 + nrm(ks[18], (Ld, SSD_INNER), 0.02),
        "mem_norm_w": 1.0 + nrm(ks[19], (Ld, D_MODEL), 0.02),
        "w_mem_k": nrm(ks[20], (Ld, D_MODEL, XA_WIDTH), D_MODEL ** -0.5),
        "w_mem_v": nrm(ks[21], (Ld, D_MODEL, XA_WIDTH), D_MODEL ** -0.5),
        "w_br_gm": nrm(ks[22], (Ld, GM_WIDTH, D_MODEL), GM_WIDTH ** -0.5),
        "w_br_ssd": nrm(ks[23], (Ld, SSD_INNER, D_MODEL), SSD_INNER ** -0.5),
        "w_br_xa": nrm(ks[24], (Ld, XA_WIDTH, D_MODEL), XA_WIDTH ** -0.5),
        "w_out": nrm(ks[25], (Ld, D_MODEL, D_MODEL), D_MODEL ** -0.5),
        "norm_ffn_w": 1.0 + nrm(ks[26], (Ld, D_MODEL), 0.02),
        "w_up": nrm(ks[27], (Ld, D_MODEL, D_FF), D_MODEL ** -0.5),
        "w_down": nrm(ks[28], (Ld, D_FF, D_MODEL), D_FF ** -0.5),
        "norm_final_w": 1.0 + nrm(ks[29], (D_MODEL,), 0.02),
    }


def reference(x_prompt, x_sample, mem_prompt, cache_mem_k, cache_mem_v, state_conv, state_ssm,
              norm_mix_w, w_in, gm_ln_w, gm_ln_b, gm_ws, gm_bs, conv_w, conv_b, dt_bias, a_log,
              d_skip, ssd_norm_w, mem_norm_w, w_mem_k, w_mem_v, w_br_gm, w_br_ssd, w_br_xa, w_out,
              norm_ffn_w, w_up, w_down, norm_final_w):
    hp, hs = x_prompt, x_sample
    mk_p, mv_p, cv_p, ss_p, cv_s, ss_s, gv_s = [], [], [], [], [], [], []
    for l in range(DEPTH):
        layer_w = (norm_mix_w[l], w_in[l], gm_ln_w[l], gm_ln_b[l], gm_ws[l], gm_bs[l],
                   conv_w[l], conv_b[l], dt_bias[l], a_log[l], d_skip[l], ssd_norm_w[l],
                   w_br_gm[l], w_br_ssd[l], w_br_xa[l], w_out[l], norm_ffn_w[l], w_up[l], w_down[l])
        mk, mv = mem_kv(mem_prompt, mem_norm_w[l], w_mem_k[l], w_mem_v[l])
        conv0 = jnp.zeros((hp.shape[0], SSD_CONV - 1, SSD_CONV_DIM), hp.dtype)
        ssm0 = jnp.zeros((hp.shape[0], SSD_HEADS, SSD_HEAD_DIM, SSD_STATE), jnp.float32)
        hp, cvp, ssp, _ = decoder_block(hp, conv0, ssm0, mk, mv, *layer_w)
        hs, cvs, sss, gvs = decoder_block(hs, state_conv[l], state_ssm[l],
                                          cache_mem_k[l], cache_mem_v[l], *layer_w)
        mk_p.append(mk); mv_p.append(mv); cv_p.append(cvp); ss_p.append(ssp)
        cv_s.append(cvs); ss_s.append(sss); gv_s.append(gvs)
    y_prompt = rmsnorm(hp, norm_final_w)
    y_sample = rmsnorm(hs, norm_final_w)
    mem_k_prompt = jnp.stack(mk_p)
    mem_v_prompt = jnp.stack(mv_p)
    conv_prompt = jnp.stack(cv_p)
    ssm_prompt = jnp.stack(ss_p)
    conv_sample = jnp.stack(cv_s)
    ssm_sample = jnp.stack(ss_s)
    gmlp_v_sample = jnp.stack(gv_s)
    return (y_prompt, y_sample, mem_k_prompt, mem_v_prompt, conv_prompt, ssm_prompt,
            conv_sample, ssm_sample, gmlp_v_sample)
```

```python
import numpy as np
from contextlib import ExitStack
import concourse.bass as bass
import concourse.mybir as mybir
from concourse.bass_utils import run_bass_kernel_spmd

F32 = mybir.dt.float32
BF16 = mybir.dt.bfloat16
AF = mybir.ActivationFunctionType
ALU = mybir.AluOpType
AX = mybir.AxisListType
ESZ = {F32: 4, BF16: 2}
PAGE = 256
COMPUTE = ("pe", "act", "dve", "pool")
EPS = 1e-6
NT = 2048
NS = 16
NTOK = NT + NS
NCH = 16


class Op:
    __slots__ = ("eng", "fn", "deps", "signal", "is_dma", "sem", "val", "q")

    def __init__(self, eng, fn, is_dma):
        self.eng = eng; self.fn = fn; self.deps = set(); self.signal = False
        self.is_dma = is_dma; self.sem = None; self.val = 0; self.q = None


class Sched:
    def __init__(self, nc, ndma_sems=24):
        self.nc = nc
        self.ops = {e: [] for e in ("pe", "act", "dve", "pool", "sp")}
        self.last_w = {}
        self.rd = {}
        self.ndma = ndma_sems
        self.dma_count = {"sp": 0, "act": 0, "pool": 0}
        self.dma_ops = []

    @staticmethod
    def pages(ap):
        sp = str(ap.space).upper()
        if "DRAM" in sp or "HBM" in sp:
            return ()
        a = ap.ap
        es = ESZ.get(ap.dtype, 4)
        pstep = a[0][0]
        off = int(ap.offset)
        lo = off % pstep if pstep > 0 else off
        hi = lo
        for st, cnt in a[1:]:
            hi += st * (cnt - 1)
        hi += 1
        name = ap.tensor.name
        pg = 2048 if name == "ps" else PAGE
        return [(name, p) for p in range((lo * es) // pg, (hi * es - 1) // pg + 1)]

    def add(self, eng, fn, reads=(), writes=(), dma=False):
        op = Op(eng, fn, dma)
        deps = set()
        rpages = []
        for ap in reads:
            rpages.extend(self.pages(ap))
        wpages = []
        for ap in writes:
            wpages.extend(self.pages(ap))
        for pg in rpages:
            w = self.last_w.get(pg)
            if w is not None:
                if not (w.eng == eng and eng == "pe" and not w.is_dma and not dma):
                    deps.add(w)
        for pg in wpages:
            w = self.last_w.get(pg)
            if w is not None:
                if w.is_dma or dma or w.eng != eng or eng != "pe":
                    deps.add(w)
            r = self.rd.get(pg)
            if r:
                for o in r.values():
                    if o.is_dma or dma or o.eng != eng or eng != "pe":
                        deps.add(o)
        for pg in rpages:
            r = self.rd.get(pg)
            if r is None:
                r = {}; self.rd[pg] = r
            if dma:
                r[("dma", id(op))] = op
            else:
                r[eng] = op
        for pg in wpages:
            self.last_w[pg] = op
            self.rd[pg] = {}
        deps.discard(op)
        op.deps = deps
        for d in deps:
            d.signal = True
        self.ops[eng].append(op)
        if dma:
            k = self.dma_count[eng]
            self.dma_count[eng] += 1
            op.q = (eng, k)
            self.dma_ops.append(op)
        return op

    def emit(self, stack):
        nc = self.nc
        sems = {e: stack.enter_context(nc.semaphore("s_" + e)) for e in COMPUTE}
        dsems = {}
        for q, cnt in self.dma_count.items():
            if cnt:
                dsems[q] = [stack.enter_context(nc.semaphore("d_%s_%d" % (q, i))) for i in range(min(cnt, self.ndma))]
        for e in COMPUTE:
            c = 0
            for op in self.ops[e]:
                if op.is_dma:
                    continue
                if op.signal:
                    c += 1
                    op.sem = sems[e]; op.val = c
        for op in self.dma_ops:
            q, k = op.q
            op.sem = dsems[q][k % self.ndma]
            op.val = 16 * (k // self.ndma + 1)
        final_waits = {}
        for op in self.dma_ops:
            final_waits[op.sem] = max(final_waits.get(op.sem, 0), op.val)

        def run(name, e):
            seen = {}
            for op in self.ops[name]:
                waits = {}
                for d in op.deps:
                    if waits.get(d.sem, 0) < d.val:
                        waits[d.sem] = d.val
                if op.is_dma:
                    q, k = op.q
                    if k >= self.ndma:
                        v = op.val - 16
                        if waits.get(op.sem, 0) < v:
                            waits[op.sem] = v
                for s, v in waits.items():
                    if seen.get(s, 0) < v:
                        e.wait_ge(s, v)
                        seen[s] = v
                inst = op.fn(e)
                if op.is_dma:
                    inst.then_inc(op.sem, 16)
                elif op.signal:
                    inst.then_inc(op.sem, 1)
            if name == "sp":
                for s, v in final_waits.items():
                    if seen.get(s, 0) < v:
                        e.wait_ge(s, v)

        block = stack.enter_context(nc.Block())

        @block.sync
        def _(e):
            run("sp", e)

        @block.scalar
        def _(e):
            run("act", e)

        @block.vector
        def _(e):
            run("dve", e)

        @block.gpsimd
        def _(e):
            run("pool", e)

        @block.tensor
        def _(e):
            run("pe", e)


class Arena:
    def __init__(self, tens, nbytes):
        self.t = tens; self.n = nbytes; self.off = 0; self.peak = 0

    def alloc(self, shape, dt):
        n = int(np.prod(shape[1:])) * ESZ[dt]
        ap = self.t[:, self.off // 4:(self.off + n) // 4]
        if dt != F32:
            ap = ap.bitcast(dt)
        if len(shape) == 3:
            ap = ap.rearrange("p (a b) -> p a b", a=shape[1])
        elif len(shape) == 4:
            ap = ap.rearrange("p (a b c) -> p a b c", a=shape[1], b=shape[2])
        if shape[0] < 128:
            ap = ap[0:shape[0]]
        self.off += (n + PAGE - 1) // PAGE * PAGE
        self.peak = max(self.peak, self.off)
        assert self.off <= self.n, ("arena overflow", self.off, self.n)
        return ap

    def alloc_top(self, shape, dt):
        n = int(np.prod(shape[1:])) * ESZ[dt]
        self.n -= (n + PAGE - 1) // PAGE * PAGE
        assert self.n >= self.peak, ("arena top overflow", self.n, self.peak)
        ap = self.t[:, self.n // 4:(self.n + n) // 4]
        if dt != F32:
            ap = ap.bitcast(dt)
        if len(shape) == 3:
            ap = ap.rearrange("p (a b) -> p a b", a=shape[1])
        return ap

    def mark(self):
        return self.off

    def release(self, m):
        self.off = m


class Rot:
    def __init__(self, ar, shape, dt, n):
        self.t = [ar.alloc(shape, dt) for _ in range(n)]; self.i = 0

    def next(self):
        r = self.t[self.i % len(self.t)]; self.i += 1
        return r


C_U, C_V, C_Z, C_XBC, C_DT, C_Q, C_G = 0, 512, 1024, 2048, 3584, 3600, 4112
K_ID, K_TRI, K_L, K_TRIL, K_ONES, K_OH, K_E, K_END = 0, 128, 256, 384, 512, 640, 644, 644 + 1024
V_NMIX, V_NMEM, V_NSSD, V_NFFN, V_CW, V_CB, V_DTB, V_ALOG, V_DSK, V_END = 0, 8, 16, 24, 32, 80, 92, 93, 94, 95

DEBUG = {}


def build(stage=99):
    nc = bass.Bass("TRN2", target_bir_lowering=False)

    def din(name, shape):
        return nc.dram_tensor(name, list(shape), F32, kind="ExternalInput").ap()

    def dout(name, shape):
        return nc.dram_tensor(name, list(shape), F32, kind="ExternalOutput").ap()

    x = din("x", [NT, 1024]); xs_in = din("xs", [NS, 1024]); mem = din("mem", [256, 1024])
    ck = din("ck", [NS, 256, 512]); cv = din("cv", [NS, 256, 512])
    sconv = din("sconv", [NS, 3, 1536]); sssm = din("sssm", [NS, 1024, 128])
    w_in = din("w_in", [1024, 7184]); w_mk = din("w_mk", [1024, 512]); w_mv = din("w_mv", [1024, 512])
    w_bg = din("w_bg", [512, 1024]); w_bs = din("w_bs", [1024, 1024]); w_bx = din("w_bx", [512, 1024])
    w_o = din("w_o", [1024, 1024]); w_up = din("w_up", [1024, 4096]); w_dn = din("w_dn", [4096, 1024])
    consts = din("consts", [128, K_END]); vecs = din("vecs", [128, V_END])
    gm_ws = din("gm_ws", [4, 128, 128])
    r_lnw = din("r_lnw", [1, 512]); r_lnb = din("r_lnb", [1, 512]); r_bs = din("r_bs", [1, 512])
    r_dtb = din("r_dtb", [1, 16]); r_alog = din("r_alog", [1, 16]); r_dsk = din("r_dsk", [1, 16])
    r_nfin = din("r_nfin", [1, 1024]); r_cw = din("r_cw", [1, 4 * 1536]); r_cb = din("r_cb", [1, 1536])
    r_ws00 = din("r_ws00", [1, 4]); r_bs0 = din("r_bs0", [1, 4])

    y = dout("y", [NT, 1024]); ys = dout("ys", [NS, 1024]); mk = dout("mk", [256, 512]); mv = dout("mv", [256, 512])
    convp = dout("convp", [3, 1536]); ssmp = dout("ssmp", [1024, 128])
    convs = dout("convs", [NS, 3, 1536]); ssms = dout("ssms", [NS, 1024, 128]); gv = dout("gv", [NS, 512])

    st = ExitStack()
    def finish():
        S.emit(st)
        st.close()
        DEBUG['peak'] = AR.peak
        return nc
    if True:
        NBY = 207 * 1024
        At = st.enter_context(nc.sbuf_tensor("arena", [128, NBY // 4], F32))
        PS = st.enter_context(nc.psum_tensor("ps", [128, 8, 512], F32))
        S = Sched(nc)
        AR = Arena(At, NBY)
        pbi = [0]

        pb_banks = [list(range(8))]

        def pb():
            bl = pb_banks[0]
            b = bl[pbi[0] % len(bl)]; pbi[0] += 1
            return PS[:, b, :]

        def isap(v):
            return not isinstance(v, (int, float)) and v is not None

        def act(out, in_, func, bias=None, scale=None, accum=None):
            rd = [in_] + [v for v in (bias, scale) if isap(v)]
            wr = [out] + ([accum] if accum is not None else [])
            kw = {}
            if bias is not None: kw["bias"] = bias
            if scale is not None: kw["scale"] = scale
            if accum is not None: kw["accum_out"] = accum
            S.add("act", lambda e: e.activation(out=out, in_=in_, func=func, **kw), reads=rd, writes=wr)

        def tt(eng, out, in0, in1, op):
            S.add(eng, lambda e: e.tensor_tensor(out=out, in0=in0, in1=in1, op=op), reads=[in0, in1], writes=[out])

        def ts(eng, out, in0, s1, s2, op0, op1=None):
            rd = [in0] + [v for v in (s1, s2) if isap(v)]
            if op1 is None:
                S.add(eng, lambda e: e.tensor_scalar(out=out, in0=in0, scalar1=s1, scalar2=None, op0=op0), reads=rd, writes=[out])
            else:
                S.add(eng, lambda e: e.tensor_scalar(out=out, in0=in0, scalar1=s1, scalar2=s2, op0=op0, op1=op1), reads=rd, writes=[out])

        def stt(eng, out, in0, sc, in1, op0, op1):
            rd = [in0, in1] + ([sc] if isap(sc) else [])
            S.add(eng, lambda e: e.scalar_tensor_tensor(out=out, in0=in0, scalar=sc, in1=in1, op0=op0, op1=op1), reads=rd, writes=[out])

        def cp(eng, out, in_):
            if eng == "act":
                act(out, in_, AF.Copy)
            else:
                S.add(eng, lambda e: e.tensor_copy(out=out, in_=in_), reads=[in_], writes=[out])

        def red(eng, out, in_, op):
            S.add(eng, lambda e: e.tensor_reduce(out=out, in_=in_, axis=AX.X, op=op), reads=[in_], writes=[out])

        def memset(eng, out, val):
            S.add(eng, lambda e: e.memset(out, val), writes=[out])

        def mm(out, pairs):
            def fn(e):
                n = len(pairs); ins = None
                for i, (l, r) in enumerate(pairs):
                    ins = e.matmul(out, l, r, start=(i == 0), stop=(i == n - 1))
                return ins
            S.add("pe", fn, reads=[a for p in pairs for a in p], writes=[out])

        def tr(out, in_, ident):
            S.add("pe", lambda e: e.transpose(out, in_, ident), reads=[in_, ident], writes=[out])

        def dma(q, out, in_, slow=False):
            if slow:
                S.add(q, lambda e: e.dma_start(out=out, in_=in_, allow_slow_non_contiguous=True), reads=[in_], writes=[out], dma=True)
            else:
                S.add(q, lambda e: e.dma_start(out=out, in_=in_), reads=[in_], writes=[out], dma=True)

        def load_w(dst, src, K, N):
            srcv = src.rearrange("(k p) n -> p k n", p=128)
            for c0 in range(0, N, 2048):
                c1 = min(N, c0 + 2048)
                dma("pool", dst[:, :, c0:c1], srcv[:, :, c0:c1])

        def bc3(ap, n):
            return ap.unsqueeze(2).to_broadcast([ap.shape[0], ap.shape[1], n])

        def bc_mid(ap, n):
            return ap.unsqueeze(1).to_broadcast([ap.shape[0], n, ap.shape[1]])

        XN = AR.alloc([128, 8, NTOK], BF16)
        BO = AR.alloc([128, 8, NTOK], BF16)
        CONST = AR.alloc([128, K_E], F32)
        VEC = AR.alloc([128, V_END], F32)
        IDB = AR.alloc([128, 128], BF16)
        NEGH = AR.alloc([128, 8], F32)
        SM = Rot(AR, [128, 8], F32, 12)
        ID = CONST[:, K_ID:K_ID + 128]; TRI = CONST[:, K_TRI:K_TRI + 128]; LST = CONST[:, K_L:K_L + 128]
        TRIL = CONST[:, K_TRIL:K_TRIL + 128]; ONES = CONST[:, K_ONES:K_ONES + 128]
        dma("sp", CONST, consts[:, 0:K_E]); dma("sp", VEC, vecs)
        cp("dve", IDB, ID)
        CB16 = AR.alloc([128, 3, 128], BF16)
        TRIb = CB16[:, 0, :]; ONESb = CB16[:, 1, :]; LSTb = CB16[:, 2, :]
        cp("dve", TRIb, TRI); cp("dve", ONESb, ONES); cp("dve", LSTb, LST)
        memset("pool", NEGH, -0.5)

        tiles4 = [(i * 512, 512) for i in range(4)] + [(NT, NS)]
        tiles17 = [(i * 128, 128) for i in range(16)] + [(NT, NS)]

        def rstd_of(ss, n, count, eps_scale=1.0):
            s = SM.next()
            ts("dve", s[0:n, 0:1], ss, 1.0 / count, EPS, ALU.mult, ALU.add)
            tt("pool", s[0:n, 1:2], s[0:n, 0:1], NEGH[0:n, 0:1], ALU.pow)
            return s[0:n, 1:2]

        def norm_to_T(src, n, t0, nwcol, dstT, XB, JK):
            s = SM.next()
            jk = JK.next()
            act(jk[0:n], src, AF.Square, accum=s[0:n, 0:1])
            r = rstd_of(s[0:n, 0:1], n, 1024.0)
            xb = XB.next()
            ts("dve", xb[0:n], src, r, None, ALU.mult)
            bank = pb().bitcast(BF16).rearrange("p (a b) -> p a b", a=8)
            for kc in range(8):
                tr(bank[:, kc, 0:n], xb[0:n, kc * 128:(kc + 1) * 128], IDB[0:n, 0:n])
            tt("dve", dstT[:, :, t0:t0 + n], bank[:, :, 0:n], bc3(VEC[:, nwcol:nwcol + 8], n), ALU.mult)

        m0 = AR.mark()
        XIN = Rot(AR, [128, 1024], F32, 3)
        XBr = Rot(AR, [128, 1024], BF16, 2)
        JKr = Rot(AR, [128, 1024], BF16, 1)
        mP = AR.mark()
        for (t0, n) in tiles17:
            xi = XIN.next()
            dma("sp", xi[0:n], x[t0:t0 + n, :] if t0 < NT else xs_in[:, :])
            norm_to_T(xi[0:n], n, t0, V_NMIX, XN, XBr, JKr)

        if stage <= 0:
            return finish()
        AR.release(m0)
        Wxbc = AR.alloc([128, 8, 1536], BF16); Wz = AR.alloc([128, 8, 1024], BF16); Wdt = AR.alloc([128, 8, 16], BF16)
        load_w(Wxbc, w_in[:, C_XBC:C_XBC + 1536], 1024, 1536)
        load_w(Wdt, w_in[:, C_DT:C_DT + 16], 1024, 16)
        load_w(Wz, w_in[:, C_Z:C_Z + 1024], 1024, 1024)
        if stage <= 0.5:
            return finish()
        RB = AR.alloc([128, 64], F32)
        DTB = RB[:, 0:16]; ALOG = RB[:, 16:32]; DSK = RB[:, 32:48]; ANEG = RB[:, 48:64]
        dma("sp", DTB, r_dtb.partition_broadcast(128)); dma("sp", ALOG, r_alog.partition_broadcast(128))
        dma("sp", DSK, r_dsk.partition_broadcast(128))
        act(ANEG, ALOG, AF.Exp)
        ts("dve", ANEG, ANEG, -1.0, None, ALU.mult)
        if stage <= 0.7:
            return finish()
        PRE_S = AR.alloc([128, 8, 256], F32)
        DT = PRE_S[:, 0, :]; AA = PRE_S[:, 1, :]; EAC = PRE_S[:, 2, :]; CD = PRE_S[:, 3, :]
        DEND = PRE_S[:, 4, :]; DTD = PRE_S[:, 5, :]; T6 = PRE_S[:, 6, :]; T7 = PRE_S[:, 7, :]
        bk = pb()
        for c in range(NCH):
            mm(bk[:, c * 16:(c + 1) * 16], [(XN[:, kc, c * 128:(c + 1) * 128], Wdt[:, kc, :]) for kc in range(8)])
        v3 = lambda ap: ap.rearrange("p (c h) -> p c h", c=16)
        tt("dve", v3(T6), v3(bk[:, 0:256]), bc_mid(DTB, 16), ALU.add)
        act(T7, T6, AF.Exp)
        ts("dve", T7, T7, 1.0, None, ALU.add)
        act(DT, T7, AF.Ln)
        if stage <= 0.8:
            return finish()
        tt("dve", v3(AA), v3(DT), bc_mid(ANEG, 16), ALU.mult)
        AHL = AR.alloc([128, 2, 256], BF16)
        AAh = AHL[:, 0, :]; AAl = AHL[:, 1, :]
        cp("dve", AAh, AA)
        tt("dve", T6, AA, AAh, ALU.subtract)
        cp("dve", AAl, T6)
        bk2 = pb()
        mm(bk2[:, 0:256], [(TRIb, AAh), (TRIb, AAl)])
        mm(bk2[:, 256:512], [(ONESb, AAh), (ONESb, AAl)])
        act(EAC, bk2[:, 0:256], AF.Exp)
        act(CD, bk2[:, 256:512], AF.Exp)
        act(T6, bk2[:, 0:256], AF.Copy)
        act(T7, bk2[:, 256:512], AF.Copy)
        tt("dve", T7, T7, T6, ALU.subtract)
        act(DEND, T7, AF.Exp)
        tt("dve", DTD, DT, DEND, ALU.mult)

        if stage <= 1:
            return finish()
        mS = AR.mark()
        HALO = AR.alloc([128, 12, 4], F32)
        memset("pool", HALO, 0.0)
        HW = 256
        PREc = Rot(AR, [128, HW + 4], F32, 2)
        XC = AR.alloc([128, 12, HW], F32)
        BCT2 = [AR.alloc([128, 4, HW], BF16), AR.alloc([128, 4, HW], BF16)]
        CTr = Rot(AR, [128, HW], F32, 1)
        XSr = Rot(AR, [128, 1024], F32, 2)
        XDTr = Rot(AR, [128, 16, 64], BF16, 2)
        XDTDr = Rot(AR, [128, 16, 64], BF16, 2)
        BTKr = Rot(AR, [128, 2, 128], BF16, 2)
        CBMr = Rot(AR, [128, 2, 128], F32, 2)
        RHr = Rot(AR, [128, 2, 4, 128], BF16, 2)
        ESr = Rot(AR, [128, 4, 128], F32, 2)
        MTr = Rot(AR, [128, 16, 128], BF16, 1)
        Y1r = Rot(AR, [128, 1024], F32, 1)
        XSDr = Rot(AR, [128, 1024], F32, 1)
        Zr = Rot(AR, [128, 1024], F32, 2)
        THr = Rot(AR, [128, 1024], F32, 1)
        YNr = Rot(AR, [128, 1024], BF16, 1)
        HT = AR.alloc([128, 1024], F32)
        HTb = AR.alloc([128, 1024], BF16)
        memset("pool", HT, 0.0)
        memset("pool", HTb, 0.0)
        JK2 = Rot(AR, [128, 512], BF16, 2)
        def conv_gen(T, bankf):
            t0 = T * HW
            for cc in range(12):
                if cc:
                    yield
                bank = bankf()
                mm(bank[:, 0:HW], [(Wxbc[:, kc, cc * 128:(cc + 1) * 128], XN[:, kc, t0:t0 + HW]) for kc in range(8)])
                pre = PREc.next()
                act(pre[:, 0:3], HALO[:, cc, 0:3], AF.Copy)
                act(pre[:, 3:HW + 3], bank[:, 0:HW], AF.Copy)
                cw = VEC[:, V_CW + cc * 4:V_CW + cc * 4 + 4]
                act(XC[:, cc, :], pre[:, 0:HW], AF.Identity, bias=VEC[:, V_CB + cc:V_CB + cc + 1], scale=cw[:, 0:1])
                for k in range(1, 4):
                    if True:
                        stt("dve", XC[:, cc, :], pre[:, k:k + HW], cw[:, k:k + 1], XC[:, cc, :], ALU.mult, ALU.add)
                    else:
                        ct = CTr.next()
                        ts("pool", ct, pre[:, k:k + HW], cw[:, k:k + 1], None, ALU.mult)
                        tt("pool", XC[:, cc, :], XC[:, cc, :], ct, ALU.add)
                act(HALO[:, cc, 0:3], pre[:, HW:HW + 3], AF.Copy)
                if t0 + HW == NT:
                    dma("sp", convp[:, cc * 128:(cc + 1) * 128].rearrange("k c -> c k"), pre[:, HW:HW + 3], slow=True)
            yield
            for cc in range(12):
                act(XC[:, cc, :], XC[:, cc, :], AF.Silu)
            for j in range(4):
                cp("act", BCT2[T % 2][:, j, :], XC[:, 8 + j, :])

        CH = {}
        pfi = [0]; pti = [0]

        def pbf():
            b = 6 + pfi[0] % 2; pfi[0] += 1
            return PS[:, b, :]

        def pbt():
            b = 4 + pti[0] % 2; pti[0] += 1
            return PS[:, b, :]

        def front(c):
            q = c % 2
            t0 = c * 128
            cs = slice(c * 16, (c + 1) * 16)
            qs = slice(q * 128, (q + 1) * 128)
            d = {}
            CH[c] = d
            BCT = BCT2[(c // 2) % 2]
            d["BCT"] = BCT
            XS = XSr.next()
            b0 = pbf(); b1 = pbf()
            for f in range(8):
                bank = b0 if f < 4 else b1
                tr(bank[:, (f % 4) * 128:(f % 4 + 1) * 128], XC[:, f, qs], ID)
            yield
            act(XS[:, 0:512], b0, AF.Copy)
            act(XS[:, 512:1024], b1, AF.Copy)
            b2 = pbf()
            for g in range(2):
                tr(b2[:, g * 128:(g + 1) * 128], XC[:, 8 + g, qs], ID)
            yield
            BTK = BTKr.next()
            cp("dve", BTK, b2[:, 0:256].rearrange("p (g n) -> p g n", g=2))
            yield
            XS3 = XS.rearrange("p (h d) -> p h d", h=16)
            XDT = XDTr.next(); XDTD = XDTDr.next()
            tt("dve", XDT, XS3, bc3(DT[:, cs], 64), ALU.mult)
            tt("pool", XDTD, XS3, bc3(DTD[:, cs], 64), ALU.mult)
            d.update(XS=XS, XS3=XS3, BTK=BTK, XDTD=XDTD, qs=qs, cs=cs, t0=t0)
            yield
            b3 = pbf()
            for g in range(2):
                mm(b3[:, g * 128:(g + 1) * 128], [(BCT[:, g, qs], BCT[:, 2 + g, qs])])
            CBM = CBMr.next()
            tt("dve", CBM, b3[:, 0:256].rearrange("p (g n) -> p g n", g=2), bc_mid(TRI, 2), ALU.mult)
            yield
            Z = Zr.next(); TH = THr.next()
            for hf in range(2):
                bz = pbf()
                mm(bz, [(XN[:, kc, t0:t0 + 128], Wz[:, kc, hf * 512:(hf + 1) * 512]) for kc in range(8)])
                act(Z[:, hf * 512:(hf + 1) * 512], bz, AF.Copy)
                act(TH[:, hf * 512:(hf + 1) * 512], bz, AF.Tanh, scale=0.5)
                yield
            stt("dve", Z, TH, 1.0, Z, ALU.add, ALU.mult)
            d["Z"] = Z
            yield
            MT = MTr.next()
            yd = [PS[:, 2 * (c % 2), :], PS[:, 2 * (c % 2) + 1, :]]
            d["yd"] = yd
            def mk_rh(hq):
                RH = RHr.next()
                hsl = slice(c * 16 + hq * 4, c * 16 + hq * 4 + 4)
                tt("dve", RH[:, 0], bc_mid(TRIb, 4), bc3(AAh[:, hsl], 128), ALU.mult)
                tt("dve", RH[:, 1], bc_mid(TRIb, 4), bc3(AAl[:, hsl], 128), ALU.mult)
                return RH

            def seg_mm(RH):
                bs = pbf()
                mm(bs, [(LSTb, RH[:, 0].rearrange("p a b -> p (a b)")), (LSTb, RH[:, 1].rearrange("p a b -> p (a b)"))])
                return bs

            segs = {}
            for hq in range(2):
                segs[hq] = seg_mm(mk_rh(hq))
                yield
            for hq in range(4):
                g = hq // 2
                ES = ESr.next()
                act(ES.rearrange("p a b -> p (a b)"), segs.pop(hq), AF.Exp)
                tt("dve", MT[:, hq * 4:hq * 4 + 4, :], ES, bc_mid(CBM[:, g, :], 4), ALU.mult)
                yield
                if hq + 2 < 4:
                    segs[hq + 2] = seg_mm(mk_rh(hq + 2))
                for hh in range(4):
                    h = hq * 4 + hh
                    mm(yd[h // 8][:, (h % 8) * 64:(h % 8 + 1) * 64], [(MT[:, h, :], XDT[:, h, :])])
                yield

        def tail(c):
            d = CH.pop(c)
            XS3 = d["XS3"]; BTK = d["BTK"]; XDTD = d["XDTD"]; qs = d["qs"]; cs = d["cs"]; t0 = d["t0"]; yd = d["yd"]; Z = d["Z"]
            BCT = d["BCT"]
            XSD = XSDr.next()
            tt("pool", XSD.rearrange("p (h d) -> p h d", h=16), XS3, bc3(DSK, 64), ALU.mult)
            Y1 = Y1r.next()
            if c > 0:
                yo = [pbt(), pbt()]
                for g in range(2):
                    mm(yo[g], [(BCT[:, 2 + g, qs], HTb[:, g * 512:(g + 1) * 512])])
                for g in range(2):
                    tt("dve", Y1[:, g * 512:(g + 1) * 512].rearrange("p (h d) -> p h d", h=8),
                       yo[g].rearrange("p (h d) -> p h d", h=8), bc3(EAC[:, c * 16 + g * 8:c * 16 + g * 8 + 8], 64), ALU.mult)
            yield
            stb = [pbt(), pbt()]
            for g in range(2):
                mm(stb[g], [(BTK[:, g, :], XDTD[:, g * 8:(g + 1) * 8, :].rearrange("p h d -> p (h d)"))])
            tt("dve", HT.rearrange("p (h d) -> p h d", h=16), HT.rearrange("p (h d) -> p h d", h=16), bc3(CD[:, cs], 64), ALU.mult)
            for g in range(2):
                tt("dve", HT[:, g * 512:(g + 1) * 512], stb[g], HT[:, g * 512:(g + 1) * 512], ALU.add)
            cp("act", HTb, HT)
            yield
            if c > 0:
                tt("dve", Y1, Y1, XSD, ALU.add)
                for g in range(2):
                    tt("dve", Y1[:, g * 512:(g + 1) * 512], yd[g], Y1[:, g * 512:(g + 1) * 512], ALU.add)
            else:
                for g in range(2):
                    tt("dve", Y1[:, g * 512:(g + 1) * 512], yd[g], XSD[:, g * 512:(g + 1) * 512], ALU.add)
            yield
            tt("dve", Y1, Y1, Z, ALU.mult)
            yield
            finish_ssd_tokens(Y1, 128, t0)
            yield

        def finish_ssd_tokens(Y2, n, t0):
            s = SM.next()
            for g in range(2):
                jk = JK2.next()
                act(jk[0:n], Y2[0:n, g * 512:(g + 1) * 512], AF.Square, accum=s[0:n, g:g + 1])
            s2 = SM.next()
            ts("dve", s2[0:n, 0:2], s[0:n, 0:2], 1.0 / (4 * 512), EPS, ALU.mult, ALU.add)
            tt("pool", s2[0:n, 2:4], s2[0:n, 0:2], NEGH[0:n, 0:2], ALU.pow)
            YN = YNr.next()
            ts("dve", s2[0:n, 4:6], s2[0:n, 2:4], 0.5, None, ALU.mult)
            for g in range(2):
                act(YN[0:n, g * 512:(g + 1) * 512], Y2[0:n, g * 512:(g + 1) * 512], AF.Copy, scale=s2[0:n, 4 + g:5 + g])
            bank = pbt().bitcast(BF16).rearrange("p (a b) -> p a b", a=8)
            for fc in range(8):
                tr(bank[:, fc, 0:n], YN[0:n, fc * 128:(fc + 1) * 128], IDB[0:n, 0:n])
            tt("dve", BO[:, :, t0:t0 + n], bank[:, :, 0:n], bc3(VEC[:, V_NSSD:V_NSSD + 8], n), ALU.mult)

        def drain(g):
            for _ in g:
                pass

        def interleave(g1, g2):
            d1 = d2 = False
            while not (d1 and d2):
                if not d1:
                    try:
                        next(g1)
                    except StopIteration:
                        d1 = True
                if not d2:
                    try:
                        next(g2)
                    except StopIteration:
                        d2 = True

        def interleave_all(gens):
            gens = list(gens)
            while gens:
                for g in list(gens):
                    try:
                        next(g)
                    except StopIteration:
                        gens.remove(g)

        drain(conv_gen(0, pb))
        if stage <= 2:
            return finish()
        drain(front(0))
        pb_banks[0] = [4, 5, 6, 7]
        for c in range(NCH):
            if c % 2 == 1 and c + 1 < NCH:
                drain(conv_gen((c + 1) // 2, pb))
            gens = [tail(c)]
            if c + 1 < NCH:
                gens.append(front(c + 1))
            interleave_all(gens)
        pb_banks[0] = list(range(8))
        for f in range(8):
            bank = pb()
            tr(bank[:, 0:128], HT[:, f * 128:(f + 1) * 128], ID)
            o = ESr.next()
            cp("act", o[:, 0, :], bank[:, 0:128])
            dma("sp", ssmp[f * 128:(f + 1) * 128, :], o[:, 0, :])

        if stage <= 3 and stage != 3.5:
            return finish()
        AR.release(mS)
        SB = slice(NT, NTOK)
        EM = AR.alloc([16, 1024], F32)
        dma("sp", EM, consts[0:16, K_E:K_E + 1024])
        SC48 = AR.alloc([48, 1536], F32)
        SCT = AR.alloc([128, 12, 48], F32)
        XBS = AR.alloc([16, 1536], F32)
        XBT = AR.alloc([128, 12, 16], F32)
        XCT = AR.alloc_top([128, 12, 16], F32)
        SMALL = AR.alloc([128, 16, 16], F32)
        DTE = AR.alloc([128, 8, 16], F32); DAE = AR.alloc_top([128, 8, 16], F32); DSE = AR.alloc_top([128, 8], F32)
        DTX = AR.alloc_top([128, 8, 16], F32); YST = AR.alloc_top([128, 8, 16], F32); ZST = AR.alloc_top([128, 8, 16], F32)
        ZSI = AR.alloc_top([128, 8, 16], F32)
        dma("sp", SC48, sconv.rearrange("b k c -> (b k) c"))
        dma("sp", convs[:, 0:2, :], sconv[:, 1:3, :])
        for cc in range(12):
            bank = pb()
            tr(bank[:, 0:48], SC48[:, cc * 128:(cc + 1) * 128], ID[0:48, 0:48])
            cp("act", SCT[:, cc, :], bank[:, 0:48])
        for j in range(3):
            bank = pb()
            mm(bank[0:16], [(XN[:, kc, SB], Wxbc[:, kc, j * 512:(j + 1) * 512]) for kc in range(8)])
            act(XBS[:, j * 512:(j + 1) * 512], bank[0:16], AF.Copy)
        dma("sp", convs[:, 2, :], XBS)
        bank = pb()
        for cc in range(12):
            mm(bank[:, cc * 16:(cc + 1) * 16], [(Wxbc[:, kc, cc * 128:(cc + 1) * 128], XN[:, kc, SB]) for kc in range(8)])
        act(XBT.rearrange("p a b -> p (a b)"), bank[:, 0:192], AF.Copy)
        for cc in range(12):
            cw = VEC[:, V_CW + cc * 4:V_CW + cc * 4 + 4]
            sv = SCT[:, cc, :].rearrange("p (b k) -> p b k", k=3)
            ts("dve", XCT[:, cc, :], XBT[:, cc, :], cw[:, 3:4], VEC[:, V_CB + cc:V_CB + cc + 1], ALU.mult, ALU.add)
            for k in range(3):
                stt("dve", XCT[:, cc, :], sv[:, :, k], cw[:, k:k + 1], XCT[:, cc, :], ALU.mult, ALU.add)
        act(XCT.rearrange("p a b -> p (a b)"), XCT.rearrange("p a b -> p (a b)"), AF.Silu)
        bank = pb()
        mm(bank[0:16, 0:16], [(Wdt[:, kc, :], XN[:, kc, SB]) for kc in range(8)])
        S0 = SMALL[0:16, 0, :]; S1 = SMALL[0:16, 1, :]; S2 = SMALL[0:16, 2, :]; S3 = SMALL[0:16, 3, :]; S4 = SMALL[0:16, 4, :]
        ts("dve", S0, bank[0:16, 0:16], VEC[0:16, V_DTB:V_DTB + 1], None, ALU.add)
        act(S1, S0, AF.Exp)
        ts("dve", S1, S1, 1.0, None, ALU.add)
        act(S2, S1, AF.Ln)
        act(S3[:, 0:1], VEC[0:16, V_ALOG:V_ALOG + 1], AF.Exp)
        ts("dve", S3[:, 1:2], S3[:, 0:1], -1.0, None, ALU.mult)
        ts("dve", S4, S2, S3[:, 1:2], None, ALU.mult)
        act(S4, S4, AF.Exp)
        bank = pb()
        for j in range(8):
            mm(bank[:, j * 16:(j + 1) * 16], [(EM[:, j * 128:(j + 1) * 128], S2)])
            mm(bank[:, 128 + j * 16:128 + (j + 1) * 16], [(EM[:, j * 128:(j + 1) * 128], S4)])
            mm(bank[:, 256 + j:256 + j + 1], [(EM[:, j * 128:(j + 1) * 128], VEC[0:16, V_DSK:V_DSK + 1])])
        act(DTE.rearrange("p a b -> p (a b)"), bank[:, 0:128], AF.Copy)
        act(DAE.rearrange("p a b -> p (a b)"), bank[:, 128:256], AF.Copy)
        act(DSE, bank[:, 256:264], AF.Copy)
        tt("dve", DTX, XCT[:, 0:8, :], DTE, ALU.mult)
        bank = pb()
        for j in range(8):
            mm(bank[:, j * 16:(j + 1) * 16], [(Wz[:, kc, j * 128:(j + 1) * 128], XN[:, kc, SB]) for kc in range(8)])
        act(ZSI.rearrange("p a b -> p (a b)"), bank[:, 0:128], AF.Silu)
        if stage <= 3.5:
            return finish()
        SAMPLE = True
        tilesF = tiles4 if SAMPLE else tiles4[:4]
        tilesT = tiles17 if SAMPLE else tiles17[:16]
        AR.release(m0)
        MACC = AR.alloc([128, 8, NTOK], BF16)
        mW = AR.mark()
        SGr = Rot(AR, [128, 512], F32, 2)
        TMr = Rot(AR, [128, 512], F32, 2)
        JK2 = Rot(AR, [128, 512], BF16, 2)
        mPh = AR.mark()
        Wu = AR.alloc([128, 8, 512], BF16); Wv = AR.alloc([128, 8, 512], BF16)
        mPh2 = AR.mark()

        def branch_out_g(Wbr, nk, SRC, Wg, mode, tiles, bankf=None):
            bankf = bankf or pb
            for (t0, n) in tiles:
                for fc in range(8):
                    yield
                    bb = bankf(); bg = bankf()
                    mm(bb[:, 0:n], [(Wbr[:, kc, fc * 128:(fc + 1) * 128], SRC[:, kc, t0:t0 + n]) for kc in range(nk)])
                    mm(bg[:, 0:n], [(Wg[:, kc, fc * 128:(fc + 1) * 128], XN[:, kc, t0:t0 + n]) for kc in range(8)])
                    sg = SGr.next()
                    act(sg[:, 0:n], bg[:, 0:n], AF.Sigmoid)
                    dst = MACC[:, fc, t0:t0 + n]
                    if mode == "first":
                        tt("dve", dst, bb[:, 0:n], sg[:, 0:n], ALU.mult)
                    else:
                        tm = TMr.next()
                        tt("dve", tm[:, 0:n], bb[:, 0:n], sg[:, 0:n], ALU.mult)
                        if mode == "mid":
                            tt("dve", dst, dst, tm[:, 0:n], ALU.add)
                        else:
                            tt("dve", BO[:, fc, t0:t0 + n], dst, tm[:, 0:n], ALU.add)

        def branch_out(Wbr, nk, SRC, Wg, mode):
            drain(branch_out_g(Wbr, nk, SRC, Wg, mode, tilesF))

        Wbs = AR.alloc([128, 8, 1024], BF16); Wg1 = AR.alloc([128, 8, 1024], BF16)
        load_w(Wbs, w_bs, 1024, 1024)
        load_w(Wg1, w_in[:, C_G + 1024:C_G + 2048], 1024, 1024)
        load_w(Wu, w_in[:, C_U:C_U + 512], 1024, 512); load_w(Wv, w_in[:, C_V:C_V + 512], 1024, 512)
        mSL = AR.mark()
        Hr = Rot(AR, [128, 8, 128], F32, 2); T2r = Rot(AR, [128, 8, 128], F32, 2); DGr = Rot(AR, [128, 4, 128], F32, 2)
        SML2 = AR.alloc([128, 4, 16], F32)

        def sample_state_loop():
            for b in range(16):
                DG = DGr.next()
                tt("pool", DG, bc_mid(ID, 4), bc3(XCT[:, 8:12, b], 128), ALU.mult)
                bcb = pb()
                mm(bcb, [(ONES, DG.rearrange("p a b -> p (a b)"))])
                H = Hr.next(); T2 = T2r.next()
                dma("sp", H, sssm[b].rearrange("(j p) n -> p j n", p=128))
                yield
                for j in range(8):
                    act(H[:, j, :], H[:, j, :], AF.Copy, scale=DAE[:, j, b:b + 1])
                Bv = bcb[:, 0:256].rearrange("p (g n) -> p g n", g=2).unsqueeze(2).to_broadcast([128, 2, 4, 128])
                Cv = bcb[:, 256:512].rearrange("p (g n) -> p g n", g=2).unsqueeze(2).to_broadcast([128, 2, 4, 128])
                T24 = T2.rearrange("p (g j) n -> p g j n", g=2)
                dxb = DTX[:, :, b].rearrange("p (g j) -> p g j", g=2).unsqueeze(3).to_broadcast([128, 2, 4, 128])
                tt("dve", T24, Bv, dxb, ALU.mult)
                yield
                tt("dve", H, H, T2, ALU.add)
                dma("sp", ssms[b].rearrange("(j p) n -> p j n", p=128), H)
                tt("dve", T24, Cv, H.rearrange("p (g j) n -> p g j n", g=2), ALU.mult)
                red("dve", YST[:, :, b], T2, ALU.add)
                yield

        if SAMPLE:
            def twice(g):
                while True:
                    try:
                        next(g); next(g)
                    except StopIteration:
                        return
                    yield
            interleave(twice(sample_state_loop()), branch_out_g(Wbs, 8, BO, Wg1, "first", tiles4[:4]))
            tt("dve", ZST, XCT[:, 0:8, :], bc3(DSE, 16), ALU.mult)
            tt("dve", YST, YST, ZST, ALU.add)
            tt("dve", YST, YST, ZSI, ALU.mult)
            tt("dve", ZST, YST, YST, ALU.mult)
            bank = pb()
            for g in range(2):
                mm(bank[:, g * 16:(g + 1) * 16], [(ONES, ZST[:, 4 * g + j, :]) for j in range(4)])
            RS2 = SML2[:, 0:2, :]; NH32 = SML2[:, 2:4, :]
            ts("dve", RS2.rearrange("p a b -> p (a b)"), bank[:, 0:32], 1.0 / 512, EPS, ALU.mult, ALU.add)
            memset("pool", NH32, -0.5)
            tt("pool", RS2, RS2, NH32, ALU.pow)
            for g in range(2):
                tt("dve", YST[:, 4 * g:4 * g + 4, :], YST[:, 4 * g:4 * g + 4, :], bc_mid(RS2[:, g, :], 4), ALU.mult)
            tt("dve", BO[:, :, SB], YST, bc3(VEC[:, V_NSSD:V_NSSD + 8], 16), ALU.mult)
            drain(branch_out_g(Wbs, 8, BO, Wg1, "first", tiles4[4:]))
        else:
            branch_out(Wbs, 8, BO, Wg1, "first")
        AR.release(mSL)
        if stage <= 4:
            return finish()
        AR.release(mPh2)
        Wg0 = AR.alloc([128, 8, 1024], BF16); Wbg = AR.alloc([128, 4, 1024], BF16)
        load_w(Wg0, w_in[:, C_G:C_G + 1024], 1024, 1024); load_w(Wbg, w_bg, 512, 1024)
        GM = AR.alloc([128, 4, NTOK], BF16)
        mP3b = AR.mark()
        WST = AR.alloc([128, 4, 128], BF16)
        LNW = AR.alloc([128, 512], F32); LNB = AR.alloc([128, 512], F32); BSB = AR.alloc([128, 512], F32)
        dma("sp", LNW, r_lnw.partition_broadcast(128)); dma("sp", LNB, r_lnb.partition_broadcast(128))
        dma("sp", BSB, r_bs.partition_broadcast(128))
        for g in range(4):
            wt_ = TMr.next()
            dma("sp", wt_[:, 0:128], gm_ws[g])
            tt("pool", wt_[:, 0:128], wt_[:, 0:128], TRIL, ALU.mult)
            bank = pb()
            tr(bank[:, 0:128], wt_[:, 0:128], ID)
            cp("act", WST[:, g, :], bank[:, 0:128])
        UT = AR.alloc([128, 4, 512], F32)
        Vr = Rot(AR, [128, 512], F32, 2); VNr = Rot(AR, [128, 512], F32, 2); VBr = Rot(AR, [128, 512], BF16, 2)
        def p3_uproj(T):
            t0 = T * 512
            for g in range(4):
                bank = pb()
                mm(bank, [(Wu[:, kc, g * 128:(g + 1) * 128], XN[:, kc, t0:t0 + 512]) for kc in range(8)])
                act(UT[:, g, :], bank, AF.Gelu)

        def p3_chunk(c, banks):
            c0 = c * 128; q = c % 4
            bank = PS[:, banks[0], :]
            mm(bank, [(XN[:, kc, c0:c0 + 128], Wv[:, kc, :]) for kc in range(8)])
            yield
            V = Vr.next(); s = SM.next()
            act(V, bank, AF.Gelu, accum=s[:, 0:1])
            jk = JK2.next()
            act(jk, V, AF.Square, accum=s[:, 1:2])
            yield
            s2 = SM.next()
            ts("dve", s2[:, 0:1], s[:, 0:1], 1.0 / 512, None, ALU.mult)
            tt("dve", s2[:, 1:2], s2[:, 0:1], s2[:, 0:1], ALU.mult)
            yield
            stt("dve", s2[:, 2:3], s[:, 1:2], 1.0 / 512, s2[:, 1:2], ALU.mult, ALU.subtract)
            ts("dve", s2[:, 3:4], s2[:, 2:3], EPS, None, ALU.add)
            yield
            tt("pool", s2[:, 4:5], s2[:, 3:4], NEGH[:, 0:1], ALU.pow)
            yield
            VN = VNr.next()
            ts("dve", VN, V, s2[:, 0:1], s2[:, 4:5], ALU.subtract, ALU.mult)
            yield
            tt("dve", VN, VN, LNW, ALU.mult)
            yield
            VB = VBr.next()
            tt("dve", VB, VN, LNB, ALU.add)
            yield
            bank2 = PS[:, banks[1], :]
            for g in range(4):
                mm(bank2[:, g * 128:(g + 1) * 128], [(VB[:, g * 128:(g + 1) * 128], WST[:, g, :])])
            yield
            tm = TMr.next()
            tt("dve", tm, bank2, BSB, ALU.add)
            yield
            tt("dve", GM[:, :, c0:c0 + 128], tm.rearrange("p (g t) -> p g t", g=4), UT[:, :, q * 128:(q + 1) * 128], ALU.mult)

        for T in range(4):
            pb_banks[0] = [4, 5, 6, 7]
            p3_uproj(T)
            for c in (4 * T, 4 * T + 2):
                interleave_all([p3_chunk(c, (0, 1)), p3_chunk(c + 1, (2, 3))])
        pb_banks[0] = list(range(8))
        if SAMPLE:
            WS0 = AR.alloc([16, 8], F32)
            dma("sp", WS0[:, 0:4], r_ws00.partition_broadcast(16)); dma("sp", WS0[:, 4:8], r_bs0.partition_broadcast(16))
            bu = pb()
            mm(bu[0:16], [(XN[:, kc, SB], Wu[:, kc, :]) for kc in range(8)])
            US = Vr.next()
            act(US[0:16], bu[0:16], AF.Gelu)
            bv = pb()
            mm(bv[0:16], [(XN[:, kc, SB], Wv[:, kc, :]) for kc in range(8)])
            V = Vr.next(); s = SM.next()
            act(V[0:16], bv[0:16], AF.Gelu, accum=s[0:16, 0:1])
            jk = JK2.next()
            act(jk[0:16], V[0:16], AF.Square, accum=s[0:16, 1:2])
            s2 = SM.next()
            ts("dve", s2[0:16, 0:1], s[0:16, 0:1], 1.0 / 512, None, ALU.mult)
            tt("dve", s2[0:16, 1:2], s2[0:16, 0:1], s2[0:16, 0:1], ALU.mult)
            stt("dve", s2[0:16, 2:3], s[0:16, 1:2], 1.0 / 512, s2[0:16, 1:2], ALU.mult, ALU.subtract)
            ts("dve", s2[0:16, 3:4], s2[0:16, 2:3], EPS, None, ALU.add)
            tt("pool", s2[0:16, 4:5], s2[0:16, 3:4], NEGH[0:16, 0:1], ALU.pow)
            VN = VNr.next()
            ts("dve", VN[0:16], V[0:16], s2[0:16, 0:1], s2[0:16, 4:5], ALU.subtract, ALU.mult)
            tt("pool", VN[0:16], VN[0:16], LNW[0:16], ALU.mult)
            tt("pool", VN[0:16], VN[0:16], LNB[0:16], ALU.add)
            dma("sp", gv, VN[0:16])
            MX = VNr.next()
            for g in range(4):
                ts("dve", MX[0:16, g * 128:(g + 1) * 128], VN[0:16, g * 128:(g + 1) * 128], WS0[:, g:g + 1], WS0[:, 4 + g:5 + g], ALU.mult, ALU.add)
            VB = VBr.next()
            tt("pool", VB[0:16], MX[0:16], US[0:16], ALU.mult)
            bank = pb().bitcast(BF16).rearrange("p (a b) -> p a b", a=8)
            for g in range(4):
                tr(bank[:, g, 0:16], VB[0:16, g * 128:(g + 1) * 128], IDB[0:16, 0:16])
            cp("act", GM[:, :, SB], bank[:, 0:4, 0:16])
        AR.release(mP3b)
        Wmk = AR.alloc([128, 8, 512], BF16); Wmv = AR.alloc([128, 8, 512], BF16)
        load_w(Wmk, w_mk, 1024, 512); load_w(Wmv, w_mv, 1024, 512)
        mP3c = AR.mark()
        branch_out(Wbg, 4, GM, Wg0, "mid")
        if stage <= 5:
            return finish()
        AR.release(mPh)
        XA = AR.alloc([128, 4, NTOK], BF16)
        KT = AR.alloc([128, 4, 256], BF16); VBm = AR.alloc([128, 2, 512], BF16)
        mC = AR.mark()
        assert mC <= mP3b, (mC, mP3b)
        AR.release(mP3c)
        MNT = AR.alloc([128, 8, 256], BF16)
        MEMr = Rot(AR, [128, 1024], F32, 1); XB4 = Rot(AR, [128, 1024], BF16, 1); JK4 = Rot(AR, [128, 1024], BF16, 1)
        for mt in range(2):
            mi = MEMr.next()
            dma("sp", mi, mem[mt * 128:(mt + 1) * 128, :])
            norm_to_T(mi, 128, mt * 128, V_NMEM, MNT, XB4, JK4)
        for mt in range(2):
            for (W_, o_, isv) in ((Wmk, mk, False), (Wmv, mv, True)):
                bank = pb()
                mm(bank, [(MNT[:, kc, mt * 128:(mt + 1) * 128], W_[:, kc, :]) for kc in range(8)])
                ko = TMr.next()
                act(ko, bank, AF.Copy)
                dma("sp", o_[mt * 128:(mt + 1) * 128, :], ko)
                if isv:
                    cp("act", VBm[:, mt, :], ko)
        for h in range(4):
            bank = pb()
            mm(bank[:, 0:256], [(Wmk[:, kc, h * 128:(h + 1) * 128], MNT[:, kc, :]) for kc in range(8)])
            cp("act", KT[:, h, :], bank[:, 0:256])
        AR.release(mC)
        Wq = AR.alloc([128, 8, 512], BF16); Wg2 = AR.alloc([128, 8, 1024], BF16); Wbx = AR.alloc([128, 4, 1024], BF16)
        load_w(Wq, w_in[:, C_Q:C_Q + 512], 1024, 512)
        load_w(Wg2, w_in[:, C_G + 2048:C_G + 3072], 1024, 1024); load_w(Wbx, w_bx, 512, 1024)
        mQ = AR.mark()
        QT = AR.alloc([128, 4, 512], BF16)
        Pr = Rot(AR, [128, 4, 256], BF16, 2); PNr = Rot(AR, [128, 4, 256], BF16, 2); PTr = Rot(AR, [128, 8, 128], BF16, 2)
        def p4_qproj(T):
            t0 = T * 512
            for h in range(4):
                bank = pb()
                mm(bank, [(Wq[:, kc, h * 128:(h + 1) * 128], XN[:, kc, t0:t0 + 512]) for kc in range(8)])
                act(QT[:, h, :], bank, AF.Copy, scale=float(128 ** -0.5))

        def p4_chunk(c, banks):
            q = c % 4; c0 = c * 128
            sc = [PS[:, banks[0], :], PS[:, banks[1], :]]
            for h in range(4):
                mm(sc[h // 2][:, (h % 2) * 256:(h % 2 + 1) * 256], [(QT[:, h, q * 128:(q + 1) * 128], KT[:, h, :])])
            yield
            s = SM.next(); s2 = SM.next(); s3 = SM.next()
            for j in range(2):
                red("dve", s[:, 2 * j:2 * j + 2], sc[j].rearrange("p (h m) -> p h m", h=2), ALU.max)
            yield
            ts("dve", s2[:, 0:4], s[:, 0:4], -1.0, None, ALU.mult)
            yield
            P = Pr.next()
            for h in range(4):
                act(P[:, h, :], sc[h // 2][:, (h % 2) * 256:(h % 2 + 1) * 256], AF.Exp, bias=s2[:, h:h + 1], accum=s3[:, h:h + 1])
            yield
            S.add("dve", lambda e, o=s3[:, 4:8], i=s3[:, 0:4]: e.reciprocal(out=o, in_=i), reads=[s3[:, 0:4]], writes=[s3[:, 4:8]])
            yield
            PN = PNr.next()
            tt("dve", PN, P, bc3(s3[:, 4:8], 256), ALU.mult)
            yield
            bank = PS[:, banks[2], :].bitcast(BF16).rearrange("p (a b) -> p a b", a=8)
            for h in range(4):
                for mh in range(2):
                    tr(bank[:, h * 2 + mh, :], PN[:, h, mh * 128:(mh + 1) * 128], IDB)
            yield
            PT = PTr.next()
            cp("act", PT, bank)
            yield
            bo = PS[:, banks[3], :]
            for h in range(4):
                mm(bo[:, h * 128:(h + 1) * 128], [(VBm[:, mh, h * 128:(h + 1) * 128], PT[:, h * 2 + mh, :]) for mh in range(2)])
            yield
            cp("act", XA[:, :, c0:c0 + 128], bo.rearrange("p (h t) -> p h t", h=4))

        for T in range(4):
            p4_qproj(T)
            for c in (4 * T, 4 * T + 2):
                interleave_all([p4_chunk(c, (0, 1, 2, 3)), p4_chunk(c + 1, (4, 5, 6, 7))])
        if SAMPLE:
            AR.release(mQ)
            QS = AR.alloc([16, 512], F32); SCTs = AR.alloc([128, 2, 64], F32); OH4 = AR.alloc([4, 4], BF16)
            PSs = AR.alloc([64, 256], F32); PTs = AR.alloc([128, 2, 64], BF16)
            KSr = Rot(AR, [128, 2, 512], F32, 2); PRr = Rot(AR, [128, 2, 512], F32, 1); SELr = Rot(AR, [16, 128], F32, 2)
            VBsr = Rot(AR, [128, 2, 512], BF16, 2); OBr = Rot(AR, [4, 512], BF16, 2)
            cp("dve", OH4, CONST[0:4, K_OH:K_OH + 4])
            bq = pb()
            mm(bq[0:16], [(XN[:, kc, SB], Wq[:, kc, :]) for kc in range(8)])
            act(QS, bq[0:16], AF.Copy, scale=float(128 ** -0.5))
            for b in range(16):
                K_ = KSr.next()
                dma("sp", K_, ck[b].rearrange("(mh m) f -> m mh f", mh=2))
                SEL = SELr.next()
                cp("act", SEL, ID[0:16, b:b + 1].to_broadcast([16, 128]))
                bqb = pb()
                mm(bqb, [(SEL, QS)])
                PR = PRr.next()
                tt("dve", PR, K_, bc_mid(bqb, 2), ALU.mult)
                red("dve", SCTs[:, :, 4 * b:4 * b + 4], PR.rearrange("p a (h d) -> p a h d", h=4), ALU.add)
            bs_ = pb()
            for mh in range(2):
                tr(bs_[0:64, mh * 128:(mh + 1) * 128], SCTs[:, mh, :], ID)
            sA = SM.next()
            red("dve", sA[0:64, 0:1], bs_[0:64, 0:256], ALU.max)
            ts("dve", sA[0:64, 1:2], sA[0:64, 0:1], -1.0, None, ALU.mult)
            act(PSs, bs_[0:64, 0:256], AF.Exp, bias=sA[0:64, 1:2], accum=sA[0:64, 2:3])
            S.add("dve", lambda e, o=sA[0:64, 3:4], i=sA[0:64, 2:3]: e.reciprocal(out=o, in_=i), reads=[sA[0:64, 2:3]], writes=[sA[0:64, 3:4]])
            ts("dve", PSs, PSs, sA[0:64, 3:4], None, ALU.mult)
            bt_ = pb()
            for mh in range(2):
                tr(bt_[:, mh * 64:(mh + 1) * 64], PSs[:, mh * 128:(mh + 1) * 128], ID[0:64, 0:64])
            cp("act", PTs.rearrange("p a b -> p (a b)"), bt_[:, 0:128])
            for b in range(16):
                V_ = KSr.next()
                dma("sp", V_, cv[b].rearrange("(mh m) f -> m mh f", mh=2))
                VBs = VBsr.next()
                cp("act", VBs, V_)
                bo_ = pb()
                mm(bo_[0:4], [(PTs[:, mh, 4 * b:4 * b + 4], VBs[:, mh, :]) for mh in range(2)])
                OB = OBr.next()
                cp("act", OB, bo_[0:4])
                bxb = pb()
                for h in range(4):
                    mm(bxb[:, h:h + 1], [(OB[:, h * 128:(h + 1) * 128], OH4[:, h:h + 1])])
                cp("act", XA[:, :, NT + b:NT + b + 1], bxb[:, 0:4].rearrange("p (h o) -> p h o", o=1))
        branch_out(Wbx, 4, XA, Wg2, "last")
        if stage <= 6:
            return finish()
        AR.release(m0)
        H2 = AR.alloc([128, 17, 1024], F32)
        mF = AR.mark()
        Wo = AR.alloc([128, 8, 1024], BF16)
        load_w(Wo, w_o, 1024, 1024)
        XIN2 = Rot(AR, [128, 1024], F32, 2); XB5 = Rot(AR, [128, 1024], BF16, 2); JK5 = Rot(AR, [128, 1024], BF16, 1)
        def p5_A(ti):
            t0, n = tilesT[ti]
            xi = XIN2.next()
            dma("sp", xi[0:n], x[t0:t0 + n, :] if t0 < NT else xs_in[:, :])
            for hf in range(2):
                bank = pb()
                mm(bank[0:n], [(BO[:, kc, t0:t0 + n], Wo[:, kc, hf * 512:(hf + 1) * 512]) for kc in range(8)])
                tt("dve", H2[0:n, ti, hf * 512:(hf + 1) * 512], bank[0:n], xi[0:n, hf * 512:(hf + 1) * 512], ALU.add)

        p5_A(0)
        for ti, (t0, n) in enumerate(tilesT):
            if ti + 1 < len(tilesT):
                p5_A(ti + 1)
            norm_to_T(H2[0:n, ti, :], n, t0, V_NFFN, XN, XB5, JK5)
        if stage <= 7:
            return finish()
        AR.release(mF)
        Wupr = Rot(AR, [128, 8, 512], BF16, 2); Wdnr = Rot(AR, [128, 4, 1024], BF16, 2)
        RL = Rot(AR, [128, 512], BF16, 2)
        NFW = AR.alloc([128, 1024], F32)
        dma("sp", NFW, r_nfin.partition_broadcast(128))
        YOr = Rot(AR, [128, 1024], F32, 2); JK6 = Rot(AR, [128, 1024], BF16, 1)
        wnext = (Wupr.next(), Wdnr.next())
        load_w(wnext[0], w_up[:, 0:512], 1024, 512); load_w(wnext[1], w_dn[0:512, :], 512, 1024)
        for dc in range(8):
            wu, wd = wnext
            if dc < 7:
                wnext = (Wupr.next(), Wdnr.next())
                load_w(wnext[0], w_up[:, (dc + 1) * 512:(dc + 2) * 512], 1024, 512)
                load_w(wnext[1], w_dn[(dc + 1) * 512:(dc + 2) * 512, :], 512, 1024)
            AT = BO[:, (dc % 2) * 4:(dc % 2) * 4 + 4, :]
            for (t0, n) in tilesF:
                for j in range(4):
                    bank = pb()
                    mm(bank[:, 0:n], [(wu[:, kc, j * 128:(j + 1) * 128], XN[:, kc, t0:t0 + n]) for kc in range(8)])
                    rl = RL.next()
                    act(rl[:, 0:n], bank[:, 0:n], AF.Relu)
                    tt("pool", AT[:, j, t0:t0 + n], rl[:, 0:n], rl[:, 0:n], ALU.mult)
            for ti, (t0, n) in enumerate(tilesT):
                for hf in range(2):
                    bank = pb()
                    mm(bank[0:n], [(AT[:, j, t0:t0 + n], wd[:, j, hf * 512:(hf + 1) * 512]) for j in range(4)])
                    hs = H2[0:n, ti, hf * 512:(hf + 1) * 512]
                    tt("dve", hs, bank[0:n], hs, ALU.add)
        for ti, (t0, n) in enumerate(tilesT):
            s = SM.next(); jk = JK6.next()
            act(jk[0:n], H2[0:n, ti, :], AF.Square, accum=s[0:n, 0:1])
            r = rstd_of(s[0:n, 0:1], n, 1024.0)
            yo = YOr.next()
            stt("dve", yo[0:n], H2[0:n, ti, :], r, NFW[0:n], ALU.mult, ALU.mult)
            dma("sp", y[t0:t0 + n, :] if t0 < NT else ys[:, :], yo[0:n])
        return finish()


def make_consts():
    c = np.zeros((128, K_END), np.float32)
    i = np.arange(128)
    c[:, K_ID:K_ID + 128] = np.eye(128)
    c[:, K_TRI:K_TRI + 128] = (i[:, None] <= i[None, :])
    c[:, K_L:K_L + 128] = (i[:, None] > i[None, :])
    c[:, K_TRIL:K_TRIL + 128] = (i[None, :] <= i[:, None])
    c[:, K_ONES:K_ONES + 128] = 1.0
    for h in range(4):
        c[h, K_OH + h] = 1.0
    for h in range(16):
        c[h, K_E + h * 64:K_E + (h + 1) * 64] = 1.0
    return c


def kernel(x_prompt, x_sample, mem_prompt, cache_mem_k, cache_mem_v, state_conv, state_ssm,
           norm_mix_w, w_in, gm_ln_w, gm_ln_b, gm_ws, gm_bs, conv_w, conv_b, dt_bias, a_log,
           d_skip, ssd_norm_w, mem_norm_w, w_mem_k, w_mem_v, w_br_gm, w_br_ssd, w_br_xa, w_out,
           norm_ffn_w, w_up, w_down, norm_final_w, _stage=99):
    f = lambda a: np.ascontiguousarray(np.asarray(a, dtype=np.float32))
    vec = np.zeros((128, V_END), np.float32)
    pl = lambda v: f(v).reshape(-1, 128).T
    vec[:, V_NMIX:V_NMIX + 8] = pl(norm_mix_w[0]); vec[:, V_NMEM:V_NMEM + 8] = pl(mem_norm_w[0])
    vec[:, V_NSSD:V_NSSD + 8] = pl(ssd_norm_w[0]); vec[:, V_NFFN:V_NFFN + 8] = pl(norm_ffn_w[0])
    cw = f(conv_w[0])
    vec[:, V_CW:V_CW + 48] = cw.reshape(4, 12, 128).transpose(2, 1, 0).reshape(128, 48)
    vec[:, V_CB:V_CB + 12] = pl(conv_b[0])
    vec[0:16, V_DTB] = f(dt_bias[0]); vec[0:16, V_ALOG] = f(a_log[0]); vec[0:16, V_DSK] = f(d_skip[0])
    shared = {
        "w_in": f(w_in[0]), "w_mk": f(w_mem_k[0]), "w_mv": f(w_mem_v[0]), "w_bg": f(w_br_gm[0]),
        "w_bs": f(w_br_ssd[0]), "w_bx": f(w_br_xa[0]), "w_o": f(w_out[0]), "w_up": f(w_up[0]), "w_dn": f(w_down[0]),
        "consts": make_consts(), "vecs": vec, "gm_ws": f(gm_ws[0]),
        "r_lnw": f(gm_ln_w[0]).reshape(1, 512), "r_lnb": f(gm_ln_b[0]).reshape(1, 512), "r_bs": f(gm_bs[0]).reshape(1, 512),
        "r_dtb": f(dt_bias[0]).reshape(1, 16), "r_alog": f(a_log[0]).reshape(1, 16), "r_dsk": f(d_skip[0]).reshape(1, 16),
        "r_nfin": f(norm_final_w).reshape(1, 1024), "r_cw": f(conv_w[0]).reshape(1, 4 * 1536), "r_cb": f(conv_b[0]).reshape(1, 1536),
        "r_ws00": f(gm_ws[0][:, 0, 0]).reshape(1, 4), "r_bs0": f(gm_bs[0][:, 0]).reshape(1, 4),
    }
    in_maps = []
    for c in range(8):
        m = dict(shared)
        sl = slice(16 * c, 16 * c + 16)
        m["x"] = f(x_prompt[c]); m["xs"] = f(x_sample[sl, 0]); m["mem"] = f(mem_prompt[c])
        m["ck"] = f(cache_mem_k[0, sl]).reshape(16, 256, 512); m["cv"] = f(cache_mem_v[0, sl]).reshape(16, 256, 512)
        m["sconv"] = f(state_conv[0, sl]); m["sssm"] = f(state_ssm[0, sl]).reshape(16, 1024, 128)
        in_maps.append(m)
    nc = build(_stage)
    res = run_bass_kernel_spmd(nc, in_maps, core_ids=list(range(8)))
    DEBUG["exec_ns"] = getattr(res, "exec_time_ns", None)
    R = res.results
    g = lambda k: np.stack([np.asarray(R[c][k], dtype=np.float32) for c in range(8)])
    y_prompt = g("y")
    y_sample = g("ys").reshape(128, 1, 1024)
    mem_k = g("mk").reshape(1, 8, 256, 4, 128)
    mem_v = g("mv").reshape(1, 8, 256, 4, 128)
    conv_p = g("convp").reshape(1, 8, 3, 1536)
    ssm_p = g("ssmp").reshape(1, 8, 16, 64, 128)
    conv_s = g("convs").reshape(1, 128, 3, 1536)
    ssm_s = g("ssms").reshape(1, 128, 16, 64, 128)
    gv_s = g("gv").reshape(1, 128, 1, 512)
    return (y_prompt, y_sample, mem_k, mem_v, conv_p, ssm_p, conv_s, ssm_s, gv_s)
```

```python
import numpy as np
from contextlib import ExitStack
import concourse.bass as bass
import concourse.mybir as mybir
from concourse.bass_utils import run_bass_kernel_spmd

F32 = mybir.dt.float32
BF16 = mybir.dt.bfloat16
AF = mybir.ActivationFunctionType
ALU = mybir.AluOpType
AX = mybir.AxisListType
ESZ = {F32: 4, BF16: 2}
PAGE = 256
COMPUTE = ("pe", "act", "dve", "pool")
EPS = 1e-6
NT = 2048
NS = 16
NTOK = NT + NS
NCH = 16


class Op:
    __slots__ = ("eng", "fn", "deps", "signal", "is_dma", "sem", "val", "q")

    def __init__(self, eng, fn, is_dma):
        self.eng = eng; self.fn = fn; self.deps = set(); self.signal = False
        self.is_dma = is_dma; self.sem = None; self.val = 0; self.q = None


class Sched:
    def __init__(self, nc, ndma_sems=24):
        self.nc = nc
        self.ops = {e: [] for e in ("pe", "act", "dve", "pool", "sp")}
        self.last_w = {}
        self.rd = {}
        self.ndma = ndma_sems
        self.dma_count = {"sp": 0, "act": 0, "pool": 0}
        self.dma_ops = []

    @staticmethod
    def pages(ap):
        sp = str(ap.space).upper()
        if "DRAM" in sp or "HBM" in sp:
            return ()
        a = ap.ap
        es = ESZ.get(ap.dtype, 4)
        pstep = a[0][0]
        off = int(ap.offset)
        lo = off % pstep if pstep > 0 else off
        hi = lo
        for st, cnt in a[1:]:
            hi += st * (cnt - 1)
        hi += 1
        name = ap.tensor.name
        pg = 2048 if name == "ps" else PAGE
        return [(name, p) for p in range((lo * es) // pg, (hi * es - 1) // pg + 1)]

    def add(self, eng, fn, reads=(), writes=(), dma=False):
        op = Op(eng, fn, dma)
        deps = set()
        rpages = []
        for ap in reads:
            rpages.extend(self.pages(ap))
        wpages = []
        for ap in writes:
            wpages.extend(self.pages(ap))
        for pg in rpages:
            w = self.last_w.get(pg)
            if w is not None:
                if not (w.eng == eng and eng == "pe" and not w.is_dma and not dma):
                    deps.add(w)
        for pg in wpages:
            w = self.last_w.get(pg)
            if w is not None:
                if w.is_dma or dma or w.eng != eng or eng != "pe":
                    deps.add(w)
            r = self.rd.get(pg)
            if r:
                for o in r.values():
                    if o.is_dma or dma or o.eng != eng or eng != "pe":
                        deps.add(o)
        for pg in rpages:
            r = self.rd.get(pg)
            if r is None:
                r = {}; self.rd[pg] = r
            if dma:
                r[("dma", id(op))] = op
            else:
                r[eng] = op
        for pg in wpages:
            self.last_w[pg] = op
            self.rd[pg] = {}
        deps.discard(op)
        op.deps = deps
        for d in deps:
            d.signal = True
        self.ops[eng].append(op)
        if dma:
            k = self.dma_count[eng]
            self.dma_count[eng] += 1
            op.q = (eng, k)
            self.dma_ops.append(op)
        return op

    def emit(self, stack):
        nc = self.nc
        sems = {e: stack.enter_context(nc.semaphore("s_" + e)) for e in COMPUTE}
        dsems = {}
        for q, cnt in self.dma_count.items():
            if cnt:
                dsems[q] = [stack.enter_context(nc.semaphore("d_%s_%d" % (q, i))) for i in range(min(cnt, self.ndma))]
        for e in COMPUTE:
            c = 0
            for op in self.ops[e]:
                if op.is_dma:
                    continue
                if op.signal:
                    c += 1
                    op.sem = sems[e]; op.val = c
        for op in self.dma_ops:
            q, k = op.q
            op.sem = dsems[q][k % self.ndma]
            op.val = 16 * (k // self.ndma + 1)
        final_waits = {}
        for op in self.dma_ops:
            final_waits[op.sem] = max(final_waits.get(op.sem, 0), op.val)

        def run(name, e):
            seen = {}
            for op in self.ops[name]:
                waits = {}
                for d in op.deps:
                    if waits.get(d.sem, 0) < d.val:
                        waits[d.sem] = d.val
                if op.is_dma:
                    q, k = op.q
                    if k >= self.ndma:
                        v = op.val - 16
                        if waits.get(op.sem, 0) < v:
                            waits[op.sem] = v
                for s, v in waits.items():
                    if seen.get(s, 0) < v:
                        e.wait_ge(s, v)
                        seen[s] = v
                inst = op.fn(e)
                if op.is_dma:
                    inst.then_inc(op.sem, 16)
                elif op.signal:
                    inst.then_inc(op.sem, 1)
            if name == "sp":
                for s, v in final_waits.items():
                    if seen.get(s, 0) < v:
                        e.wait_ge(s, v)

        block = stack.enter_context(nc.Block())

        @block.sync
        def _(e):
            run("sp", e)

        @block.scalar
        def _(e):
            run("act", e)

        @block.vector
        def _(e):
            run("dve", e)

        @block.gpsimd
        def _(e):
            run("pool", e)

        @block.tensor
        def _(e):
            run("pe", e)


class Arena:
    def __init__(self, tens, nbytes):
        self.t = tens; self.n = nbytes; self.off = 0; self.peak = 0

    def alloc(self, shape, dt):
        n = int(np.prod(shape[1:])) * ESZ[dt]
        ap = self.t[:, self.off // 4:(self.off + n) // 4]
        if dt != F32:
            ap = ap.bitcast(dt)
        if len(shape) == 3:
            ap = ap.rearrange("p (a b) -> p a b", a=shape[1])
        elif len(shape) == 4:
            ap = ap.rearrange("p (a b c) -> p a b c", a=shape[1], b=shape[2])
        if shape[0] < 128:
            ap = ap[0:shape[0]]
        self.off += (n + PAGE - 1) // PAGE * PAGE
        self.peak = max(self.peak, self.off)
        assert self.off <= self.n, ("arena overflow", self.off, self.n)
        return ap

    def alloc_top(self, shape, dt):
        n = int(np.prod(shape[1:])) * ESZ[dt]
        self.n -= (n + PAGE - 1) // PAGE * PAGE
        assert self.n >= self.peak, ("arena top overflow", self.n, self.peak)
        ap = self.t[:, self.n // 4:(self.n + n) // 4]
        if dt != F32:
            ap = ap.bitcast(dt)
        if len(shape) == 3:
            ap = ap.rearrange("p (a b) -> p a b", a=shape[1])
        return ap

    def mark(self):
        return self.off

    def release(self, m):
        self.off = m


class Rot:
    def __init__(self, ar, shape, dt, n):
        self.t = [ar.alloc(shape, dt) for _ in range(n)]; self.i = 0

    def next(self):
        r = self.t[self.i % len(self.t)]; self.i += 1
        return r


C_U, C_V, C_Z, C_XBC, C_DT, C_Q, C_G = 0, 512, 1024, 2048, 3584, 3600, 4112
K_ID, K_TRI, K_L, K_TRIL, K_ONES, K_OH, K_E, K_END = 0, 128, 256, 384, 512, 640, 644, 644 + 1024
V_NMIX, V_NMEM, V_NSSD, V_NFFN, V_CW, V_CB, V_DTB, V_ALOG, V_DSK, V_END = 0, 8, 16, 24, 32, 80, 92, 93, 94, 95

DEBUG = {}


def build(stage=99):
    nc = bass.Bass("TRN2", target_bir_lowering=False)

    def din(name, shape):
        return nc.dram_tensor(name, list(shape), F32, kind="ExternalInput").ap()

    def dout(name, shape):
        return nc.dram_tensor(name, list(shape), F32, kind="ExternalOutput").ap()

    x = din("x", [NT, 1024]); xs_in = din("xs", [NS, 1024]); mem = din("mem", [256, 1024])
    ck = din("ck", [NS, 256, 512]); cv = din("cv", [NS, 256, 512])
    sconv = din("sconv", [NS, 3, 1536]); sssm = din("sssm", [NS, 1024, 128])
    w_in = din("w_in", [1024, 7184]); w_mk = din("w_mk", [1024, 512]); w_mv = din("w_mv", [1024, 512])
    w_bg = din("w_bg", [512, 1024]); w_bs = din("w_bs", [1024, 1024]); w_bx = din("w_bx", [512, 1024])
    w_o = din("w_o", [1024, 1024]); w_up = din("w_up", [1024, 4096]); w_dn = din("w_dn", [4096, 1024])
    consts = din("consts", [128, K_END]); vecs = din("vecs", [128, V_END])
    gm_ws = din("gm_ws", [4, 128, 128])
    r_lnw = din("r_lnw", [1, 512]); r_lnb = din("r_lnb", [1, 512]); r_bs = din("r_bs", [1, 512])
    r_dtb = din("r_dtb", [1, 16]); r_alog = din("r_alog", [1, 16]); r_dsk = din("r_dsk", [1, 16])
    r_nfin = din("r_nfin", [1, 1024]); r_cw = din("r_cw", [1, 4 * 1536]); r_cb = din("r_cb", [1, 1536])
    r_ws00 = din("r_ws00", [1, 4]); r_bs0 = din("r_bs0", [1, 4])

    y = dout("y", [NT, 1024]); ys = dout("ys", [NS, 1024]); mk = dout("mk", [256, 512]); mv = dout("mv", [256, 512])
    convp = dout("convp", [3, 1536]); ssmp = dout("ssmp", [1024, 128])
    convs = dout("convs", [NS, 3, 1536]); ssms = dout("ssms", [NS, 1024, 128]); gv = dout("gv", [NS, 512])

    st = ExitStack()
    def finish():
        S.emit(st)
        st.close()
        DEBUG['peak'] = AR.peak
        return nc
    if True:
        NBY = 207 * 1024
        At = st.enter_context(nc.sbuf_tensor("arena", [128, NBY // 4], F32))
        PS = st.enter_context(nc.psum_tensor("ps", [128, 8, 512], F32))
        S = Sched(nc)
        AR = Arena(At, NBY)
        pbi = [0]

        pb_banks = [list(range(8))]

        def pb():
            bl = pb_banks[0]
            b = bl[pbi[0] % len(bl)]; pbi[0] += 1
            return PS[:, b, :]

        def isap(v):
            return not isinstance(v, (int, float)) and v is not None

        def act(out, in_, func, bias=None, scale=None, accum=None):
            rd = [in_] + [v for v in (bias, scale) if isap(v)]
            wr = [out] + ([accum] if accum is not None else [])
            kw = {}
            if bias is not None: kw["bias"] = bias
            if scale is not None: kw["scale"] = scale
            if accum is not None: kw["accum_out"] = accum
            S.add("act", lambda e: e.activation(out=out, in_=in_, func=func, **kw), reads=rd, writes=wr)

        def tt(eng, out, in0, in1, op):
            S.add(eng, lambda e: e.tensor_tensor(out=out, in0=in0, in1=in1, op=op), reads=[in0, in1], writes=[out])

        def ts(eng, out, in0, s1, s2, op0, op1=None):
            rd = [in0] + [v for v in (s1, s2) if isap(v)]
            if op1 is None:
                S.add(eng, lambda e: e.tensor_scalar(out=out, in0=in0, scalar1=s1, scalar2=None, op0=op0), reads=rd, writes=[out])
            else:
                S.add(eng, lambda e: e.tensor_scalar(out=out, in0=in0, scalar1=s1, scalar2=s2, op0=op0, op1=op1), reads=rd, writes=[out])

        def stt(eng, out, in0, sc, in1, op0, op1):
            rd = [in0, in1] + ([sc] if isap(sc) else [])
            S.add(eng, lambda e: e.scalar_tensor_tensor(out=out, in0=in0, scalar=sc, in1=in1, op0=op0, op1=op1), reads=rd, writes=[out])

        def cp(eng, out, in_):
            if eng == "act":
                act(out, in_, AF.Copy)
            else:
                S.add(eng, lambda e: e.tensor_copy(out=out, in_=in_), reads=[in_], writes=[out])

        def red(eng, out, in_, op):
            S.add(eng, lambda e: e.tensor_reduce(out=out, in_=in_, axis=AX.X, op=op), reads=[in_], writes=[out])

        def memset(eng, out, val):
            S.add(eng, lambda e: e.memset(out, val), writes=[out])

        def mm(out, pairs):
            def fn(e):
                n = len(pairs); ins = None
                for i, (l, r) in enumerate(pairs):
                    ins = e.matmul(out, l, r, start=(i == 0), stop=(i == n - 1))
                return ins
            S.add("pe", fn, reads=[a for p in pairs for a in p], writes=[out])

        def tr(out, in_, ident):
            S.add("pe", lambda e: e.transpose(out, in_, ident), reads=[in_, ident], writes=[out])

        def dma(q, out, in_, slow=False):
            if slow:
                S.add(q, lambda e: e.dma_start(out=out, in_=in_, allow_slow_non_contiguous=True), reads=[in_], writes=[out], dma=True)
            else:
                S.add(q, lambda e: e.dma_start(out=out, in_=in_), reads=[in_], writes=[out], dma=True)

        def load_w(dst, src, K, N):
            srcv = src.rearrange("(k p) n -> p k n", p=128)
            for c0 in range(0, N, 2048):
                c1 = min(N, c0 + 2048)
                dma("pool", dst[:, :, c0:c1], srcv[:, :, c0:c1])

        def bc3(ap, n):
            return ap.unsqueeze(2).to_broadcast([ap.shape[0], ap.shape[1], n])

        def bc_mid(ap, n):
            return ap.unsqueeze(1).to_broadcast([ap.shape[0], n, ap.shape[1]])

        XN = AR.alloc([128, 8, NTOK], BF16)
        BO = AR.alloc([128, 8, NTOK], BF16)
        CONST = AR.alloc([128, K_E], F32)
        VEC = AR.alloc([128, V_END], F32)
        IDB = AR.alloc([128, 128], BF16)
        NEGH = AR.alloc([128, 8], F32)
        SM = Rot(AR, [128, 8], F32, 12)
        ID = CONST[:, K_ID:K_ID + 128]; TRI = CONST[:, K_TRI:K_TRI + 128]; LST = CONST[:, K_L:K_L + 128]
        TRIL = CONST[:, K_TRIL:K_TRIL + 128]; ONES = CONST[:, K_ONES:K_ONES + 128]
        dma("sp", CONST, consts[:, 0:K_E]); dma("sp", VEC, vecs)
        cp("dve", IDB, ID)
        CB16 = AR.alloc([128, 3, 128], BF16)
        TRIb = CB16[:, 0, :]; ONESb = CB16[:, 1, :]; LSTb = CB16[:, 2, :]
        cp("dve", TRIb, TRI); cp("dve", ONESb, ONES); cp("dve", LSTb, LST)
        memset("pool", NEGH, -0.5)

        tiles4 = [(i * 512, 512) for i in range(4)] + [(NT, NS)]
        tiles17 = [(i * 128, 128) for i in range(16)] + [(NT, NS)]

        def rstd_of(ss, n, count, eps_scale=1.0):
            s = SM.next()
            ts("dve", s[0:n, 0:1], ss, 1.0 / count, EPS, ALU.mult, ALU.add)
            tt("pool", s[0:n, 1:2], s[0:n, 0:1], NEGH[0:n, 0:1], ALU.pow)
            return s[0:n, 1:2]

        def norm_to_T(src, n, t0, nwcol, dstT, XB, JK):
            s = SM.next()
            jk = JK.next()
            act(jk[0:n], src, AF.Square, accum=s[0:n, 0:1])
            r = rstd_of(s[0:n, 0:1], n, 1024.0)
            xb = XB.next()
            ts("dve", xb[0:n], src, r, None, ALU.mult)
            bank = pb().bitcast(BF16).rearrange("p (a b) -> p a b", a=8)
            for kc in range(8):
                tr(bank[:, kc, 0:n], xb[0:n, kc * 128:(kc + 1) * 128], IDB[0:n, 0:n])
            tt("dve", dstT[:, :, t0:t0 + n], bank[:, :, 0:n], bc3(VEC[:, nwcol:nwcol + 8], n), ALU.mult)

        m0 = AR.mark()
        XIN = Rot(AR, [128, 1024], F32, 3)
        XBr = Rot(AR, [128, 1024], BF16, 2)
        JKr = Rot(AR, [128, 1024], BF16, 1)
        mP = AR.mark()
        for (t0, n) in tiles17:
            xi = XIN.next()
            dma("sp", xi[0:n], x[t0:t0 + n, :] if t0 < NT else xs_in[:, :])
            norm_to_T(xi[0:n], n, t0, V_NMIX, XN, XBr, JKr)

        if stage <= 0:
            return finish()
        AR.release(m0)
        Wxbc = AR.alloc([128, 8, 1536], BF16); Wz = AR.alloc([128, 8, 1024], BF16); Wdt = AR.alloc([128, 8, 16], BF16)
        load_w(Wxbc, w_in[:, C_XBC:C_XBC + 1536], 1024, 1536)
        load_w(Wdt, w_in[:, C_DT:C_DT + 16], 1024, 16)
        load_w(Wz, w_in[:, C_Z:C_Z + 1024], 1024, 1024)
        if stage <= 0.5:
            return finish()
        RB = AR.alloc([128, 64], F32)
        DTB = RB[:, 0:16]; ALOG = RB[:, 16:32]; DSK = RB[:, 32:48]; ANEG = RB[:, 48:64]
        dma("sp", DTB, r_dtb.partition_broadcast(128)); dma("sp", ALOG, r_alog.partition_broadcast(128))
        dma("sp", DSK, r_dsk.partition_broadcast(128))
        act(ANEG, ALOG, AF.Exp)
        ts("dve", ANEG, ANEG, -1.0, None, ALU.mult)
        if stage <= 0.7:
            return finish()
        PRE_S = AR.alloc([128, 8, 256], F32)
        DT = PRE_S[:, 0, :]; AA = PRE_S[:, 1, :]; EAC = PRE_S[:, 2, :]; CD = PRE_S[:, 3, :]
        DEND = PRE_S[:, 4, :]; DTD = PRE_S[:, 5, :]; T6 = PRE_S[:, 6, :]; T7 = PRE_S[:, 7, :]
        bk = pb()
        for c in range(NCH):
            mm(bk[:, c * 16:(c + 1) * 16], [(XN[:, kc, c * 128:(c + 1) * 128], Wdt[:, kc, :]) for kc in range(8)])
        v3 = lambda ap: ap.rearrange("p (c h) -> p c h", c=16)
        tt("dve", v3(T6), v3(bk[:, 0:256]), bc_mid(DTB, 16), ALU.add)
        act(T7, T6, AF.Exp)
        ts("dve", T7, T7, 1.0, None, ALU.add)
        act(DT, T7, AF.Ln)
        if stage <= 0.8:
            return finish()
        tt("dve", v3(AA), v3(DT), bc_mid(ANEG, 16), ALU.mult)
        AHL = AR.alloc([128, 2, 256], BF16)
        AAh = AHL[:, 0, :]; AAl = AHL[:, 1, :]
        cp("dve", AAh, AA)
        tt("dve", T6, AA, AAh, ALU.subtract)
        cp("dve", AAl, T6)
        bk2 = pb()
        mm(bk2[:, 0:256], [(TRIb, AAh), (TRIb, AAl)])
        mm(bk2[:, 256:512], [(ONESb, AAh), (ONESb, AAl)])
        act(EAC, bk2[:, 0:256], AF.Exp)
        act(CD, bk2[:, 256:512], AF.Exp)
        act(T6, bk2[:, 0:256], AF.Copy)
        act(T7, bk2[:, 256:512], AF.Copy)
        tt("dve", T7, T7, T6, ALU.subtract)
        act(DEND, T7, AF.Exp)
        tt("dve", DTD, DT, DEND, ALU.mult)

        if stage <= 1:
            return finish()
        mS = AR.mark()
        HALO = AR.alloc([128, 12, 4], F32)
        memset("pool", HALO, 0.0)
        HW = 256
        PREc = Rot(AR, [128, HW + 4], F32, 2)
        XC = AR.alloc([128, 12, HW], F32)
        BCT2 = [AR.alloc([128, 4, HW], BF16), AR.alloc([128, 4, HW], BF16)]
        CTr = Rot(AR, [128, HW], F32, 1)
        XSr = Rot(AR, [128, 1024], F32, 2)
        XDTr = Rot(AR, [128, 16, 64], BF16, 2)
        XDTDr = Rot(AR, [128, 16, 64], BF16, 2)
        BTKr = Rot(AR, [128, 2, 128], BF16, 2)
        CBMr = Rot(AR, [128, 2, 128], F32, 2)
        RHr = Rot(AR, [128, 2, 4, 128], BF16, 2)
        ESr = Rot(AR, [128, 4, 128], F32, 2)
        MTr = Rot(AR, [128, 16, 128], BF16, 1)
        Y1r = Rot(AR, [128, 1024], F32, 1)
        XSDr = Rot(AR, [128, 1024], F32, 1)
        Zr = Rot(AR, [128, 1024], F32, 2)
        THr = Rot(AR, [128, 1024], F32, 1)
        YNr = Rot(AR, [128, 1024], BF16, 1)
        HT = AR.alloc([128, 1024], F32)
        HTb = AR.alloc([128, 1024], BF16)
        memset("pool", HT, 0.0)
        memset("pool", HTb, 0.0)
        JK2 = Rot(AR, [128, 512], BF16, 2)
        def conv_gen(T, bankf):
            t0 = T * HW
            for cc in range(12):
                if cc:
                    yield
                bank = bankf()
                mm(bank[:, 0:HW], [(Wxbc[:, kc, cc * 128:(cc + 1) * 128], XN[:, kc, t0:t0 + HW]) for kc in range(8)])
                pre = PREc.next()
                act(pre[:, 0:3], HALO[:, cc, 0:3], AF.Copy)
                act(pre[:, 3:HW + 3], bank[:, 0:HW], AF.Copy)
                cw = VEC[:, V_CW + cc * 4:V_CW + cc * 4 + 4]
                act(XC[:, cc, :], pre[:, 0:HW], AF.Identity, bias=VEC[:, V_CB + cc:V_CB + cc + 1], scale=cw[:, 0:1])
                for k in range(1, 4):
                    if True:
                        stt("dve", XC[:, cc, :], pre[:, k:k + HW], cw[:, k:k + 1], XC[:, cc, :], ALU.mult, ALU.add)
                    else:
                        ct = CTr.next()
                        ts("pool", ct, pre[:, k:k + HW], cw[:, k:k + 1], None, ALU.mult)
                        tt("pool", XC[:, cc, :], XC[:, cc, :], ct, ALU.add)
                act(HALO[:, cc, 0:3], pre[:, HW:HW + 3], AF.Copy)
                if t0 + HW == NT:
                    dma("sp", convp[:, cc * 128:(cc + 1) * 128].rearrange("k c -> c k"), pre[:, HW:HW + 3], slow=True)
            yield
            for cc in range(12):
                act(XC[:, cc, :], XC[:, cc, :], AF.Silu)
            for j in range(4):
                cp("act", BCT2[T % 2][:, j, :], XC[:, 8 + j, :])

        CH = {}
        pfi = [0]; pti = [0]

        def pbf():
            b = 6 + pfi[0] % 2; pfi[0] += 1
            return PS[:, b, :]

        def pbt():
            b = 4 + pti[0] % 2; pti[0] += 1
            return PS[:, b, :]

        def front(c):
            q = c % 2
            t0 = c * 128
            cs = slice(c * 16, (c + 1) * 16)
            qs = slice(q * 128, (q + 1) * 128)
            d = {}
            CH[c] = d
            BCT = BCT2[(c // 2) % 2]
            d["BCT"] = BCT
            XS = XSr.next()
            b0 = pbf(); b1 = pbf()
            for f in range(8):
                bank = b0 if f < 4 else b1
                tr(bank[:, (f % 4) * 128:(f % 4 + 1) * 128], XC[:, f, qs], ID)
            yield
            act(XS[:, 0:512], b0, AF.Copy)
            act(XS[:, 512:1024], b1, AF.Copy)
            b2 = pbf()
            for g in range(2):
                tr(b2[:, g * 128:(g + 1) * 128], XC[:, 8 + g, qs], ID)
            yield
            BTK = BTKr.next()
            cp("dve", BTK, b2[:, 0:256].rearrange("p (g n) -> p g n", g=2))
            yield
            XS3 = XS.rearrange("p (h d) -> p h d", h=16)
            XDT = XDTr.next(); XDTD = XDTDr.next()
            tt("dve", XDT, XS3, bc3(DT[:, cs], 64), ALU.mult)
            tt("pool", XDTD, XS3, bc3(DTD[:, cs], 64), ALU.mult)
            d.update(XS=XS, XS3=XS3, BTK=BTK, XDTD=XDTD, qs=qs, cs=cs, t0=t0)
            yield
            b3 = pbf()
            for g in range(2):
                mm(b3[:, g * 128:(g + 1) * 128], [(BCT[:, g, qs], BCT[:, 2 + g, qs])])
            CBM = CBMr.next()
            tt("dve", CBM, b3[:, 0:256].rearrange("p (g n) -> p g n", g=2), bc_mid(TRI, 2), ALU.mult)
            yield
            Z = Zr.next(); TH = THr.next()
            for hf in range(2):
                bz = pbf()
                mm(bz, [(XN[:, kc, t0:t0 + 128], Wz[:, kc, hf * 512:(hf + 1) * 512]) for kc in range(8)])
                act(Z[:, hf * 512:(hf + 1) * 512], bz, AF.Copy)
                act(TH[:, hf * 512:(hf + 1) * 512], bz, AF.Tanh, scale=0.5)
                yield
            stt("dve", Z, TH, 1.0, Z, ALU.add, ALU.mult)
            d["Z"] = Z
            yield
            MT = MTr.next()
            yd = [PS[:, 2 * (c % 2), :], PS[:, 2 * (c % 2) + 1, :]]
            d["yd"] = yd
            def mk_rh(hq):
                RH = RHr.next()
                hsl = slice(c * 16 + hq * 4, c * 16 + hq * 4 + 4)
                tt("dve", RH[:, 0], bc_mid(TRIb, 4), bc3(AAh[:, hsl], 128), ALU.mult)
                tt("dve", RH[:, 1], bc_mid(TRIb, 4), bc3(AAl[:, hsl], 128), ALU.mult)
                return RH

            def seg_mm(RH):
                bs = pbf()
                mm(bs, [(LSTb, RH[:, 0].rearrange("p a b -> p (a b)")), (LSTb, RH[:, 1].rearrange("p a b -> p (a b)"))])
                return bs

            segs = {}
            for hq in range(2):
                segs[hq] = seg_mm(mk_rh(hq))
                yield
            for hq in range(4):
                g = hq // 2
                ES = ESr.next()
                act(ES.rearrange("p a b -> p (a b)"), segs.pop(hq), AF.Exp)
                tt("dve", MT[:, hq * 4:hq * 4 + 4, :], ES, bc_mid(CBM[:, g, :], 4), ALU.mult)
                yield
                if hq + 2 < 4:
                    segs[hq + 2] = seg_mm(mk_rh(hq + 2))
                for hh in range(4):
                    h = hq * 4 + hh
                    mm(yd[h // 8][:, (h % 8) * 64:(h % 8 + 1) * 64], [(MT[:, h, :], XDT[:, h, :])])
                yield

        def tail(c):
            d = CH.pop(c)
            XS3 = d["XS3"]; BTK = d["BTK"]; XDTD = d["XDTD"]; qs = d["qs"]; cs = d["cs"]; t0 = d["t0"]; yd = d["yd"]; Z = d["Z"]
            BCT = d["BCT"]
            XSD = XSDr.next()
            tt("pool", XSD.rearrange("p (h d) -> p h d", h=16), XS3, bc3(DSK, 64), ALU.mult)
            Y1 = Y1r.next()
            if c > 0:
                yo = [pbt(), pbt()]
                for g in range(2):
                    mm(yo[g], [(BCT[:, 2 + g, qs], HTb[:, g * 512:(g + 1) * 512])])
                for g in range(2):
                    tt("dve", Y1[:, g * 512:(g + 1) * 512].rearrange("p (h d) -> p h d", h=8),
                       yo[g].rearrange("p (h d) -> p h d", h=8), bc3(EAC[:, c * 16 + g * 8:c * 16 + g * 8 + 8], 64), ALU.mult)
            yield
            stb = [pbt(), pbt()]
            for g in range(2):
                mm(stb[g], [(BTK[:, g, :], XDTD[:, g * 8:(g + 1) * 8, :].rearrange("p h d -> p (h d)"))])
            tt("dve", HT.rearrange("p (h d) -> p h d", h=16), HT.rearrange("p (h d) -> p h d", h=16), bc3(CD[:, cs], 64), ALU.mult)
            for g in range(2):
                tt("dve", HT[:, g * 512:(g + 1) * 512], stb[g], HT[:, g * 512:(g + 1) * 512], ALU.add)
            cp("act", HTb, HT)
            yield
            if c > 0:
                tt("dve", Y1, Y1, XSD, ALU.add)
                for g in range(2):
                    tt("dve", Y1[:, g * 512:(g + 1) * 512], yd[g], Y1[:, g * 512:(g + 1) * 512], ALU.add)
            else:
                for g in range(2):
                    tt("dve", Y1[:, g * 512:(g + 1) * 512], yd[g], XSD[:, g * 512:(g + 1) * 512], ALU.add)
            yield
            tt("dve", Y1, Y1, Z, ALU.mult)
            yield
            finish_ssd_tokens(Y1, 128, t0)
            yield

        def finish_ssd_tokens(Y2, n, t0):
            s = SM.next()
            for g in range(2):
                jk = JK2.next()
                act(jk[0:n], Y2[0:n, g * 512:(g + 1) * 512], AF.Square, accum=s[0:n, g:g + 1])
            s2 = SM.next()
            ts("dve", s2[0:n, 0:2], s[0:n, 0:2], 1.0 / (4 * 512), EPS, ALU.mult, ALU.add)
            tt("pool", s2[0:n, 2:4], s2[0:n, 0:2], NEGH[0:n, 0:2], ALU.pow)
            YN = YNr.next()
            ts("dve", s2[0:n, 4:6], s2[0:n, 2:4], 0.5, None, ALU.mult)
            for g in range(2):
                act(YN[0:n, g * 512:(g + 1) * 512], Y2[0:n, g * 512:(g + 1) * 512], AF.Copy, scale=s2[0:n, 4 + g:5 + g])
            bank = pbt().bitcast(BF16).rearrange("p (a b) -> p a b", a=8)
            for fc in range(8):
                tr(bank[:, fc, 0:n], YN[0:n, fc * 128:(fc + 1) * 128], IDB[0:n, 0:n])
            tt("dve", BO[:, :, t0:t0 + n], bank[:, :, 0:n], bc3(VEC[:, V_NSSD:V_NSSD + 8], n), ALU.mult)

        def drain(g):
            for _ in g:
                pass

        def interleave(g1, g2):
            d1 = d2 = False
            while not (d1 and d2):
                if not d1:
                    try:
                        next(g1)
                    except StopIteration:
                        d1 = True
                if not d2:
                    try:
                        next(g2)
                    except StopIteration:
                        d2 = True

        def interleave_all(gens):
            gens = list(gens)
            while gens:
                for g in list(gens):
                    try:
                        next(g)
                    except StopIteration:
                        gens.remove(g)

        drain(conv_gen(0, pb))
        if stage <= 2:
            return finish()
        drain(front(0))
        pb_banks[0] = [4, 5, 6, 7]
        for c in range(NCH):
            if c % 2 == 1 and c + 1 < NCH:
                drain(conv_gen((c + 1) // 2, pb))
            gens = [tail(c)]
            if c + 1 < NCH:
                gens.append(front(c + 1))
            interleave_all(gens)
        pb_banks[0] = list(range(8))
        for f in range(8):
            bank = pb()
            tr(bank[:, 0:128], HT[:, f * 128:(f + 1) * 128], ID)
            o = ESr.next()
            cp("act", o[:, 0, :], bank[:, 0:128])
            dma("sp", ssmp[f * 128:(f + 1) * 128, :], o[:, 0, :])

        if stage <= 3 and stage != 3.5:
            return finish()
        AR.release(mS)
        SB = slice(NT, NTOK)
        EM = AR.alloc([16, 1024], F32)
        dma("sp", EM, consts[0:16, K_E:K_E + 1024])
        SC48 = AR.alloc([48, 1536], F32)
        SCT = AR.alloc([128, 12, 48], F32)
        XBS = AR.alloc([16, 1536], F32)
        XBT = AR.alloc([128, 12, 16], F32)
        XCT = AR.alloc_top([128, 12, 16], F32)
        SMALL = AR.alloc([128, 16, 16], F32)
        DTE = AR.alloc([128, 8, 16], F32); DAE = AR.alloc_top([128, 8, 16], F32); DSE = AR.alloc_top([128, 8], F32)
        DTX = AR.alloc_top([128, 8, 16], F32); YST = AR.alloc_top([128, 8, 16], F32); ZST = AR.alloc_top([128, 8, 16], F32)
        ZSI = AR.alloc_top([128, 8, 16], F32)
        dma("sp", SC48, sconv.rearrange("b k c -> (b k) c"))
        dma("sp", convs[:, 0:2, :], sconv[:, 1:3, :])
        for cc in range(12):
            bank = pb()
            tr(bank[:, 0:48], SC48[:, cc * 128:(cc + 1) * 128], ID[0:48, 0:48])
            cp("act", SCT[:, cc, :], bank[:, 0:48])
        for j in range(3):
            bank = pb()
            mm(bank[0:16], [(XN[:, kc, SB], Wxbc[:, kc, j * 512:(j + 1) * 512]) for kc in range(8)])
            act(XBS[:, j * 512:(j + 1) * 512], bank[0:16], AF.Copy)
        dma("sp", convs[:, 2, :], XBS)
        bank = pb()
        for cc in range(12):
            mm(bank[:, cc * 16:(cc + 1) * 16], [(Wxbc[:, kc, cc * 128:(cc + 1) * 128], XN[:, kc, SB]) for kc in range(8)])
        act(XBT.rearrange("p a b -> p (a b)"), bank[:, 0:192], AF.Copy)
        for cc in range(12):
            cw = VEC[:, V_CW + cc * 4:V_CW + cc * 4 + 4]
            sv = SCT[:, cc, :].rearrange("p (b k) -> p b k", k=3)
            ts("dve", XCT[:, cc, :], XBT[:, cc, :], cw[:, 3:4], VEC[:, V_CB + cc:V_CB + cc + 1], ALU.mult, ALU.add)
            for k in range(3):
                stt("dve", XCT[:, cc, :], sv[:, :, k], cw[:, k:k + 1], XCT[:, cc, :], ALU.mult, ALU.add)
        act(XCT.rearrange("p a b -> p (a b)"), XCT.rearrange("p a b -> p (a b)"), AF.Silu)
        bank = pb()
        mm(bank[0:16, 0:16], [(Wdt[:, kc, :], XN[:, kc, SB]) for kc in range(8)])
        S0 = SMALL[0:16, 0, :]; S1 = SMALL[0:16, 1, :]; S2 = SMALL[0:16, 2, :]; S3 = SMALL[0:16, 3, :]; S4 = SMALL[0:16, 4, :]
        ts("dve", S0, bank[0:16, 0:16], VEC[0:16, V_DTB:V_DTB + 1], None, ALU.add)
        act(S1, S0, AF.Exp)
        ts("dve", S1, S1, 1.0, None, ALU.add)
        act(S2, S1, AF.Ln)
        act(S3[:, 0:1], VEC[0:16, V_ALOG:V_ALOG + 1], AF.Exp)
        ts("dve", S3[:, 1:2], S3[:, 0:1], -1.0, None, ALU.mult)
        ts("dve", S4, S2, S3[:, 1:2], None, ALU.mult)
        act(S4, S4, AF.Exp)
        bank = pb()
        for j in range(8):
            mm(bank[:, j * 16:(j + 1) * 16], [(EM[:, j * 128:(j + 1) * 128], S2)])
            mm(bank[:, 128 + j * 16:128 + (j + 1) * 16], [(EM[:, j * 128:(j + 1) * 128], S4)])
            mm(bank[:, 256 + j:256 + j + 1], [(EM[:, j * 128:(j + 1) * 128], VEC[0:16, V_DSK:V_DSK + 1])])
        act(DTE.rearrange("p a b -> p (a b)"), bank[:, 0:128], AF.Copy)
        act(DAE.rearrange("p a b -> p (a b)"), bank[:, 128:256], AF.Copy)
        act(DSE, bank[:, 256:264], AF.Copy)
        tt("dve", DTX, XCT[:, 0:8, :], DTE, ALU.mult)
        bank = pb()
        for j in range(8):
            mm(bank[:, j * 16:(j + 1) * 16], [(Wz[:, kc, j * 128:(j + 1) * 128], XN[:, kc, SB]) for kc in range(8)])
        act(ZSI.rearrange("p a b -> p (a b)"), bank[:, 0:128], AF.Silu)
        if stage <= 3.5:
            return finish()
        SAMPLE = True
        tilesF = tiles4 if SAMPLE else tiles4[:4]
        tilesT = tiles17 if SAMPLE else tiles17[:16]
        AR.release(m0)
        MACC = AR.alloc([128, 8, NTOK], BF16)
        mW = AR.mark()
        SGr = Rot(AR, [128, 512], F32, 2)
        TMr = Rot(AR, [128, 512], F32, 2)
        JK2 = Rot(AR, [128, 512], BF16, 2)
        mPh = AR.mark()
        Wu = AR.alloc([128, 8, 512], BF16); Wv = AR.alloc([128, 8, 512], BF16)
        mPh2 = AR.mark()

        def branch_out_g(Wbr, nk, SRC, Wg, mode, tiles, bankf=None):
            bankf = bankf or pb
            for (t0, n) in tiles:
                for fc in range(8):
                    yield
                    bb = bankf(); bg = bankf()
                    mm(bb[:, 0:n], [(Wbr[:, kc, fc * 128:(fc + 1) * 128], SRC[:, kc, t0:t0 + n]) for kc in range(nk)])
                    mm(bg[:, 0:n], [(Wg[:, kc, fc * 128:(fc + 1) * 128], XN[:, kc, t0:t0 + n]) for kc in range(8)])
                    sg = SGr.next()
                    act(sg[:, 0:n], bg[:, 0:n], AF.Sigmoid)
                    dst = MACC[:, fc, t0:t0 + n]
                    if mode == "first":
                        tt("dve", dst, bb[:, 0:n], sg[:, 0:n], ALU.mult)
                    else:
                        tm = TMr.next()
                        tt("dve", tm[:, 0:n], bb[:, 0:n], sg[:, 0:n], ALU.mult)
                        if mode == "mid":
                            tt("dve", dst, dst, tm[:, 0:n], ALU.add)
                        else:
                            tt("dve", BO[:, fc, t0:t0 + n], dst, tm[:, 0:n], ALU.add)

        def branch_out(Wbr, nk, SRC, Wg, mode):
            drain(branch_out_g(Wbr, nk, SRC, Wg, mode, tilesF))

        Wbs = AR.alloc([128, 8, 1024], BF16); Wg1 = AR.alloc([128, 8, 1024], BF16)
        load_w(Wbs, w_bs, 1024, 1024)
        load_w(Wg1, w_in[:, C_G + 1024:C_G + 2048], 1024, 1024)
        load_w(Wu, w_in[:, C_U:C_U + 512], 1024, 512); load_w(Wv, w_in[:, C_V:C_V + 512], 1024, 512)
        mSL = AR.mark()
        Hr = Rot(AR, [128, 8, 128], F32, 3); T2r = Rot(AR, [128, 8, 128], F32, 2); DGr = Rot(AR, [128, 4, 128], F32, 2)
        SML2 = AR.alloc([128, 4, 16], F32)

        def sample_state_loop():
            Hs = {}

            def load_state(b):
                Hs[b] = Hr.next()
                dma("sp", Hs[b], sssm[b].rearrange("(j p) n -> p j n", p=128))

            load_state(0)
            for b in range(16):
                if b + 1 < 16:
                    load_state(b + 1)
                DG = DGr.next()
                tt("pool", DG, bc_mid(ID, 4), bc3(XCT[:, 8:12, b], 128), ALU.mult)
                bcb = pb()
                mm(bcb, [(ONES, DG.rearrange("p a b -> p (a b)"))])
                H = Hs.pop(b); T2 = T2r.next()
                yield
                for j in range(8):
                    act(H[:, j, :], H[:, j, :], AF.Copy, scale=DAE[:, j, b:b + 1])
                Bv = bcb[:, 0:256].rearrange("p (g n) -> p g n", g=2).unsqueeze(2).to_broadcast([128, 2, 4, 128])
                Cv = bcb[:, 256:512].rearrange("p (g n) -> p g n", g=2).unsqueeze(2).to_broadcast([128, 2, 4, 128])
                T24 = T2.rearrange("p (g j) n -> p g j n", g=2)
                dxb = DTX[:, :, b].rearrange("p (g j) -> p g j", g=2).unsqueeze(3).to_broadcast([128, 2, 4, 128])
                tt("dve", T24, Bv, dxb, ALU.mult)
                yield
                tt("dve", H, H, T2, ALU.add)
                dma("sp", ssms[b].rearrange("(j p) n -> p j n", p=128), H)
                tt("dve", T24, Cv, H.rearrange("p (g j) n -> p g j n", g=2), ALU.mult)
                red("dve", YST[:, :, b], T2, ALU.add)
                yield

        if SAMPLE:
            def twice(g):
                while True:
                    try:
                        next(g); next(g)
                    except StopIteration:
                        return
                    yield
            interleave(twice(sample_state_loop()), branch_out_g(Wbs, 8, BO, Wg1, "first", tiles4[:4]))
            tt("dve", ZST, XCT[:, 0:8, :], bc3(DSE, 16), ALU.mult)
            tt("dve", YST, YST, ZST, ALU.add)
            tt("dve", YST, YST, ZSI, ALU.mult)
            tt("dve", ZST, YST, YST, ALU.mult)
            bank = pb()
            for g in range(2):
                mm(bank[:, g * 16:(g + 1) * 16], [(ONES, ZST[:, 4 * g + j, :]) for j in range(4)])
            RS2 = SML2[:, 0:2, :]; NH32 = SML2[:, 2:4, :]
            ts("dve", RS2.rearrange("p a b -> p (a b)"), bank[:, 0:32], 1.0 / 512, EPS, ALU.mult, ALU.add)
            memset("pool", NH32, -0.5)
            tt("pool", RS2, RS2, NH32, ALU.pow)
            for g in range(2):
                tt("dve", YST[:, 4 * g:4 * g + 4, :], YST[:, 4 * g:4 * g + 4, :], bc_mid(RS2[:, g, :], 4), ALU.mult)
            tt("dve", BO[:, :, SB], YST, bc3(VEC[:, V_NSSD:V_NSSD + 8], 16), ALU.mult)
            drain(branch_out_g(Wbs, 8, BO, Wg1, "first", tiles4[4:]))
        else:
            branch_out(Wbs, 8, BO, Wg1, "first")
        AR.release(mSL)
        if stage <= 4:
            return finish()
        AR.release(mPh2)
        Wg0 = AR.alloc([128, 8, 1024], BF16); Wbg = AR.alloc([128, 4, 1024], BF16)
        load_w(Wg0, w_in[:, C_G:C_G + 1024], 1024, 1024); load_w(Wbg, w_bg, 512, 1024)
        GM = AR.alloc([128, 4, NTOK], BF16)
        mP3b = AR.mark()
        WST = AR.alloc([128, 4, 128], BF16)
        LNW = AR.alloc([128, 512], F32); LNB = AR.alloc([128, 512], F32); BSB = AR.alloc([128, 512], F32)
        dma("sp", LNW, r_lnw.partition_broadcast(128)); dma("sp", LNB, r_lnb.partition_broadcast(128))
        dma("sp", BSB, r_bs.partition_broadcast(128))
        for g in range(4):
            wt_ = TMr.next()
            dma("sp", wt_[:, 0:128], gm_ws[g])
            tt("pool", wt_[:, 0:128], wt_[:, 0:128], TRIL, ALU.mult)
            bank = pb()
            tr(bank[:, 0:128], wt_[:, 0:128], ID)
            cp("act", WST[:, g, :], bank[:, 0:128])
        UT = AR.alloc([128, 4, 512], F32)
        Vr = Rot(AR, [128, 512], F32, 2); VNr = Rot(AR, [128, 512], F32, 2); VBr = Rot(AR, [128, 512], BF16, 2)
        def p3_uproj(T):
            t0 = T * 512
            for g in range(4):
                bank = pb()
                mm(bank, [(Wu[:, kc, g * 128:(g + 1) * 128], XN[:, kc, t0:t0 + 512]) for kc in range(8)])
                act(UT[:, g, :], bank, AF.Gelu)

        def p3_chunk(c, banks):
            c0 = c * 128; q = c % 4
            bank = PS[:, banks[0], :]
            mm(bank, [(XN[:, kc, c0:c0 + 128], Wv[:, kc, :]) for kc in range(8)])
            yield
            V = Vr.next(); s = SM.next()
            act(V, bank, AF.Gelu, accum=s[:, 0:1])
            jk = JK2.next()
            act(jk, V, AF.Square, accum=s[:, 1:2])
            yield
            s2 = SM.next()
            ts("dve", s2[:, 0:1], s[:, 0:1], 1.0 / 512, None, ALU.mult)
            tt("dve", s2[:, 1:2], s2[:, 0:1], s2[:, 0:1], ALU.mult)
            yield
            stt("dve", s2[:, 2:3], s[:, 1:2], 1.0 / 512, s2[:, 1:2], ALU.mult, ALU.subtract)
            ts("dve", s2[:, 3:4], s2[:, 2:3], EPS, None, ALU.add)
            yield
            tt("pool", s2[:, 4:5], s2[:, 3:4], NEGH[:, 0:1], ALU.pow)
            yield
            VN = VNr.next()
            ts("dve", VN, V, s2[:, 0:1], s2[:, 4:5], ALU.subtract, ALU.mult)
            yield
            tt("dve", VN, VN, LNW, ALU.mult)
            yield
            VB = VBr.next()
            tt("dve", VB, VN, LNB, ALU.add)
            yield
            bank2 = PS[:, banks[1], :]
            for g in range(4):
                mm(bank2[:, g * 128:(g + 1) * 128], [(VB[:, g * 128:(g + 1) * 128], WST[:, g, :])])
            yield
            tm = TMr.next()
            tt("dve", tm, bank2, BSB, ALU.add)
            yield
            tt("dve", GM[:, :, c0:c0 + 128], tm.rearrange("p (g t) -> p g t", g=4), UT[:, :, q * 128:(q + 1) * 128], ALU.mult)

        for T in range(4):
            pb_banks[0] = [4, 5, 6, 7]
            p3_uproj(T)
            for c in (4 * T, 4 * T + 2):
                interleave_all([p3_chunk(c, (0, 1)), p3_chunk(c + 1, (2, 3))])
        pb_banks[0] = list(range(8))
        if SAMPLE:
            WS0 = AR.alloc([16, 8], F32)
            dma("sp", WS0[:, 0:4], r_ws00.partition_broadcast(16)); dma("sp", WS0[:, 4:8], r_bs0.partition_broadcast(16))
            bu = pb()
            mm(bu[0:16], [(XN[:, kc, SB], Wu[:, kc, :]) for kc in range(8)])
            US = Vr.next()
            act(US[0:16], bu[0:16], AF.Gelu)
            bv = pb()
            mm(bv[0:16], [(XN[:, kc, SB], Wv[:, kc, :]) for kc in range(8)])
            V = Vr.next(); s = SM.next()
            act(V[0:16], bv[0:16], AF.Gelu, accum=s[0:16, 0:1])
            jk = JK2.next()
            act(jk[0:16], V[0:16], AF.Square, accum=s[0:16, 1:2])
            s2 = SM.next()
            ts("dve", s2[0:16, 0:1], s[0:16, 0:1], 1.0 / 512, None, ALU.mult)
            tt("dve", s2[0:16, 1:2], s2[0:16, 0:1], s2[0:16, 0:1], ALU.mult)
            stt("dve", s2[0:16, 2:3], s[0:16, 1:2], 1.0 / 512, s2[0:16, 1:2], ALU.mult, ALU.subtract)
            ts("dve", s2[0:16, 3:4], s2[0:16, 2:3], EPS, None, ALU.add)
            tt("pool", s2[0:16, 4:5], s2[0:16, 3:4], NEGH[0:16, 0:1], ALU.pow)
            VN = VNr.next()
            ts("dve", VN[0:16], V[0:16], s2[0:16, 0:1], s2[0:16, 4:5], ALU.subtract, ALU.mult)
            tt("pool", VN[0:16], VN[0:16], LNW[0:16], ALU.mult)
            tt("pool", VN[0:16], VN[0:16], LNB[0:16], ALU.add)
            dma("sp", gv, VN[0:16])
            MX = VNr.next()
            for g in range(4):
                ts("dve", MX[0:16, g * 128:(g + 1) * 128], VN[0:16, g * 128:(g + 1) * 128], WS0[:, g:g + 1], WS0[:, 4 + g:5 + g], ALU.mult, ALU.add)
            VB = VBr.next()
            tt("pool", VB[0:16], MX[0:16], US[0:16], ALU.mult)
            bank = pb().bitcast(BF16).rearrange("p (a b) -> p a b", a=8)
            for g in range(4):
                tr(bank[:, g, 0:16], VB[0:16, g * 128:(g + 1) * 128], IDB[0:16, 0:16])
            cp("act", GM[:, :, SB], bank[:, 0:4, 0:16])
        AR.release(mP3b)
        Wmk = AR.alloc([128, 8, 512], BF16); Wmv = AR.alloc([128, 8, 512], BF16)
        load_w(Wmk, w_mk, 1024, 512); load_w(Wmv, w_mv, 1024, 512)
        mP3c = AR.mark()
        branch_out(Wbg, 4, GM, Wg0, "mid")
        if stage <= 5:
            return finish()
        AR.release(mPh)
        XA = AR.alloc([128, 4, NTOK], BF16)
        KT = AR.alloc([128, 4, 256], BF16); VBm = AR.alloc([128, 2, 512], BF16)
        mC = AR.mark()
        assert mC <= mP3b, (mC, mP3b)
        AR.release(mP3c)
        MNT = AR.alloc([128, 8, 256], BF16)
        MEMr = Rot(AR, [128, 1024], F32, 1); XB4 = Rot(AR, [128, 1024], BF16, 1); JK4 = Rot(AR, [128, 1024], BF16, 1)
        for mt in range(2):
            mi = MEMr.next()
            dma("sp", mi, mem[mt * 128:(mt + 1) * 128, :])
            norm_to_T(mi, 128, mt * 128, V_NMEM, MNT, XB4, JK4)
        for mt in range(2):
            for (W_, o_, isv) in ((Wmk, mk, False), (Wmv, mv, True)):
                bank = pb()
                mm(bank, [(MNT[:, kc, mt * 128:(mt + 1) * 128], W_[:, kc, :]) for kc in range(8)])
                ko = TMr.next()
                act(ko, bank, AF.Copy)
                dma("sp", o_[mt * 128:(mt + 1) * 128, :], ko)
                if isv:
                    cp("act", VBm[:, mt, :], ko)
        for h in range(4):
            bank = pb()
            mm(bank[:, 0:256], [(Wmk[:, kc, h * 128:(h + 1) * 128], MNT[:, kc, :]) for kc in range(8)])
            cp("act", KT[:, h, :], bank[:, 0:256])
        AR.release(mC)
        Wq = AR.alloc([128, 8, 512], BF16); Wg2 = AR.alloc([128, 8, 1024], BF16); Wbx = AR.alloc([128, 4, 1024], BF16)
        load_w(Wq, w_in[:, C_Q:C_Q + 512], 1024, 512)
        load_w(Wg2, w_in[:, C_G + 2048:C_G + 3072], 1024, 1024); load_w(Wbx, w_bx, 512, 1024)
        mQ = AR.mark()
        QT = AR.alloc([128, 4, 512], BF16)
        Pr = Rot(AR, [128, 4, 256], BF16, 2); PNr = Rot(AR, [128, 4, 256], BF16, 2); PTr = Rot(AR, [128, 8, 128], BF16, 2)
        def p4_qproj(T):
            t0 = T * 512
            for h in range(4):
                bank = pb()
                mm(bank, [(Wq[:, kc, h * 128:(h + 1) * 128], XN[:, kc, t0:t0 + 512]) for kc in range(8)])
                act(QT[:, h, :], bank, AF.Copy, scale=float(128 ** -0.5))

        def p4_chunk(c, banks):
            q = c % 4; c0 = c * 128
            sc = [PS[:, banks[0], :], PS[:, banks[1], :]]
            for h in range(4):
                mm(sc[h // 2][:, (h % 2) * 256:(h % 2 + 1) * 256], [(QT[:, h, q * 128:(q + 1) * 128], KT[:, h, :])])
            yield
            s = SM.next(); s2 = SM.next(); s3 = SM.next()
            for j in range(2):
                red("dve", s[:, 2 * j:2 * j + 2], sc[j].rearrange("p (h m) -> p h m", h=2), ALU.max)
            yield
            ts("dve", s2[:, 0:4], s[:, 0:4], -1.0, None, ALU.mult)
            yield
            P = Pr.next()
            for h in range(4):
                act(P[:, h, :], sc[h // 2][:, (h % 2) * 256:(h % 2 + 1) * 256], AF.Exp, bias=s2[:, h:h + 1], accum=s3[:, h:h + 1])
            yield
            S.add("dve", lambda e, o=s3[:, 4:8], i=s3[:, 0:4]: e.reciprocal(out=o, in_=i), reads=[s3[:, 0:4]], writes=[s3[:, 4:8]])
            yield
            PN = PNr.next()
            tt("dve", PN, P, bc3(s3[:, 4:8], 256), ALU.mult)
            yield
            bank = PS[:, banks[2], :].bitcast(BF16).rearrange("p (a b) -> p a b", a=8)
            for h in range(4):
                for mh in range(2):
                    tr(bank[:, h * 2 + mh, :], PN[:, h, mh * 128:(mh + 1) * 128], IDB)
            yield
            PT = PTr.next()
            cp("act", PT, bank)
            yield
            bo = PS[:, banks[3], :]
            for h in range(4):
                mm(bo[:, h * 128:(h + 1) * 128], [(VBm[:, mh, h * 128:(h + 1) * 128], PT[:, h * 2 + mh, :]) for mh in range(2)])
            yield
            cp("act", XA[:, :, c0:c0 + 128], bo.rearrange("p (h t) -> p h t", h=4))

        for T in range(4):
            p4_qproj(T)
            for c in (4 * T, 4 * T + 2):
                interleave_all([p4_chunk(c, (0, 1, 2, 3)), p4_chunk(c + 1, (4, 5, 6, 7))])
        if SAMPLE:
            AR.release(mQ)
            QS = AR.alloc([16, 512], F32); SCTs = AR.alloc([128, 2, 64], F32); OH4 = AR.alloc([4, 4], BF16)
            PSs = AR.alloc([64, 256], F32); PTs = AR.alloc([128, 2, 64], BF16)
            KSr = Rot(AR, [128, 2, 512], F32, 2); PRr = Rot(AR, [128, 2, 512], F32, 1); SELr = Rot(AR, [16, 128], F32, 2)
            VBsr = Rot(AR, [128, 2, 512], BF16, 2); OBr = Rot(AR, [4, 512], BF16, 2)
            cp("dve", OH4, CONST[0:4, K_OH:K_OH + 4])
            bq = pb()
            mm(bq[0:16], [(XN[:, kc, SB], Wq[:, kc, :]) for kc in range(8)])
            act(QS, bq[0:16], AF.Copy, scale=float(128 ** -0.5))
            for b in range(16):
                K_ = KSr.next()
                dma("sp", K_, ck[b].rearrange("(mh m) f -> m mh f", mh=2))
                SEL = SELr.next()
                cp("act", SEL, ID[0:16, b:b + 1].to_broadcast([16, 128]))
                bqb = pb()
                mm(bqb, [(SEL, QS)])
                PR = PRr.next()
                tt("dve", PR, K_, bc_mid(bqb, 2), ALU.mult)
                red("dve", SCTs[:, :, 4 * b:4 * b + 4], PR.rearrange("p a (h d) -> p a h d", h=4), ALU.add)
            bs_ = pb()
            for mh in range(2):
                tr(bs_[0:64, mh * 128:(mh + 1) * 128], SCTs[:, mh, :], ID)
            sA = SM.next()
            red("dve", sA[0:64, 0:1], bs_[0:64, 0:256], ALU.max)
            ts("dve", sA[0:64, 1:2], sA[0:64, 0:1], -1.0, None, ALU.mult)
            act(PSs, bs_[0:64, 0:256], AF.Exp, bias=sA[0:64, 1:2], accum=sA[0:64, 2:3])
            S.add("dve", lambda e, o=sA[0:64, 3:4], i=sA[0:64, 2:3]: e.reciprocal(out=o, in_=i), reads=[sA[0:64, 2:3]], writes=[sA[0:64, 3:4]])
            ts("dve", PSs, PSs, sA[0:64, 3:4], None, ALU.mult)
            bt_ = pb()
            for mh in range(2):
                tr(bt_[:, mh * 64:(mh + 1) * 64], PSs[:, mh * 128:(mh + 1) * 128], ID[0:64, 0:64])
            cp("act", PTs.rearrange("p a b -> p (a b)"), bt_[:, 0:128])
            for b in range(16):
                V_ = KSr.next()
                dma("sp", V_, cv[b].rearrange("(mh m) f -> m mh f", mh=2))
                VBs = VBsr.next()
                cp("act", VBs, V_)
                bo_ = pb()
                mm(bo_[0:4], [(PTs[:, mh, 4 * b:4 * b + 4], VBs[:, mh, :]) for mh in range(2)])
                OB = OBr.next()
                cp("act", OB, bo_[0:4])
                bxb = pb()
                for h in range(4):
                    mm(bxb[:, h:h + 1], [(OB[:, h * 128:(h + 1) * 128], OH4[:, h:h + 1])])
                cp("act", XA[:, :, NT + b:NT + b + 1], bxb[:, 0:4].rearrange("p (h o) -> p h o", o=1))
        branch_out(Wbx, 4, XA, Wg2, "last")
        if stage <= 6:
            return finish()
        AR.release(m0)
        H2 = AR.alloc([128, 17, 1024], F32)
        mF = AR.mark()
        Wo = AR.alloc([128, 8, 1024], BF16)
        load_w(Wo, w_o, 1024, 1024)
        XIN2 = Rot(AR, [128, 1024], F32, 2); XB5 = Rot(AR, [128, 1024], BF16, 2); JK5 = Rot(AR, [128, 1024], BF16, 1)
        def p5_A(ti):
            t0, n = tilesT[ti]
            xi = XIN2.next()
            dma("sp", xi[0:n], x[t0:t0 + n, :] if t0 < NT else xs_in[:, :])
            for hf in range(2):
                bank = pb()
                mm(bank[0:n], [(BO[:, kc, t0:t0 + n], Wo[:, kc, hf * 512:(hf + 1) * 512]) for kc in range(8)])
                tt("dve", H2[0:n, ti, hf * 512:(hf + 1) * 512], bank[0:n], xi[0:n, hf * 512:(hf + 1) * 512], ALU.add)

        p5_A(0)
        for ti, (t0, n) in enumerate(tilesT):
            if ti + 1 < len(tilesT):
                p5_A(ti + 1)
            norm_to_T(H2[0:n, ti, :], n, t0, V_NFFN, XN, XB5, JK5)
        if stage <= 7:
            return finish()
        AR.release(mF)
        Wupr = Rot(AR, [128, 8, 512], BF16, 2); Wdnr = Rot(AR, [128, 4, 1024], BF16, 2)
        RL = Rot(AR, [128, 512], BF16, 2)
        NFW = AR.alloc([128, 1024], F32)
        dma("sp", NFW, r_nfin.partition_broadcast(128))
        YOr = Rot(AR, [128, 1024], F32, 2); JK6 = Rot(AR, [128, 1024], BF16, 1)
        wnext = (Wupr.next(), Wdnr.next())
        load_w(wnext[0], w_up[:, 0:512], 1024, 512); load_w(wnext[1], w_dn[0:512, :], 512, 1024)
        for dc in range(8):
            wu, wd = wnext
            if dc < 7:
                wnext = (Wupr.next(), Wdnr.next())
                load_w(wnext[0], w_up[:, (dc + 1) * 512:(dc + 2) * 512], 1024, 512)
                load_w(wnext[1], w_dn[(dc + 1) * 512:(dc + 2) * 512, :], 512, 1024)
            AT = BO[:, (dc % 2) * 4:(dc % 2) * 4 + 4, :]
            for (t0, n) in tilesF:
                for j in range(4):
                    bank = pb()
                    mm(bank[:, 0:n], [(wu[:, kc, j * 128:(j + 1) * 128], XN[:, kc, t0:t0 + n]) for kc in range(8)])
                    rl = RL.next()
                    act(rl[:, 0:n], bank[:, 0:n], AF.Relu)
                    tt("pool", AT[:, j, t0:t0 + n], rl[:, 0:n], rl[:, 0:n], ALU.mult)
            for ti, (t0, n) in enumerate(tilesT):
                for hf in range(2):
                    bank = pb()
                    mm(bank[0:n], [(AT[:, j, t0:t0 + n], wd[:, j, hf * 512:(hf + 1) * 512]) for j in range(4)])
                    hs = H2[0:n, ti, hf * 512:(hf + 1) * 512]
                    tt("dve", hs, bank[0:n], hs, ALU.add)
        for ti, (t0, n) in enumerate(tilesT):
            s = SM.next(); jk = JK6.next()
            act(jk[0:n], H2[0:n, ti, :], AF.Square, accum=s[0:n, 0:1])
            r = rstd_of(s[0:n, 0:1], n, 1024.0)
            yo = YOr.next()
            stt("dve", yo[0:n], H2[0:n, ti, :], r, NFW[0:n], ALU.mult, ALU.mult)
            dma("sp", y[t0:t0 + n, :] if t0 < NT else ys[:, :], yo[0:n])
        return finish()


def make_consts():
    c = np.zeros((128, K_END), np.float32)
    i = np.arange(128)
    c[:, K_ID:K_ID + 128] = np.eye(128)
    c[:, K_TRI:K_TRI + 128] = (i[:, None] <= i[None, :])
    c[:, K_L:K_L + 128] = (i[:, None] > i[None, :])
    c[:, K_TRIL:K_TRIL + 128] = (i[None, :] <= i[:, None])
    c[:, K_ONES:K_ONES + 128] = 1.0
    for h in range(4):
        c[h, K_OH + h] = 1.0
    for h in range(16):
        c[h, K_E + h * 64:K_E + (h + 1) * 64] = 1.0
    return c


def kernel(x_prompt, x_sample, mem_prompt, cache_mem_k, cache_mem_v, state_conv, state_ssm,
           norm_mix_w, w_in, gm_ln_w, gm_ln_b, gm_ws, gm_bs, conv_w, conv_b, dt_bias, a_log,
           d_skip, ssd_norm_w, mem_norm_w, w_mem_k, w_mem_v, w_br_gm, w_br_ssd, w_br_xa, w_out,
           norm_ffn_w, w_up, w_down, norm_final_w, _stage=99):
    f = lambda a: np.ascontiguousarray(np.asarray(a, dtype=np.float32))
    vec = np.zeros((128, V_END), np.float32)
    pl = lambda v: f(v).reshape(-1, 128).T
    vec[:, V_NMIX:V_NMIX + 8] = pl(norm_mix_w[0]); vec[:, V_NMEM:V_NMEM + 8] = pl(mem_norm_w[0])
    vec[:, V_NSSD:V_NSSD + 8] = pl(ssd_norm_w[0]); vec[:, V_NFFN:V_NFFN + 8] = pl(norm_ffn_w[0])
    cw = f(conv_w[0])
    vec[:, V_CW:V_CW + 48] = cw.reshape(4, 12, 128).transpose(2, 1, 0).reshape(128, 48)
    vec[:, V_CB:V_CB + 12] = pl(conv_b[0])
    vec[0:16, V_DTB] = f(dt_bias[0]); vec[0:16, V_ALOG] = f(a_log[0]); vec[0:16, V_DSK] = f(d_skip[0])
    shared = {
        "w_in": f(w_in[0]), "w_mk": f(w_mem_k[0]), "w_mv": f(w_mem_v[0]), "w_bg": f(w_br_gm[0]),
        "w_bs": f(w_br_ssd[0]), "w_bx": f(w_br_xa[0]), "w_o": f(w_out[0]), "w_up": f(w_up[0]), "w_dn": f(w_down[0]),
        "consts": make_consts(), "vecs": vec, "gm_ws": f(gm_ws[0]),
        "r_lnw": f(gm_ln_w[0]).reshape(1, 512), "r_lnb": f(gm_ln_b[0]).reshape(1, 512), "r_bs": f(gm_bs[0]).reshape(1, 512),
        "r_dtb": f(dt_bias[0]).reshape(1, 16), "r_alog": f(a_log[0]).reshape(1, 16), "r_dsk": f(d_skip[0]).reshape(1, 16),
        "r_nfin": f(norm_final_w).reshape(1, 1024), "r_cw": f(conv_w[0]).reshape(1, 4 * 1536), "r_cb": f(conv_b[0]).reshape(1, 1536),
        "r_ws00": f(gm_ws[0][:, 0, 0]).reshape(1, 4), "r_bs0": f(gm_bs[0][:, 0]).reshape(1, 4),
    }
    in_maps = []
    for c in range(8):
        m = dict(shared)
        sl = slice(16 * c, 16 * c + 16)
        m["x"] = f(x_prompt[c]); m["xs"] = f(x_sample[sl, 0]); m["mem"] = f(mem_prompt[c])
        m["ck"] = f(cache_mem_k[0, sl]).reshape(16, 256, 512); m["cv"] = f(cache_mem_v[0, sl]).reshape(16, 256, 512)
        m["sconv"] = f(state_conv[0, sl]); m["sssm"] = f(state_ssm[0, sl]).reshape(16, 1024, 128)
        in_maps.append(m)
    nc = build(_stage)
    res = run_bass_kernel_spmd(nc, in_maps, core_ids=list(range(8)))
    DEBUG["exec_ns"] = getattr(res, "exec_time_ns", None)
    R = res.results
    g = lambda k: np.stack([np.asarray(R[c][k], dtype=np.float32) for c in range(8)])
    y_prompt = g("y")
    y_sample = g("ys").reshape(128, 1, 1024)
    mem_k = g("mk").reshape(1, 8, 256, 4, 128)
    mem_v = g("mv").reshape(1, 8, 256, 4, 128)
    conv_p = g("convp").reshape(1, 8, 3, 1536)
    ssm_p = g("ssmp").reshape(1, 8, 16, 64, 128)
    conv_s = g("convs").reshape(1, 128, 3, 1536)
    ssm_s = g("ssms").reshape(1, 128, 16, 64, 128)
    gv_s = g("gv").reshape(1, 128, 1, 512)
    return (y_prompt, y_sample, mem_k, mem_v, conv_p, ssm_p, conv_s, ssm_s, gv_s)
```

```python
import numpy as np
from contextlib import ExitStack
import concourse.bass as bass
import concourse.mybir as mybir
from concourse.bass_utils import run_bass_kernel_spmd

F32 = mybir.dt.float32
BF16 = mybir.dt.bfloat16
AF = mybir.ActivationFunctionType
ALU = mybir.AluOpType
AX = mybir.AxisListType
ESZ = {F32: 4, BF16: 2}
PAGE = 256
COMPUTE = ("pe", "act", "dve", "pool")
EPS = 1e-6
NT = 2048
NS = 16
NTOK = NT + NS
NCH = 16


class Op:
    __slots__ = ("eng", "fn", "deps", "signal", "is_dma", "sem", "val", "q")

    def __init__(self, eng, fn, is_dma):
        self.eng = eng; self.fn = fn; self.deps = set(); self.signal = False
        self.is_dma = is_dma; self.sem = None; self.val = 0; self.q = None


class Sched:
    def __init__(self, nc, ndma_sems=24):
        self.nc = nc
        self.ops = {e: [] for e in ("pe", "act", "dve", "pool", "sp")}
        self.last_w = {}
        self.rd = {}
        self.ndma = ndma_sems
        self.dma_count = {"sp": 0, "act": 0, "pool": 0}
        self.dma_ops = []

    @staticmethod
    def pages(ap):
        sp = str(ap.space).upper()
        if "DRAM" in sp or "HBM" in sp:
            return ()
        a = ap.ap
        es = ESZ.get(ap.dtype, 4)
        pstep = a[0][0]
        off = int(ap.offset)
        lo = off % pstep if pstep > 0 else off
        hi = lo
        for st, cnt in a[1:]:
            hi += st * (cnt - 1)
        hi += 1
        name = ap.tensor.name
        pg = 2048 if name == "ps" else PAGE
        return [(name, p) for p in range((lo * es) // pg, (hi * es - 1) // pg + 1)]

    def add(self, eng, fn, reads=(), writes=(), dma=False):
        op = Op(eng, fn, dma)
        deps = set()
        rpages = []
        for ap in reads:
            rpages.extend(self.pages(ap))
        wpages = []
        for ap in writes:
            wpages.extend(self.pages(ap))
        for pg in rpages:
            w = self.last_w.get(pg)
            if w is not None:
                if not (w.eng == eng and eng == "pe" and not w.is_dma and not dma):
                    deps.add(w)
        for pg in wpages:
            w = self.last_w.get(pg)
            if w is not None:
                if w.is_dma or dma or w.eng != eng or eng != "pe":
                    deps.add(w)
            r = self.rd.get(pg)
            if r:
                for o in r.values():
                    if o.is_dma or dma or o.eng != eng or eng != "pe":
                        deps.add(o)
        for pg in rpages:
            r = self.rd.get(pg)
            if r is None:
                r = {}; self.rd[pg] = r
            if dma:
                r[("dma", id(op))] = op
            else:
                r[eng] = op
        for pg in wpages:
            self.last_w[pg] = op
            self.rd[pg] = {}
        deps.discard(op)
        op.deps = deps
        for d in deps:
            d.signal = True
        self.ops[eng].append(op)
        if dma:
            k = self.dma_count[eng]
            self.dma_count[eng] += 1
            op.q = (eng, k)
            self.dma_ops.append(op)
        return op

    def emit(self, stack):
        nc = self.nc
        sems = {e: stack.enter_context(nc.semaphore("s_" + e)) for e in COMPUTE}
        dsems = {}
        for q, cnt in self.dma_count.items():
            if cnt:
                dsems[q] = [stack.enter_context(nc.semaphore("d_%s_%d" % (q, i))) for i in range(min(cnt, self.ndma))]
        for e in COMPUTE:
            c = 0
            for op in self.ops[e]:
                if op.is_dma:
                    continue
                if op.signal:
                    c += 1
                    op.sem = sems[e]; op.val = c
        for op in self.dma_ops:
            q, k = op.q
            op.sem = dsems[q][k % self.ndma]
            op.val = 16 * (k // self.ndma + 1)
        final_waits = {}
        for op in self.dma_ops:
            final_waits[op.sem] = max(final_waits.get(op.sem, 0), op.val)

        def run(name, e):
            seen = {}
            for op in self.ops[name]:
                waits = {}
                for d in op.deps:
                    if waits.get(d.sem, 0) < d.val:
                        waits[d.sem] = d.val
                if op.is_dma:
                    q, k = op.q
                    if k >= self.ndma:
                        v = op.val - 16
                        if waits.get(op.sem, 0) < v:
                            waits[op.sem] = v
                for s, v in waits.items():
                    if seen.get(s, 0) < v:
                        e.wait_ge(s, v)
                        seen[s] = v
                inst = op.fn(e)
                if op.is_dma:
                    inst.then_inc(op.sem, 16)
                elif op.signal:
                    inst.then_inc(op.sem, 1)
            if name == "sp":
                for s, v in final_waits.items():
                    if seen.get(s, 0) < v:
                        e.wait_ge(s, v)

        block = stack.enter_context(nc.Block())

        @block.sync
        def _(e):
            run("sp", e)

        @block.scalar
        def _(e):
            run("act", e)

        @block.vector
        def _(e):
            run("dve", e)

        @block.gpsimd
        def _(e):
            run("pool", e)

        @block.tensor
        def _(e):
            run("pe", e)


class Arena:
    def __init__(self, tens, nbytes):
        self.t = tens; self.n = nbytes; self.off = 0; self.peak = 0

    def alloc(self, shape, dt):
        n = int(np.prod(shape[1:])) * ESZ[dt]
        ap = self.t[:, self.off // 4:(self.off + n) // 4]
        if dt != F32:
            ap = ap.bitcast(dt)
        if len(shape) == 3:
            ap = ap.rearrange("p (a b) -> p a b", a=shape[1])
        elif len(shape) == 4:
            ap = ap.rearrange("p (a b c) -> p a b c", a=shape[1], b=shape[2])
        if shape[0] < 128:
            ap = ap[0:shape[0]]
        self.off += (n + PAGE - 1) // PAGE * PAGE
        self.peak = max(self.peak, self.off)
        assert self.off <= self.n, ("arena overflow", self.off, self.n)
        return ap

    def alloc_top(self, shape, dt):
        n = int(np.prod(shape[1:])) * ESZ[dt]
        self.n -= (n + PAGE - 1) // PAGE * PAGE
        assert self.n >= self.peak, ("arena top overflow", self.n, self.peak)
        ap = self.t[:, self.n // 4:(self.n + n) // 4]
        if dt != F32:
            ap = ap.bitcast(dt)
        if len(shape) == 3:
            ap = ap.rearrange("p (a b) -> p a b", a=shape[1])
        return ap

    def mark(self):
        return self.off

    def release(self, m):
        self.off = m


class Rot:
    def __init__(self, ar, shape, dt, n):
        self.t = [ar.alloc(shape, dt) for _ in range(n)]; self.i = 0

    def next(self):
        r = self.t[self.i % len(self.t)]; self.i += 1
        return r


C_U, C_V, C_Z, C_XBC, C_DT, C_Q, C_G = 0, 512, 1024, 2048, 3584, 3600, 4112
K_ID, K_TRI, K_L, K_TRIL, K_ONES, K_OH, K_E, K_END = 0, 128, 256, 384, 512, 640, 644, 644 + 1024
V_NMIX, V_NMEM, V_NSSD, V_NFFN, V_CW, V_CB, V_DTB, V_ALOG, V_DSK, V_END = 0, 8, 16, 24, 32, 80, 92, 93, 94, 95

DEBUG = {}


def build(stage=99):
    nc = bass.Bass("TRN2", target_bir_lowering=False)

    def din(name, shape):
        return nc.dram_tensor(name, list(shape), F32, kind="ExternalInput").ap()

    def dout(name, shape):
        return nc.dram_tensor(name, list(shape), F32, kind="ExternalOutput").ap()

    x = din("x", [NT, 1024]); xs_in = din("xs", [NS, 1024]); mem = din("mem", [256, 1024])
    ck = din("ck", [NS, 256, 512]); cv = din("cv", [NS, 256, 512])
    sconv = din("sconv", [NS, 3, 1536]); sssm = din("sssm", [NS, 1024, 128])
    w_in = din("w_in", [1024, 7184]); w_mk = din("w_mk", [1024, 512]); w_mv = din("w_mv", [1024, 512])
    w_bg = din("w_bg", [512, 1024]); w_bs = din("w_bs", [1024, 1024]); w_bx = din("w_bx", [512, 1024])
    w_o = din("w_o", [1024, 1024]); w_up = din("w_up", [1024, 4096]); w_dn = din("w_dn", [4096, 1024])
    consts = din("consts", [128, K_END]); vecs = din("vecs", [128, V_END])
    gm_ws = din("gm_ws", [4, 128, 128])
    r_lnw = din("r_lnw", [1, 512]); r_lnb = din("r_lnb", [1, 512]); r_bs = din("r_bs", [1, 512])
    r_dtb = din("r_dtb", [1, 16]); r_alog = din("r_alog", [1, 16]); r_dsk = din("r_dsk", [1, 16])
    r_nfin = din("r_nfin", [1, 1024]); r_cw = din("r_cw", [1, 4 * 1536]); r_cb = din("r_cb", [1, 1536])
    r_ws00 = din("r_ws00", [1, 4]); r_bs0 = din("r_bs0", [1, 4])

    y = dout("y", [NT, 1024]); ys = dout("ys", [NS, 1024]); mk = dout("mk", [256, 512]); mv = dout("mv", [256, 512])
    convp = dout("convp", [3, 1536]); ssmp = dout("ssmp", [1024, 128])
    convs = dout("convs", [NS, 3, 1536]); ssms = dout("ssms", [NS, 1024, 128]); gv = dout("gv", [NS, 512])

    st = ExitStack()
    def finish():
        S.emit(st)
        st.close()
        DEBUG['peak'] = AR.peak
        return nc
    if True:
        NBY = 207 * 1024
        At = st.enter_context(nc.sbuf_tensor("arena", [128, NBY // 4], F32))
        PS = st.enter_context(nc.psum_tensor("ps", [128, 8, 512], F32))
        S = Sched(nc)
        AR = Arena(At, NBY)
        pbi = [0]

        pb_banks = [list(range(8))]

        def pb():
            bl = pb_banks[0]
            b = bl[pbi[0] % len(bl)]; pbi[0] += 1
            return PS[:, b, :]

        def isap(v):
            return not isinstance(v, (int, float)) and v is not None

        def act(out, in_, func, bias=None, scale=None, accum=None):
            rd = [in_] + [v for v in (bias, scale) if isap(v)]
            wr = [out] + ([accum] if accum is not None else [])
            kw = {}
            if bias is not None: kw["bias"] = bias
            if scale is not None: kw["scale"] = scale
            if accum is not None: kw["accum_out"] = accum
            S.add("act", lambda e: e.activation(out=out, in_=in_, func=func, **kw), reads=rd, writes=wr)

        def tt(eng, out, in0, in1, op):
            S.add(eng, lambda e: e.tensor_tensor(out=out, in0=in0, in1=in1, op=op), reads=[in0, in1], writes=[out])

        def ts(eng, out, in0, s1, s2, op0, op1=None):
            rd = [in0] + [v for v in (s1, s2) if isap(v)]
            if op1 is None:
                S.add(eng, lambda e: e.tensor_scalar(out=out, in0=in0, scalar1=s1, scalar2=None, op0=op0), reads=rd, writes=[out])
            else:
                S.add(eng, lambda e: e.tensor_scalar(out=out, in0=in0, scalar1=s1, scalar2=s2, op0=op0, op1=op1), reads=rd, writes=[out])

        def stt(eng, out, in0, sc, in1, op0, op1):
            rd = [in0, in1] + ([sc] if isap(sc) else [])
            S.add(eng, lambda e: e.scalar_tensor_tensor(out=out, in0=in0, scalar=sc, in1=in1, op0=op0, op1=op1), reads=rd, writes=[out])

        def cp(eng, out, in_):
            if eng == "act":
                act(out, in_, AF.Copy)
            else:
                S.add(eng, lambda e: e.tensor_copy(out=out, in_=in_), reads=[in_], writes=[out])

        def red(eng, out, in_, op):
            S.add(eng, lambda e: e.tensor_reduce(out=out, in_=in_, axis=AX.X, op=op), reads=[in_], writes=[out])

        def memset(eng, out, val):
            S.add(eng, lambda e: e.memset(out, val), writes=[out])

        def mm(out, pairs):
            def fn(e):
                n = len(pairs); ins = None
                for i, (l, r) in enumerate(pairs):
                    ins = e.matmul(out, l, r, start=(i == 0), stop=(i == n - 1))
                return ins
            S.add("pe", fn, reads=[a for p in pairs for a in p], writes=[out])

        def tr(out, in_, ident):
            S.add("pe", lambda e: e.transpose(out, in_, ident), reads=[in_, ident], writes=[out])

        def dma(q, out, in_, slow=False):
            if slow:
                S.add(q, lambda e: e.dma_start(out=out, in_=in_, allow_slow_non_contiguous=True), reads=[in_], writes=[out], dma=True)
            else:
                S.add(q, lambda e: e.dma_start(out=out, in_=in_), reads=[in_], writes=[out], dma=True)

        def load_w(dst, src, K, N):
            srcv = src.rearrange("(k p) n -> p k n", p=128)
            for c0 in range(0, N, 2048):
                c1 = min(N, c0 + 2048)
                dma("pool", dst[:, :, c0:c1], srcv[:, :, c0:c1])

        def bc3(ap, n):
            return ap.unsqueeze(2).to_broadcast([ap.shape[0], ap.shape[1], n])

        def bc_mid(ap, n):
            return ap.unsqueeze(1).to_broadcast([ap.shape[0], n, ap.shape[1]])

        XN = AR.alloc([128, 8, NTOK], BF16)
        BO = AR.alloc([128, 8, NTOK], BF16)
        CONST = AR.alloc([128, K_E], F32)
        VEC = AR.alloc([128, V_END], F32)
        IDB = AR.alloc([128, 128], BF16)
        NEGH = AR.alloc([128, 8], F32)
        SM = Rot(AR, [128, 8], F32, 12)
        ID = CONST[:, K_ID:K_ID + 128]; TRI = CONST[:, K_TRI:K_TRI + 128]; LST = CONST[:, K_L:K_L + 128]
        TRIL = CONST[:, K_TRIL:K_TRIL + 128]; ONES = CONST[:, K_ONES:K_ONES + 128]
        dma("sp", CONST, consts[:, 0:K_E]); dma("sp", VEC, vecs)
        cp("dve", IDB, ID)
        CB16 = AR.alloc([128, 3, 128], BF16)
        TRIb = CB16[:, 0, :]; ONESb = CB16[:, 1, :]; LSTb = CB16[:, 2, :]
        cp("dve", TRIb, TRI); cp("dve", ONESb, ONES); cp("dve", LSTb, LST)
        memset("pool", NEGH, -0.5)

        tiles4 = [(i * 512, 512) for i in range(4)] + [(NT, NS)]
        tiles17 = [(i * 128, 128) for i in range(16)] + [(NT, NS)]

        def rstd_of(ss, n, count, eps_scale=1.0):
            s = SM.next()
            ts("dve", s[0:n, 0:1], ss, 1.0 / count, EPS, ALU.mult, ALU.add)
            tt("pool", s[0:n, 1:2], s[0:n, 0:1], NEGH[0:n, 0:1], ALU.pow)
            return s[0:n, 1:2]

        def norm_to_T(src, n, t0, nwcol, dstT, XB, JK):
            s = SM.next()
            jk = JK.next()
            act(jk[0:n], src, AF.Square, accum=s[0:n, 0:1])
            r = rstd_of(s[0:n, 0:1], n, 1024.0)
            xb = XB.next()
            ts("dve", xb[0:n], src, r, None, ALU.mult)
            bank = pb().bitcast(BF16).rearrange("p (a b) -> p a b", a=8)
            for kc in range(8):
                tr(bank[:, kc, 0:n], xb[0:n, kc * 128:(kc + 1) * 128], IDB[0:n, 0:n])
            tt("dve", dstT[:, :, t0:t0 + n], bank[:, :, 0:n], bc3(VEC[:, nwcol:nwcol + 8], n), ALU.mult)

        m0 = AR.mark()
        XIN = Rot(AR, [128, 1024], F32, 3)
        XBr = Rot(AR, [128, 1024], BF16, 2)
        JKr = Rot(AR, [128, 1024], BF16, 1)
        mP = AR.mark()
        for (t0, n) in tiles17:
            xi = XIN.next()
            dma("sp", xi[0:n], x[t0:t0 + n, :] if t0 < NT else xs_in[:, :])
            norm_to_T(xi[0:n], n, t0, V_NMIX, XN, XBr, JKr)

        if stage <= 0:
            return finish()
        AR.release(m0)
        Wxbc = AR.alloc([128, 8, 1536], BF16); Wz = AR.alloc([128, 8, 1024], BF16); Wdt = AR.alloc([128, 8, 16], BF16)
        load_w(Wxbc, w_in[:, C_XBC:C_XBC + 1536], 1024, 1536)
        load_w(Wdt, w_in[:, C_DT:C_DT + 16], 1024, 16)
        load_w(Wz, w_in[:, C_Z:C_Z + 1024], 1024, 1024)
        if stage <= 0.5:
            return finish()
        RB = AR.alloc([128, 64], F32)
        DTB = RB[:, 0:16]; ALOG = RB[:, 16:32]; DSK = RB[:, 32:48]; ANEG = RB[:, 48:64]
        dma("sp", DTB, r_dtb.partition_broadcast(128)); dma("sp", ALOG, r_alog.partition_broadcast(128))
        dma("sp", DSK, r_dsk.partition_broadcast(128))
        act(ANEG, ALOG, AF.Exp)
        ts("dve", ANEG, ANEG, -1.0, None, ALU.mult)
        if stage <= 0.7:
            return finish()
        PRE_S = AR.alloc([128, 8, 256], F32)
        DT = PRE_S[:, 0, :]; AA = PRE_S[:, 1, :]; EAC = PRE_S[:, 2, :]; CD = PRE_S[:, 3, :]
        DEND = PRE_S[:, 4, :]; DTD = PRE_S[:, 5, :]; T6 = PRE_S[:, 6, :]; T7 = PRE_S[:, 7, :]
        bk = pb()
        for c in range(NCH):
            mm(bk[:, c * 16:(c + 1) * 16], [(XN[:, kc, c * 128:(c + 1) * 128], Wdt[:, kc, :]) for kc in range(8)])
        v3 = lambda ap: ap.rearrange("p (c h) -> p c h", c=16)
        tt("dve", v3(T6), v3(bk[:, 0:256]), bc_mid(DTB, 16), ALU.add)
        act(T7, T6, AF.Exp)
        ts("dve", T7, T7, 1.0, None, ALU.add)
        act(DT, T7, AF.Ln)
        if stage <= 0.8:
            return finish()
        tt("dve", v3(AA), v3(DT), bc_mid(ANEG, 16), ALU.mult)
        AHL = AR.alloc([128, 2, 256], BF16)
        AAh = AHL[:, 0, :]; AAl = AHL[:, 1, :]
        cp("dve", AAh, AA)
        tt("dve", T6, AA, AAh, ALU.subtract)
        cp("dve", AAl, T6)
        bk2 = pb()
        mm(bk2[:, 0:256], [(TRIb, AAh), (TRIb, AAl)])
        mm(bk2[:, 256:512], [(ONESb, AAh), (ONESb, AAl)])
        act(EAC, bk2[:, 0:256], AF.Exp)
        act(CD, bk2[:, 256:512], AF.Exp)
        act(T6, bk2[:, 0:256], AF.Copy)
        act(T7, bk2[:, 256:512], AF.Copy)
        tt("dve", T7, T7, T6, ALU.subtract)
        act(DEND, T7, AF.Exp)
        tt("dve", DTD, DT, DEND, ALU.mult)

        if stage <= 1:
            return finish()
        mS = AR.mark()
        HALO = AR.alloc([128, 12, 4], F32)
        memset("pool", HALO, 0.0)
        HW = 256
        PREc = Rot(AR, [128, HW + 4], F32, 2)
        XC = AR.alloc([128, 12, HW], F32)
        BCT2 = [AR.alloc([128, 4, HW], BF16), AR.alloc([128, 4, HW], BF16)]
        CTr = Rot(AR, [128, HW], F32, 1)
        XSr = Rot(AR, [128, 1024], F32, 2)
        XDTr = Rot(AR, [128, 16, 64], BF16, 2)
        XDTDr = Rot(AR, [128, 16, 64], BF16, 2)
        BTKr = Rot(AR, [128, 2, 128], BF16, 2)
        CBMr = Rot(AR, [128, 2, 128], F32, 2)
        RHr = Rot(AR, [128, 2, 4, 128], BF16, 2)
        ESr = Rot(AR, [128, 4, 128], F32, 2)
        MTr = Rot(AR, [128, 16, 128], BF16, 1)
        Y1r = Rot(AR, [128, 1024], F32, 1)
        XSDr = Rot(AR, [128, 1024], F32, 1)
        Zr = Rot(AR, [128, 1024], F32, 2)
        THr = Rot(AR, [128, 1024], F32, 1)
        YNr = Rot(AR, [128, 1024], BF16, 1)
        HT = AR.alloc([128, 1024], F32)
        HTb = AR.alloc([128, 1024], BF16)
        memset("pool", HT, 0.0)
        memset("pool", HTb, 0.0)
        JK2 = Rot(AR, [128, 512], BF16, 2)
        def conv_gen(T, bankf):
            t0 = T * HW
            for cc in range(12):
                if cc:
                    yield
                bank = bankf()
                mm(bank[:, 0:HW], [(Wxbc[:, kc, cc * 128:(cc + 1) * 128], XN[:, kc, t0:t0 + HW]) for kc in range(8)])
                pre = PREc.next()
                act(pre[:, 0:3], HALO[:, cc, 0:3], AF.Copy)
                act(pre[:, 3:HW + 3], bank[:, 0:HW], AF.Copy)
                cw = VEC[:, V_CW + cc * 4:V_CW + cc * 4 + 4]
                act(XC[:, cc, :], pre[:, 0:HW], AF.Identity, bias=VEC[:, V_CB + cc:V_CB + cc + 1], scale=cw[:, 0:1])
                for k in range(1, 4):
                    if True:
                        stt("dve", XC[:, cc, :], pre[:, k:k + HW], cw[:, k:k + 1], XC[:, cc, :], ALU.mult, ALU.add)
                    else:
                        ct = CTr.next()
                        ts("pool", ct, pre[:, k:k + HW], cw[:, k:k + 1], None, ALU.mult)
                        tt("pool", XC[:, cc, :], XC[:, cc, :], ct, ALU.add)
                act(HALO[:, cc, 0:3], pre[:, HW:HW + 3], AF.Copy)
                if t0 + HW == NT:
                    dma("sp", convp[:, cc * 128:(cc + 1) * 128].rearrange("k c -> c k"), pre[:, HW:HW + 3], slow=True)
            yield
            for cc in range(12):
                act(XC[:, cc, :], XC[:, cc, :], AF.Silu)
            for j in range(4):
                cp("act", BCT2[T % 2][:, j, :], XC[:, 8 + j, :])

        CH = {}
        pfi = [0]; pti = [0]

        def pbf():
            b = 6 + pfi[0] % 2; pfi[0] += 1
            return PS[:, b, :]

        def pbt():
            b = 4 + pti[0] % 2; pti[0] += 1
            return PS[:, b, :]

        def front(c):
            q = c % 2
            t0 = c * 128
            cs = slice(c * 16, (c + 1) * 16)
            qs = slice(q * 128, (q + 1) * 128)
            d = {}
            CH[c] = d
            BCT = BCT2[(c // 2) % 2]
            d["BCT"] = BCT
            XS = XSr.next()
            b0 = pbf(); b1 = pbf()
            for f in range(8):
                bank = b0 if f < 4 else b1
                tr(bank[:, (f % 4) * 128:(f % 4 + 1) * 128], XC[:, f, qs], ID)
            yield
            act(XS[:, 0:512], b0, AF.Copy)
            act(XS[:, 512:1024], b1, AF.Copy)
            b2 = pbf()
            for g in range(2):
                tr(b2[:, g * 128:(g + 1) * 128], XC[:, 8 + g, qs], ID)
            yield
            BTK = BTKr.next()
            cp("dve", BTK, b2[:, 0:256].rearrange("p (g n) -> p g n", g=2))
            yield
            XS3 = XS.rearrange("p (h d) -> p h d", h=16)
            XDT = XDTr.next(); XDTD = XDTDr.next()
            tt("dve", XDT, XS3, bc3(DT[:, cs], 64), ALU.mult)
            tt("pool", XDTD, XS3, bc3(DTD[:, cs], 64), ALU.mult)
            d.update(XS=XS, XS3=XS3, BTK=BTK, XDTD=XDTD, qs=qs, cs=cs, t0=t0)
            yield
            b3 = pbf()
            for g in range(2):
                mm(b3[:, g * 128:(g + 1) * 128], [(BCT[:, g, qs], BCT[:, 2 + g, qs])])
            CBM = CBMr.next()
            tt("dve", CBM, b3[:, 0:256].rearrange("p (g n) -> p g n", g=2), bc_mid(TRI, 2), ALU.mult)
            yield
            Z = Zr.next(); TH = THr.next()
            for hf in range(2):
                bz = pbf()
                mm(bz, [(XN[:, kc, t0:t0 + 128], Wz[:, kc, hf * 512:(hf + 1) * 512]) for kc in range(8)])
                act(Z[:, hf * 512:(hf + 1) * 512], bz, AF.Copy)
                act(TH[:, hf * 512:(hf + 1) * 512], bz, AF.Tanh, scale=0.5)
                yield
            stt("dve", Z, TH, 1.0, Z, ALU.add, ALU.mult)
            d["Z"] = Z
            yield
            MT = MTr.next()
            yd = [PS[:, 2 * (c % 2), :], PS[:, 2 * (c % 2) + 1, :]]
            d["yd"] = yd
            def mk_rh(hq):
                RH = RHr.next()
                hsl = slice(c * 16 + hq * 4, c * 16 + hq * 4 + 4)
                tt("dve", RH[:, 0], bc_mid(TRIb, 4), bc3(AAh[:, hsl], 128), ALU.mult)
                tt("dve", RH[:, 1], bc_mid(TRIb, 4), bc3(AAl[:, hsl], 128), ALU.mult)
                return RH

            def seg_mm(RH):
                bs = pbf()
                mm(bs, [(LSTb, RH[:, 0].rearrange("p a b -> p (a b)")), (LSTb, RH[:, 1].rearrange("p a b -> p (a b)"))])
                return bs

            segs = {}
            for hq in range(2):
                segs[hq] = seg_mm(mk_rh(hq))
                yield
            for hq in range(4):
                g = hq // 2
                ES = ESr.next()
                act(ES.rearrange("p a b -> p (a b)"), segs.pop(hq), AF.Exp)
                tt("dve", MT[:, hq * 4:hq * 4 + 4, :], ES, bc_mid(CBM[:, g, :], 4), ALU.mult)
                yield
                if hq + 2 < 4:
                    segs[hq + 2] = seg_mm(mk_rh(hq + 2))
                for hh in range(4):
                    h = hq * 4 + hh
                    mm(yd[h // 8][:, (h % 8) * 64:(h % 8 + 1) * 64], [(MT[:, h, :], XDT[:, h, :])])
                yield

        def tail(c):
            d = CH.pop(c)
            XS3 = d["XS3"]; BTK = d["BTK"]; XDTD = d["XDTD"]; qs = d["qs"]; cs = d["cs"]; t0 = d["t0"]; yd = d["yd"]; Z = d["Z"]
            BCT = d["BCT"]
            XSD = XSDr.next()
            tt("pool", XSD.rearrange("p (h d) -> p h d", h=16), XS3, bc3(DSK, 64), ALU.mult)
            Y1 = Y1r.next()
            if c > 0:
                yo = [pbt(), pbt()]
                for g in range(2):
                    mm(yo[g], [(BCT[:, 2 + g, qs], HTb[:, g * 512:(g + 1) * 512])])
                for g in range(2):
                    tt("dve", Y1[:, g * 512:(g + 1) * 512].rearrange("p (h d) -> p h d", h=8),
                       yo[g].rearrange("p (h d) -> p h d", h=8), bc3(EAC[:, c * 16 + g * 8:c * 16 + g * 8 + 8], 64), ALU.mult)
            yield
            stb = [pbt(), pbt()]
            for g in range(2):
                mm(stb[g], [(BTK[:, g, :], XDTD[:, g * 8:(g + 1) * 8, :].rearrange("p h d -> p (h d)"))])
            tt("dve", HT.rearrange("p (h d) -> p h d", h=16), HT.rearrange("p (h d) -> p h d", h=16), bc3(CD[:, cs], 64), ALU.mult)
            for g in range(2):
                tt("dve", HT[:, g * 512:(g + 1) * 512], stb[g], HT[:, g * 512:(g + 1) * 512], ALU.add)
            cp("act", HTb, HT)
            yield
            if c > 0:
                tt("dve", Y1, Y1, XSD, ALU.add)
                for g in range(2):
                    tt("dve", Y1[:, g * 512:(g + 1) * 512], yd[g], Y1[:, g * 512:(g + 1) * 512], ALU.add)
            else:
                for g in range(2):
                    tt("dve", Y1[:, g * 512:(g + 1) * 512], yd[g], XSD[:, g * 512:(g + 1) * 512], ALU.add)
            yield
            tt("dve", Y1, Y1, Z, ALU.mult)
            yield
            finish_ssd_tokens(Y1, 128, t0)
            yield

        def finish_ssd_tokens(Y2, n, t0):
            s = SM.next()
            for g in range(2):
                jk = JK2.next()
                act(jk[0:n], Y2[0:n, g * 512:(g + 1) * 512], AF.Square, accum=s[0:n, g:g + 1])
            s2 = SM.next()
            ts("dve", s2[0:n, 0:2], s[0:n, 0:2], 1.0 / (4 * 512), EPS, ALU.mult, ALU.add)
            tt("pool", s2[0:n, 2:4], s2[0:n, 0:2], NEGH[0:n, 0:2], ALU.pow)
            YN = YNr.next()
            ts("dve", s2[0:n, 4:6], s2[0:n, 2:4], 0.5, None, ALU.mult)
            for g in range(2):
                act(YN[0:n, g * 512:(g + 1) * 512], Y2[0:n, g * 512:(g + 1) * 512], AF.Copy, scale=s2[0:n, 4 + g:5 + g])
            bank = pbt().bitcast(BF16).rearrange("p (a b) -> p a b", a=8)
            for fc in range(8):
                tr(bank[:, fc, 0:n], YN[0:n, fc * 128:(fc + 1) * 128], IDB[0:n, 0:n])
            tt("dve", BO[:, :, t0:t0 + n], bank[:, :, 0:n], bc3(VEC[:, V_NSSD:V_NSSD + 8], n), ALU.mult)

        def drain(g):
            for _ in g:
                pass

        def interleave(g1, g2):
            d1 = d2 = False
            while not (d1 and d2):
                if not d1:
                    try:
                        next(g1)
                    except StopIteration:
                        d1 = True
                if not d2:
                    try:
                        next(g2)
                    except StopIteration:
                        d2 = True

        def interleave_all(gens):
            gens = list(gens)
            while gens:
                for g in list(gens):
                    try:
                        next(g)
                    except StopIteration:
                        gens.remove(g)

        drain(conv_gen(0, pb))
        if stage <= 2:
            return finish()
        drain(front(0))
        pb_banks[0] = [4, 5, 6, 7]
        for c in range(NCH):
            if c % 2 == 1 and c + 1 < NCH:
                drain(conv_gen((c + 1) // 2, pb))
            gens = [tail(c)]
            if c + 1 < NCH:
                gens.append(front(c + 1))
            interleave_all(gens)
        pb_banks[0] = list(range(8))
        for f in range(8):
            bank = pb()
            tr(bank[:, 0:128], HT[:, f * 128:(f + 1) * 128], ID)
            o = ESr.next()
            cp("act", o[:, 0, :], bank[:, 0:128])
            dma("sp", ssmp[f * 128:(f + 1) * 128, :], o[:, 0, :])

        if stage <= 3 and stage != 3.5:
            return finish()
        AR.release(mS)
        SB = slice(NT, NTOK)
        EM = AR.alloc([16, 1024], F32)
        dma("sp", EM, consts[0:16, K_E:K_E + 1024])
        SC48 = AR.alloc([48, 1536], F32)
        SCT = AR.alloc([128, 12, 48], F32)
        XBS = AR.alloc([16, 1536], F32)
        XBT = AR.alloc([128, 12, 16], F32)
        XCT = AR.alloc_top([128, 12, 16], F32)
        SMALL = AR.alloc([128, 16, 16], F32)
        DTE = AR.alloc([128, 8, 16], F32); DAE = AR.alloc_top([128, 8, 16], F32); DSE = AR.alloc_top([128, 8], F32)
        DTX = AR.alloc_top([128, 8, 16], F32); YST = AR.alloc_top([128, 8, 16], F32); ZST = AR.alloc_top([128, 8, 16], F32)
        ZSI = AR.alloc_top([128, 8, 16], F32)
        dma("sp", SC48, sconv.rearrange("b k c -> (b k) c"))
        dma("sp", convs[:, 0:2, :], sconv[:, 1:3, :])
        for cc in range(12):
            bank = pb()
            tr(bank[:, 0:48], SC48[:, cc * 128:(cc + 1) * 128], ID[0:48, 0:48])
            cp("act", SCT[:, cc, :], bank[:, 0:48])
        for j in range(3):
            bank = pb()
            mm(bank[0:16], [(XN[:, kc, SB], Wxbc[:, kc, j * 512:(j + 1) * 512]) for kc in range(8)])
            act(XBS[:, j * 512:(j + 1) * 512], bank[0:16], AF.Copy)
        dma("sp", convs[:, 2, :], XBS)
        bank = pb()
        for cc in range(12):
            mm(bank[:, cc * 16:(cc + 1) * 16], [(Wxbc[:, kc, cc * 128:(cc + 1) * 128], XN[:, kc, SB]) for kc in range(8)])
        act(XBT.rearrange("p a b -> p (a b)"), bank[:, 0:192], AF.Copy)
        for cc in range(12):
            cw = VEC[:, V_CW + cc * 4:V_CW + cc * 4 + 4]
            sv = SCT[:, cc, :].rearrange("p (b k) -> p b k", k=3)
            ts("dve", XCT[:, cc, :], XBT[:, cc, :], cw[:, 3:4], VEC[:, V_CB + cc:V_CB + cc + 1], ALU.mult, ALU.add)
            for k in range(3):
                stt("dve", XCT[:, cc, :], sv[:, :, k], cw[:, k:k + 1], XCT[:, cc, :], ALU.mult, ALU.add)
        act(XCT.rearrange("p a b -> p (a b)"), XCT.rearrange("p a b -> p (a b)"), AF.Silu)
        bank = pb()
        mm(bank[0:16, 0:16], [(Wdt[:, kc, :], XN[:, kc, SB]) for kc in range(8)])
        S0 = SMALL[0:16, 0, :]; S1 = SMALL[0:16, 1, :]; S2 = SMALL[0:16, 2, :]; S3 = SMALL[0:16, 3, :]; S4 = SMALL[0:16, 4, :]
        ts("dve", S0, bank[0:16, 0:16], VEC[0:16, V_DTB:V_DTB + 1], None, ALU.add)
        act(S1, S0, AF.Exp)
        ts("dve", S1, S1, 1.0, None, ALU.add)
        act(S2, S1, AF.Ln)
        act(S3[:, 0:1], VEC[0:16, V_ALOG:V_ALOG + 1], AF.Exp)
        ts("dve", S3[:, 1:2], S3[:, 0:1], -1.0, None, ALU.mult)
        ts("dve", S4, S2, S3[:, 1:2], None, ALU.mult)
        act(S4, S4, AF.Exp)
        bank = pb()
        for j in range(8):
            mm(bank[:, j * 16:(j + 1) * 16], [(EM[:, j * 128:(j + 1) * 128], S2)])
            mm(bank[:, 128 + j * 16:128 + (j + 1) * 16], [(EM[:, j * 128:(j + 1) * 128], S4)])
            mm(bank[:, 256 + j:256 + j + 1], [(EM[:, j * 128:(j + 1) * 128], VEC[0:16, V_DSK:V_DSK + 1])])
        act(DTE.rearrange("p a b -> p (a b)"), bank[:, 0:128], AF.Copy)
        act(DAE.rearrange("p a b -> p (a b)"), bank[:, 128:256], AF.Copy)
        act(DSE, bank[:, 256:264], AF.Copy)
        tt("dve", DTX, XCT[:, 0:8, :], DTE, ALU.mult)
        bank = pb()
        for j in range(8):
            mm(bank[:, j * 16:(j + 1) * 16], [(Wz[:, kc, j * 128:(j + 1) * 128], XN[:, kc, SB]) for kc in range(8)])
        act(ZSI.rearrange("p a b -> p (a b)"), bank[:, 0:128], AF.Silu)
        if stage <= 3.5:
            return finish()
        SAMPLE = True
        tilesF = tiles4 if SAMPLE else tiles4[:4]
        tilesT = tiles17 if SAMPLE else tiles17[:16]
        AR.release(m0)
        MACC = AR.alloc([128, 8, NTOK], BF16)
        mW = AR.mark()
        SGr = Rot(AR, [128, 512], F32, 2)
        TMr = Rot(AR, [128, 512], F32, 2)
        JK2 = Rot(AR, [128, 512], BF16, 2)
        mPh = AR.mark()
        Wu = AR.alloc([128, 8, 512], BF16); Wv = AR.alloc([128, 8, 512], BF16)
        mPh2 = AR.mark()

        def branch_out_g(Wbr, nk, SRC, Wg, mode, tiles, bankf=None):
            bankf = bankf or pb
            for (t0, n) in tiles:
                for fc in range(8):
                    yield
                    bb = bankf(); bg = bankf()
                    mm(bb[:, 0:n], [(Wbr[:, kc, fc * 128:(fc + 1) * 128], SRC[:, kc, t0:t0 + n]) for kc in range(nk)])
                    mm(bg[:, 0:n], [(Wg[:, kc, fc * 128:(fc + 1) * 128], XN[:, kc, t0:t0 + n]) for kc in range(8)])
                    sg = SGr.next()
                    act(sg[:, 0:n], bg[:, 0:n], AF.Sigmoid)
                    dst = MACC[:, fc, t0:t0 + n]
                    if mode == "first":
                        tt("dve", dst, bb[:, 0:n], sg[:, 0:n], ALU.mult)
                    else:
                        tm = TMr.next()
                        tt("dve", tm[:, 0:n], bb[:, 0:n], sg[:, 0:n], ALU.mult)
                        if mode == "mid":
                            tt("dve", dst, dst, tm[:, 0:n], ALU.add)
                        else:
                            tt("dve", BO[:, fc, t0:t0 + n], dst, tm[:, 0:n], ALU.add)

        def branch_out(Wbr, nk, SRC, Wg, mode):
            drain(branch_out_g(Wbr, nk, SRC, Wg, mode, tilesF))

        Wbs = AR.alloc([128, 8, 1024], BF16); Wg1 = AR.alloc([128, 8, 1024], BF16)
        load_w(Wbs, w_bs, 1024, 1024)
        load_w(Wg1, w_in[:, C_G + 1024:C_G + 2048], 1024, 1024)
        load_w(Wu, w_in[:, C_U:C_U + 512], 1024, 512); load_w(Wv, w_in[:, C_V:C_V + 512], 1024, 512)
        mSL = AR.mark()
        Hr = Rot(AR, [128, 8, 128], F32, 3); T2r = Rot(AR, [128, 8, 128], F32, 2); DGr = Rot(AR, [128, 4, 128], F32, 2)
        SML2 = AR.alloc([128, 4, 16], F32)

        def sample_state_loop():
            Hs = {}

            def load_state(b):
                Hs[b] = Hr.next()
                dma("sp", Hs[b], sssm[b].rearrange("(j p) n -> p j n", p=128))

            load_state(0)
            for b in range(16):
                if b + 1 < 16:
                    load_state(b + 1)
                DG = DGr.next()
                tt("pool", DG, bc_mid(ID, 4), bc3(XCT[:, 8:12, b], 128), ALU.mult)
                bcb = pb()
                mm(bcb, [(ONES, DG.rearrange("p a b -> p (a b)"))])
                H = Hs.pop(b); T2 = T2r.next()
                yield
                for j in range(8):
                    act(H[:, j, :], H[:, j, :], AF.Copy, scale=DAE[:, j, b:b + 1])
                Bv = bcb[:, 0:256].rearrange("p (g n) -> p g n", g=2).unsqueeze(2).to_broadcast([128, 2, 4, 128])
                Cv = bcb[:, 256:512].rearrange("p (g n) -> p g n", g=2).unsqueeze(2).to_broadcast([128, 2, 4, 128])
                T24 = T2.rearrange("p (g j) n -> p g j n", g=2)
                dxb = DTX[:, :, b].rearrange("p (g j) -> p g j", g=2).unsqueeze(3).to_broadcast([128, 2, 4, 128])
                tt("dve", T24, Bv, dxb, ALU.mult)
                yield
                tt("dve", H, H, T2, ALU.add)
                dma("sp", ssms[b].rearrange("(j p) n -> p j n", p=128), H)
                tt("dve", T24, Cv, H.rearrange("p (g j) n -> p g j n", g=2), ALU.mult)
                red("dve", YST[:, :, b], T2, ALU.add)
                yield

        if SAMPLE:
            def twice(g):
                while True:
                    try:
                        next(g); next(g)
                    except StopIteration:
                        return
                    yield
            interleave(twice(sample_state_loop()), branch_out_g(Wbs, 8, BO, Wg1, "first", tiles4[:4]))
            tt("dve", ZST, XCT[:, 0:8, :], bc3(DSE, 16), ALU.mult)
            tt("dve", YST, YST, ZST, ALU.add)
            tt("dve", YST, YST, ZSI, ALU.mult)
            tt("dve", ZST, YST, YST, ALU.mult)
            bank = pb()
            for g in range(2):
                mm(bank[:, g * 16:(g + 1) * 16], [(ONES, ZST[:, 4 * g + j, :]) for j in range(4)])
            RS2 = SML2[:, 0:2, :]; NH32 = SML2[:, 2:4, :]
            ts("dve", RS2.rearrange("p a b -> p (a b)"), bank[:, 0:32], 1.0 / 512, EPS, ALU.mult, ALU.add)
            memset("pool", NH32, -0.5)
            tt("pool", RS2, RS2, NH32, ALU.pow)
            for g in range(2):
                tt("dve", YST[:, 4 * g:4 * g + 4, :], YST[:, 4 * g:4 * g + 4, :], bc_mid(RS2[:, g, :], 4), ALU.mult)
            tt("dve", BO[:, :, SB], YST, bc3(VEC[:, V_NSSD:V_NSSD + 8], 16), ALU.mult)
            drain(branch_out_g(Wbs, 8, BO, Wg1, "first", tiles4[4:]))
        else:
            branch_out(Wbs, 8, BO, Wg1, "first")
        AR.release(mSL)
        if stage <= 4:
            return finish()
        AR.release(mPh2)
        Wg0 = AR.alloc([128, 8, 1024], BF16); Wbg = AR.alloc([128, 4, 1024], BF16)
        load_w(Wg0, w_in[:, C_G:C_G + 1024], 1024, 1024); load_w(Wbg, w_bg, 512, 1024)
        GM = AR.alloc([128, 4, NTOK], BF16)
        mP3b = AR.mark()
        WST = AR.alloc([128, 4, 128], BF16)
        LNW = AR.alloc([128, 512], F32); LNB = AR.alloc([128, 512], F32); BSB = AR.alloc([128, 512], F32)
        dma("sp", LNW, r_lnw.partition_broadcast(128)); dma("sp", LNB, r_lnb.partition_broadcast(128))
        dma("sp", BSB, r_bs.partition_broadcast(128))
        for g in range(4):
            wt_ = TMr.next()
            dma("sp", wt_[:, 0:128], gm_ws[g])
            tt("pool", wt_[:, 0:128], wt_[:, 0:128], TRIL, ALU.mult)
            bank = pb()
            tr(bank[:, 0:128], wt_[:, 0:128], ID)
            cp("act", WST[:, g, :], bank[:, 0:128])
        UT = AR.alloc([128, 4, 512], F32)
        Vr = Rot(AR, [128, 512], F32, 2); VNr = Rot(AR, [128, 512], F32, 2); VBr = Rot(AR, [128, 512], BF16, 4)
        def p3_uproj(T):
            t0 = T * 512
            for g in range(4):
                bank = pb()
                mm(bank, [(Wu[:, kc, g * 128:(g + 1) * 128], XN[:, kc, t0:t0 + 512]) for kc in range(8)])
                act(UT[:, g, :], bank, AF.Gelu)

        P3V = {}

        def p3_A(c, bk):
            c0 = c * 128
            bank = PS[:, bk, :]
            mm(bank, [(XN[:, kc, c0:c0 + 128], Wv[:, kc, :]) for kc in range(8)])
            yield
            V = Vr.next(); s = SM.next()
            act(V, bank, AF.Gelu, accum=s[:, 0:1])
            jk = JK2.next()
            act(jk, V, AF.Square, accum=s[:, 1:2])
            yield
            s2 = SM.next()
            ts("dve", s2[:, 0:1], s[:, 0:1], 1.0 / 512, None, ALU.mult)
            tt("dve", s2[:, 1:2], s2[:, 0:1], s2[:, 0:1], ALU.mult)
            yield
            stt("dve", s2[:, 2:3], s[:, 1:2], 1.0 / 512, s2[:, 1:2], ALU.mult, ALU.subtract)
            ts("dve", s2[:, 3:4], s2[:, 2:3], EPS, None, ALU.add)
            yield
            tt("pool", s2[:, 4:5], s2[:, 3:4], NEGH[:, 0:1], ALU.pow)
            yield
            VN = VNr.next()
            ts("dve", VN, V, s2[:, 0:1], s2[:, 4:5], ALU.subtract, ALU.mult)
            yield
            tt("dve", VN, VN, LNW, ALU.mult)
            yield
            VB = VBr.next()
            tt("dve", VB, VN, LNB, ALU.add)
            P3V[c] = VB

        def p3_B(c, bk):
            c0 = c * 128; q = c % 4
            VB = P3V.pop(c)
            bank2 = PS[:, bk, :]
            for g in range(4):
                mm(bank2[:, g * 128:(g + 1) * 128], [(VB[:, g * 128:(g + 1) * 128], WST[:, g, :])])
            yield
            tm = TMr.next()
            tt("dve", tm, bank2, BSB, ALU.add)
            yield
            tt("dve", GM[:, :, c0:c0 + 128], tm.rearrange("p (g t) -> p g t", g=4), UT[:, :, q * 128:(q + 1) * 128], ALU.mult)

        pb_banks[0] = [4, 5, 6, 7]
        p3_uproj(0)
        interleave_all([p3_A(0, 0), p3_A(1, 1)])
        for k in range(8):
            c = 2 * k
            gens = [p3_B(c, 2), p3_B(c + 1, 3)]
            if c + 2 < 16:
                gens += [p3_A(c + 2, 0), p3_A(c + 3, 1)]
            interleave_all(gens)
            if (c + 2) % 4 == 0 and c + 2 < 16:
                p3_uproj((c + 2) // 4)
        pb_banks[0] = list(range(8))
        if SAMPLE:
            WS0 = AR.alloc([16, 8], F32)
            dma("sp", WS0[:, 0:4], r_ws00.partition_broadcast(16)); dma("sp", WS0[:, 4:8], r_bs0.partition_broadcast(16))
            bu = pb()
            mm(bu[0:16], [(XN[:, kc, SB], Wu[:, kc, :]) for kc in range(8)])
            US = Vr.next()
            act(US[0:16], bu[0:16], AF.Gelu)
            bv = pb()
            mm(bv[0:16], [(XN[:, kc, SB], Wv[:, kc, :]) for kc in range(8)])
            V = Vr.next(); s = SM.next()
            act(V[0:16], bv[0:16], AF.Gelu, accum=s[0:16, 0:1])
            jk = JK2.next()
            act(jk[0:16], V[0:16], AF.Square, accum=s[0:16, 1:2])
            s2 = SM.next()
            ts("dve", s2[0:16, 0:1], s[0:16, 0:1], 1.0 / 512, None, ALU.mult)
            tt("dve", s2[0:16, 1:2], s2[0:16, 0:1], s2[0:16, 0:1], ALU.mult)
            stt("dve", s2[0:16, 2:3], s[0:16, 1:2], 1.0 / 512, s2[0:16, 1:2], ALU.mult, ALU.subtract)
            ts("dve", s2[0:16, 3:4], s2[0:16, 2:3], EPS, None, ALU.add)
            tt("pool", s2[0:16, 4:5], s2[0:16, 3:4], NEGH[0:16, 0:1], ALU.pow)
            VN = VNr.next()
            ts("dve", VN[0:16], V[0:16], s2[0:16, 0:1], s2[0:16, 4:5], ALU.subtract, ALU.mult)
            tt("pool", VN[0:16], VN[0:16], LNW[0:16], ALU.mult)
            tt("pool", VN[0:16], VN[0:16], LNB[0:16], ALU.add)
            dma("sp", gv, VN[0:16])
            MX = VNr.next()
            for g in range(4):
                ts("dve", MX[0:16, g * 128:(g + 1) * 128], VN[0:16, g * 128:(g + 1) * 128], WS0[:, g:g + 1], WS0[:, 4 + g:5 + g], ALU.mult, ALU.add)
            VB = VBr.next()
            tt("pool", VB[0:16], MX[0:16], US[0:16], ALU.mult)
            bank = pb().bitcast(BF16).rearrange("p (a b) -> p a b", a=8)
            for g in range(4):
                tr(bank[:, g, 0:16], VB[0:16, g * 128:(g + 1) * 128], IDB[0:16, 0:16])
            cp("act", GM[:, :, SB], bank[:, 0:4, 0:16])
        AR.release(mP3b)
        Wmk = AR.alloc([128, 8, 512], BF16); Wmv = AR.alloc([128, 8, 512], BF16)
        load_w(Wmk, w_mk, 1024, 512); load_w(Wmv, w_mv, 1024, 512)
        mP3c = AR.mark()
        branch_out(Wbg, 4, GM, Wg0, "mid")
        if stage <= 5:
            return finish()
        AR.release(mPh)
        XA = AR.alloc([128, 4, NTOK], BF16)
        KT = AR.alloc([128, 4, 256], BF16); VBm = AR.alloc([128, 2, 512], BF16)
        mC = AR.mark()
        assert mC <= mP3b, (mC, mP3b)
        AR.release(mP3c)
        MNT = AR.alloc([128, 8, 256], BF16)
        MEMr = Rot(AR, [128, 1024], F32, 1); XB4 = Rot(AR, [128, 1024], BF16, 1); JK4 = Rot(AR, [128, 1024], BF16, 1)
        for mt in range(2):
            mi = MEMr.next()
            dma("sp", mi, mem[mt * 128:(mt + 1) * 128, :])
            norm_to_T(mi, 128, mt * 128, V_NMEM, MNT, XB4, JK4)
        for mt in range(2):
            for (W_, o_, isv) in ((Wmk, mk, False), (Wmv, mv, True)):
                bank = pb()
                mm(bank, [(MNT[:, kc, mt * 128:(mt + 1) * 128], W_[:, kc, :]) for kc in range(8)])
                ko = TMr.next()
                act(ko, bank, AF.Copy)
                dma("sp", o_[mt * 128:(mt + 1) * 128, :], ko)
                if isv:
                    cp("act", VBm[:, mt, :], ko)
        for h in range(4):
            bank = pb()
            mm(bank[:, 0:256], [(Wmk[:, kc, h * 128:(h + 1) * 128], MNT[:, kc, :]) for kc in range(8)])
            cp("act", KT[:, h, :], bank[:, 0:256])
        AR.release(mC)
        Wq = AR.alloc([128, 8, 512], BF16); Wg2 = AR.alloc([128, 8, 1024], BF16); Wbx = AR.alloc([128, 4, 1024], BF16)
        load_w(Wq, w_in[:, C_Q:C_Q + 512], 1024, 512)
        load_w(Wg2, w_in[:, C_G + 2048:C_G + 3072], 1024, 1024); load_w(Wbx, w_bx, 512, 1024)
        mQ = AR.mark()
        QT = AR.alloc([128, 4, 512], BF16)
        Pr = Rot(AR, [128, 4, 256], BF16, 2); PNr = Rot(AR, [128, 4, 256], BF16, 4); PTr = Rot(AR, [128, 8, 128], BF16, 2)
        def p4_qproj(T):
            t0 = T * 512
            for h in range(4):
                bank = pb()
                mm(bank, [(Wq[:, kc, h * 128:(h + 1) * 128], XN[:, kc, t0:t0 + 512]) for kc in range(8)])
                act(QT[:, h, :], bank, AF.Copy, scale=float(128 ** -0.5))

        P4N = {}

        def p4_A(c, banks):
            q = c % 4
            sc = [PS[:, banks[0], :], PS[:, banks[1], :]]
            for h in range(4):
                mm(sc[h // 2][:, (h % 2) * 256:(h % 2 + 1) * 256], [(QT[:, h, q * 128:(q + 1) * 128], KT[:, h, :])])
            yield
            s = SM.next(); s2 = SM.next(); s3 = SM.next()
            for j in range(2):
                red("dve", s[:, 2 * j:2 * j + 2], sc[j].rearrange("p (h m) -> p h m", h=2), ALU.max)
            yield
            ts("dve", s2[:, 0:4], s[:, 0:4], -1.0, None, ALU.mult)
            yield
            P = Pr.next()
            for h in range(4):
                act(P[:, h, :], sc[h // 2][:, (h % 2) * 256:(h % 2 + 1) * 256], AF.Exp, bias=s2[:, h:h + 1], accum=s3[:, h:h + 1])
            yield
            S.add("dve", lambda e, o=s3[:, 4:8], i=s3[:, 0:4]: e.reciprocal(out=o, in_=i), reads=[s3[:, 0:4]], writes=[s3[:, 4:8]])
            yield
            PN = PNr.next()
            tt("dve", PN, P, bc3(s3[:, 4:8], 256), ALU.mult)
            P4N[c] = PN

        def p4_B(c, bk):
            c0 = c * 128
            PN = P4N.pop(c)
            bank = PS[:, bk, :].bitcast(BF16).rearrange("p (a b) -> p a b", a=8)
            for h in range(4):
                for mh in range(2):
                    tr(bank[:, h * 2 + mh, :], PN[:, h, mh * 128:(mh + 1) * 128], IDB)
            yield
            PT = PTr.next()
            cp("act", PT, bank)
            yield
            bo = PS[:, bk, :]
            for h in range(4):
                mm(bo[:, h * 128:(h + 1) * 128], [(VBm[:, mh, h * 128:(h + 1) * 128], PT[:, h * 2 + mh, :]) for mh in range(2)])
            yield
            cp("act", XA[:, :, c0:c0 + 128], bo.rearrange("p (h t) -> p h t", h=4))

        pb_banks[0] = [6, 7]
        p4_qproj(0)
        interleave_all([p4_A(0, (0, 1)), p4_A(1, (2, 3))])
        for k in range(8):
            c = 2 * k
            gens = [p4_B(c, 4), p4_B(c + 1, 5)]
            if (c + 2) % 4 == 0 and c + 2 < 16:
                p4_qproj((c + 2) // 4)
            if c + 2 < 16:
                gens += [p4_A(c + 2, (0, 1)), p4_A(c + 3, (2, 3))]
            interleave_all(gens)
        pb_banks[0] = list(range(8))
        if SAMPLE:
            sbi = [0]; bbi = [0]

            def sbank():
                b = 4 + sbi[0] % 4; sbi[0] += 1
                return PS[:, b, :]

            def bbank():
                b = bbi[0] % 4; bbi[0] += 1
                return PS[:, b, :]

            def sample_attn():
                AR.release(mQ)
                QS = AR.alloc([16, 512], F32); SCTs = AR.alloc([128, 2, 64], F32); OH4 = AR.alloc([4, 4], BF16)
                PSs = AR.alloc([64, 256], F32); PTs = AR.alloc([128, 2, 64], BF16)
                KSr = Rot(AR, [128, 2, 512], F32, 3); PRr = Rot(AR, [128, 2, 512], F32, 1); SELr = Rot(AR, [16, 128], F32, 2)
                VBsr = Rot(AR, [128, 2, 512], BF16, 2); OBr = Rot(AR, [4, 512], BF16, 2)
                cp("dve", OH4, CONST[0:4, K_OH:K_OH + 4])
                bq = sbank()
                mm(bq[0:16], [(XN[:, kc, SB], Wq[:, kc, :]) for kc in range(8)])
                act(QS, bq[0:16], AF.Copy, scale=float(128 ** -0.5))
                Ks = {}

                def load_k(b):
                    Ks[b] = KSr.next()
                    dma("sp", Ks[b], ck[b].rearrange("(mh m) f -> m mh f", mh=2))

                load_k(0); load_k(1)
                for b in range(16):
                    if b + 2 < 16:
                        load_k(b + 2)
                    K_ = Ks.pop(b)
                    SEL = SELr.next()
                    cp("act", SEL, ID[0:16, b:b + 1].to_broadcast([16, 128]))
                    bqb = sbank()
                    mm(bqb, [(SEL, QS)])
                    PR = PRr.next()
                    tt("dve", PR, K_, bc_mid(bqb, 2), ALU.mult)
                    red("dve", SCTs[:, :, 4 * b:4 * b + 4], PR.rearrange("p a (h d) -> p a h d", h=4), ALU.add)
                    yield
                bs_ = sbank()
                for mh in range(2):
                    tr(bs_[0:64, mh * 128:(mh + 1) * 128], SCTs[:, mh, :], ID)
                sA = SM.next()
                red("dve", sA[0:64, 0:1], bs_[0:64, 0:256], ALU.max)
                ts("dve", sA[0:64, 1:2], sA[0:64, 0:1], -1.0, None, ALU.mult)
                act(PSs, bs_[0:64, 0:256], AF.Exp, bias=sA[0:64, 1:2], accum=sA[0:64, 2:3])
                S.add("dve", lambda e, o=sA[0:64, 3:4], i=sA[0:64, 2:3]: e.reciprocal(out=o, in_=i), reads=[sA[0:64, 2:3]], writes=[sA[0:64, 3:4]])
                ts("dve", PSs, PSs, sA[0:64, 3:4], None, ALU.mult)
                yield
                bt_ = sbank()
                for mh in range(2):
                    tr(bt_[:, mh * 64:(mh + 1) * 64], PSs[:, mh * 128:(mh + 1) * 128], ID[0:64, 0:64])
                cp("act", PTs.rearrange("p a b -> p (a b)"), bt_[:, 0:128])
                Vs = {}

                def load_v(b):
                    V_ = KSr.next()
                    dma("sp", V_, cv[b].rearrange("(mh m) f -> m mh f", mh=2))
                    Vs[b] = VBsr.next()
                    cp("dve", Vs[b], V_)

                load_v(0)
                for b in range(16):
                    if b + 1 < 16:
                        load_v(b + 1)
                    VBs = Vs.pop(b)
                    bo_ = sbank()
                    mm(bo_[0:4], [(PTs[:, mh, 4 * b:4 * b + 4], VBs[:, mh, :]) for mh in range(2)])
                    OB = OBr.next()
                    cp("dve", OB, bo_[0:4])
                    yield
                    bxb = sbank()
                    for h in range(4):
                        mm(bxb[:, h:h + 1], [(OB[:, h * 128:(h + 1) * 128], OH4[:, h:h + 1])])
                    cp("act", XA[:, :, NT + b:NT + b + 1], bxb[:, 0:4].rearrange("p (h o) -> p h o", o=1))
                    yield

            interleave_all([sample_attn(), branch_out_g(Wbx, 4, XA, Wg2, "last", tiles4[:4], bbank)])
            drain(branch_out_g(Wbx, 4, XA, Wg2, "last", tiles4[4:]))
        else:
            branch_out(Wbx, 4, XA, Wg2, "last")
        if stage <= 6:
            return finish()
        AR.release(m0)
        H2 = AR.alloc([128, 17, 1024], F32)
        mF = AR.mark()
        Wo = AR.alloc([128, 8, 1024], BF16)
        load_w(Wo, w_o, 1024, 1024)
        XIN2 = Rot(AR, [128, 1024], F32, 2); XB5 = Rot(AR, [128, 1024], BF16, 2); JK5 = Rot(AR, [128, 1024], BF16, 1)
        def p5_A(ti):
            t0, n = tilesT[ti]
            xi = XIN2.next()
            dma("sp", xi[0:n], x[t0:t0 + n, :] if t0 < NT else xs_in[:, :])
            for hf in range(2):
                bank = pb()
                mm(bank[0:n], [(BO[:, kc, t0:t0 + n], Wo[:, kc, hf * 512:(hf + 1) * 512]) for kc in range(8)])
                tt("dve", H2[0:n, ti, hf * 512:(hf + 1) * 512], bank[0:n], xi[0:n, hf * 512:(hf + 1) * 512], ALU.add)

        p5_A(0)
        for ti, (t0, n) in enumerate(tilesT):
            if ti + 1 < len(tilesT):
                p5_A(ti + 1)
            norm_to_T(H2[0:n, ti, :], n, t0, V_NFFN, XN, XB5, JK5)
        if stage <= 7:
            return finish()
        AR.release(mF)
        Wupr = Rot(AR, [128, 8, 512], BF16, 2); Wdnr = Rot(AR, [128, 4, 1024], BF16, 2)
        RL = Rot(AR, [128, 512], BF16, 2)
        NFW = AR.alloc([128, 1024], F32)
        dma("sp", NFW, r_nfin.partition_broadcast(128))
        YOr = Rot(AR, [128, 1024], F32, 2); JK6 = Rot(AR, [128, 1024], BF16, 1)
        wnext = (Wupr.next(), Wdnr.next())
        load_w(wnext[0], w_up[:, 0:512], 1024, 512); load_w(wnext[1], w_dn[0:512, :], 512, 1024)
        for dc in range(8):
            wu, wd = wnext
            if dc < 7:
                wnext = (Wupr.next(), Wdnr.next())
                load_w(wnext[0], w_up[:, (dc + 1) * 512:(dc + 2) * 512], 1024, 512)
                load_w(wnext[1], w_dn[(dc + 1) * 512:(dc + 2) * 512, :], 512, 1024)
            AT = BO[:, (dc % 2) * 4:(dc % 2) * 4 + 4, :]
            for (t0, n) in tilesF:
                for j in range(4):
                    bank = pb()
                    mm(bank[:, 0:n], [(wu[:, kc, j * 128:(j + 1) * 128], XN[:, kc, t0:t0 + n]) for kc in range(8)])
                    rl = RL.next()
                    act(rl[:, 0:n], bank[:, 0:n], AF.Relu)
                    tt("pool", AT[:, j, t0:t0 + n], rl[:, 0:n], rl[:, 0:n], ALU.mult)
            for ti, (t0, n) in enumerate(tilesT):
                for hf in range(2):
                    bank = pb()
                    mm(bank[0:n], [(AT[:, j, t0:t0 + n], wd[:, j, hf * 512:(hf + 1) * 512]) for j in range(4)])
                    hs = H2[0:n, ti, hf * 512:(hf + 1) * 512]
                    tt("dve", hs, bank[0:n], hs, ALU.add)
        for ti, (t0, n) in enumerate(tilesT):
            s = SM.next(); jk = JK6.next()
            act(jk[0:n], H2[0:n, ti, :], AF.Square, accum=s[0:n, 0:1])
            r = rstd_of(s[0:n, 0:1], n, 1024.0)
            yo = YOr.next()
            stt("dve", yo[0:n], H2[0:n, ti, :], r, NFW[0:n], ALU.mult, ALU.mult)
            dma("sp", y[t0:t0 + n, :] if t0 < NT else ys[:, :], yo[0:n])
        return finish()


def make_consts():
    c = np.zeros((128, K_END), np.float32)
    i = np.arange(128)
    c[:, K_ID:K_ID + 128] = np.eye(128)
    c[:, K_TRI:K_TRI + 128] = (i[:, None] <= i[None, :])
    c[:, K_L:K_L + 128] = (i[:, None] > i[None, :])
    c[:, K_TRIL:K_TRIL + 128] = (i[None, :] <= i[:, None])
    c[:, K_ONES:K_ONES + 128] = 1.0
    for h in range(4):
        c[h, K_OH + h] = 1.0
    for h in range(16):
        c[h, K_E + h * 64:K_E + (h + 1) * 64] = 1.0
    return c


def kernel(x_prompt, x_sample, mem_prompt, cache_mem_k, cache_mem_v, state_conv, state_ssm,
           norm_mix_w, w_in, gm_ln_w, gm_ln_b, gm_ws, gm_bs, conv_w, conv_b, dt_bias, a_log,
           d_skip, ssd_norm_w, mem_norm_w, w_mem_k, w_mem_v, w_br_gm, w_br_ssd, w_br_xa, w_out,
           norm_ffn_w, w_up, w_down, norm_final_w, _stage=99):
    f = lambda a: np.ascontiguousarray(np.asarray(a, dtype=np.float32))
    vec = np.zeros((128, V_END), np.float32)
    pl = lambda v: f(v).reshape(-1, 128).T
    vec[:, V_NMIX:V_NMIX + 8] = pl(norm_mix_w[0]); vec[:, V_NMEM:V_NMEM + 8] = pl(mem_norm_w[0])
    vec[:, V_NSSD:V_NSSD + 8] = pl(ssd_norm_w[0]); vec[:, V_NFFN:V_NFFN + 8] = pl(norm_ffn_w[0])
    cw = f(conv_w[0])
    vec[:, V_CW:V_CW + 48] = cw.reshape(4, 12, 128).transpose(2, 1, 0).reshape(128, 48)
    vec[:, V_CB:V_CB + 12] = pl(conv_b[0])
    vec[0:16, V_DTB] = f(dt_bias[0]); vec[0:16, V_ALOG] = f(a_log[0]); vec[0:16, V_DSK] = f(d_skip[0])
    shared = {
        "w_in": f(w_in[0]), "w_mk": f(w_mem_k[0]), "w_mv": f(w_mem_v[0]), "w_bg": f(w_br_gm[0]),
        "w_bs": f(w_br_ssd[0]), "w_bx": f(w_br_xa[0]), "w_o": f(w_out[0]), "w_up": f(w_up[0]), "w_dn": f(w_down[0]),
        "consts": make_consts(), "vecs": vec, "gm_ws": f(gm_ws[0]),
        "r_lnw": f(gm_ln_w[0]).reshape(1, 512), "r_lnb": f(gm_ln_b[0]).reshape(1, 512), "r_bs": f(gm_bs[0]).reshape(1, 512),
        "r_dtb": f(dt_bias[0]).reshape(1, 16), "r_alog": f(a_log[0]).reshape(1, 16), "r_dsk": f(d_skip[0]).reshape(1, 16),
        "r_nfin": f(norm_final_w).reshape(1, 1024), "r_cw": f(conv_w[0]).reshape(1, 4 * 1536), "r_cb": f(conv_b[0]).reshape(1, 1536),
        "r_ws00": f(gm_ws[0][:, 0, 0]).reshape(1, 4), "r_bs0": f(gm_bs[0][:, 0]).reshape(1, 4),
    }
    in_maps = []
    for c in range(8):
        m = dict(shared)
        sl = slice(16 * c, 16 * c + 16)
        m["x"] = f(x_prompt[c]); m["xs"] = f(x_sample[sl, 0]); m["mem"] = f(mem_prompt[c])
        m["ck"] = f(cache_mem_k[0, sl]).reshape(16, 256, 512); m["cv"] = f(cache_mem_v[0, sl]).reshape(16, 256, 512)
        m["sconv"] = f(state_conv[0, sl]); m["sssm"] = f(state_ssm[0, sl]).reshape(16, 1024, 128)
        in_maps.append(m)
    nc = build(_stage)
    res = run_bass_kernel_spmd(nc, in_maps, core_ids=list(range(8)))
    DEBUG["exec_ns"] = getattr(res, "exec_time_ns", None)
    R = res.results
    g = lambda k: np.stack([np.asarray(R[c][k], dtype=np.float32) for c in range(8)])
    y_prompt = g("y")
    y_sample = g("ys").reshape(128, 1, 1024)
    mem_k = g("mk").reshape(1, 8, 256, 4, 128)
    mem_v = g("mv").reshape(1, 8, 256, 4, 128)
    conv_p = g("convp").reshape(1, 8, 3, 1536)
    ssm_p = g("ssmp").reshape(1, 8, 16, 64, 128)
    conv_s = g("convs").reshape(1, 128, 3, 1536)
    ssm_s = g("ssms").reshape(1, 128, 16, 64, 128)
    gv_s = g("gv").reshape(1, 128, 1, 512)
    return (y_prompt, y_sample, mem_k, mem_v, conv_p, ssm_p, conv_s, ssm_s, gv_s)
```

```python
import numpy as np
from contextlib import ExitStack
import concourse.bass as bass
import concourse.mybir as mybir
from concourse.bass_utils import run_bass_kernel_spmd

F32 = mybir.dt.float32
BF16 = mybir.dt.bfloat16
AF = mybir.ActivationFunctionType
ALU = mybir.AluOpType
AX = mybir.AxisListType
ESZ = {F32: 4, BF16: 2}
PAGE = 256
COMPUTE = ("pe", "act", "dve", "pool")
EPS = 1e-6
NT = 2048
NS = 16
NTOK = NT + NS
NCH = 16


class Op:
    __slots__ = ("eng", "fn", "deps", "signal", "is_dma", "sem", "val", "q")

    def __init__(self, eng, fn, is_dma):
        self.eng = eng; self.fn = fn; self.deps = set(); self.signal = False
        self.is_dma = is_dma; self.sem = None; self.val = 0; self.q = None


class Sched:
    def __init__(self, nc, ndma_sems=24):
        self.nc = nc
        self.ops = {e: [] for e in ("pe", "act", "dve", "pool", "sp")}
        self.last_w = {}
        self.rd = {}
        self.ndma = ndma_sems
        self.dma_count = {"sp": 0, "act": 0, "pool": 0}
        self.dma_ops = []

    @staticmethod
    def pages(ap):
        sp = str(ap.space).upper()
        if "DRAM" in sp or "HBM" in sp:
            return ()
        a = ap.ap
        es = ESZ.get(ap.dtype, 4)
        pstep = a[0][0]
        off = int(ap.offset)
        lo = off % pstep if pstep > 0 else off
        hi = lo
        for st, cnt in a[1:]:
            hi += st * (cnt - 1)
        hi += 1
        name = ap.tensor.name
        pg = 2048 if name == "ps" else PAGE
        return [(name, p) for p in range((lo * es) // pg, (hi * es - 1) // pg + 1)]

    def add(self, eng, fn, reads=(), writes=(), dma=False):
        op = Op(eng, fn, dma)
        deps = set()
        rpages = []
        for ap in reads:
            rpages.extend(self.pages(ap))
        wpages = []
        for ap in writes:
            wpages.extend(self.pages(ap))
        for pg in rpages:
            w = self.last_w.get(pg)
            if w is not None:
                if not (w.eng == eng and eng == "pe" and not w.is_dma and not dma):
                    deps.add(w)
        for pg in wpages:
            w = self.last_w.get(pg)
            if w is not None:
                if w.is_dma or dma or w.eng != eng or eng != "pe":
                    deps.add(w)
            r = self.rd.get(pg)
            if r:
                for o in r.values():
                    if o.is_dma or dma or o.eng != eng or eng != "pe":
                        deps.add(o)
        for pg in rpages:
            r = self.rd.get(pg)
            if r is None:
                r = {}; self.rd[pg] = r
            if dma:
                r[("dma", id(op))] = op
            else:
                r[eng] = op
        for pg in wpages:
            self.last_w[pg] = op
            self.rd[pg] = {}
        deps.discard(op)
        op.deps = deps
        for d in deps:
            d.signal = True
        self.ops[eng].append(op)
        if dma:
            k = self.dma_count[eng]
            self.dma_count[eng] += 1
            op.q = (eng, k)
            self.dma_ops.append(op)
        return op

    def emit(self, stack):
        nc = self.nc
        sems = {e: stack.enter_context(nc.semaphore("s_" + e)) for e in COMPUTE}
        dsems = {}
        for q, cnt in self.dma_count.items():
            if cnt:
                dsems[q] = [stack.enter_context(nc.semaphore("d_%s_%d" % (q, i))) for i in range(min(cnt, self.ndma))]
        for e in COMPUTE:
            c = 0
            for op in self.ops[e]:
                if op.is_dma:
                    continue
                if op.signal:
                    c += 1
                    op.sem = sems[e]; op.val = c
        for op in self.dma_ops:
            q, k = op.q
            op.sem = dsems[q][k % self.ndma]
            op.val = 16 * (k // self.ndma + 1)
        final_waits = {}
        for op in self.dma_ops:
            final_waits[op.sem] = max(final_waits.get(op.sem, 0), op.val)

        def run(name, e):
            seen = {}
            for op in self.ops[name]:
                waits = {}
                for d in op.deps:
                    if waits.get(d.sem, 0) < d.val:
                        waits[d.sem] = d.val
                if op.is_dma:
                    q, k = op.q
                    if k >= self.ndma:
                        v = op.val - 16
                        if waits.get(op.sem, 0) < v:
                            waits[op.sem] = v
                for s, v in waits.items():
                    if seen.get(s, 0) < v:
                        e.wait_ge(s, v)
                        seen[s] = v
                inst = op.fn(e)
                if op.is_dma:
                    inst.then_inc(op.sem, 16)
                elif op.signal:
                    inst.then_inc(op.sem, 1)
            if name == "sp":
                for s, v in final_waits.items():
                    if seen.get(s, 0) < v:
                        e.wait_ge(s, v)

        block = stack.enter_context(nc.Block())

        @block.sync
        def _(e):
            run("sp", e)

        @block.scalar
        def _(e):
            run("act", e)

        @block.vector
        def _(e):
            run("dve", e)

        @block.gpsimd
        def _(e):
            run("pool", e)

        @block.tensor
        def _(e):
            run("pe", e)


class Arena:
    def __init__(self, tens, nbytes):
        self.t = tens; self.n = nbytes; self.off = 0; self.peak = 0

    def alloc(self, shape, dt):
        n = int(np.prod(shape[1:])) * ESZ[dt]
        ap = self.t[:, self.off // 4:(self.off + n) // 4]
        if dt != F32:
            ap = ap.bitcast(dt)
        if len(shape) == 3:
            ap = ap.rearrange("p (a b) -> p a b", a=shape[1])
        elif len(shape) == 4:
            ap = ap.rearrange("p (a b c) -> p a b c", a=shape[1], b=shape[2])
        if shape[0] < 128:
            ap = ap[0:shape[0]]
        self.off += (n + PAGE - 1) // PAGE * PAGE
        self.peak = max(self.peak, self.off)
        assert self.off <= self.n, ("arena overflow", self.off, self.n)
        return ap

    def alloc_top(self, shape, dt):
        n = int(np.prod(shape[1:])) * ESZ[dt]
        self.n -= (n + PAGE - 1) // PAGE * PAGE
        assert self.n >= self.peak, ("arena top overflow", self.n, self.peak)
        ap = self.t[:, self.n // 4:(self.n + n) // 4]
        if dt != F32:
            ap = ap.bitcast(dt)
        if len(shape) == 3:
            ap = ap.rearrange("p (a b) -> p a b", a=shape[1])
        return ap

    def mark(self):
        return self.off

    def release(self, m):
        self.off = m


class Rot:
    def __init__(self, ar, shape, dt, n):
        self.t = [ar.alloc(shape, dt) for _ in range(n)]; self.i = 0

    def next(self):
        r = self.t[self.i % len(self.t)]; self.i += 1
        return r


C_U, C_V, C_Z, C_XBC, C_DT, C_Q, C_G = 0, 512, 1024, 2048, 3584, 3600, 4112
K_ID, K_TRI, K_L, K_TRIL, K_ONES, K_OH, K_E, K_END = 0, 128, 256, 384, 512, 640, 644, 644 + 1024
V_NMIX, V_NMEM, V_NSSD, V_NFFN, V_CW, V_CB, V_DTB, V_ALOG, V_DSK, V_END = 0, 8, 16, 24, 32, 80, 92, 93, 94, 95

DEBUG = {}


def build(stage=99):
    nc = bass.Bass("TRN2", target_bir_lowering=False)

    def din(name, shape):
        return nc.dram_tensor(name, list(shape), F32, kind="ExternalInput").ap()

    def dout(name, shape):
        return nc.dram_tensor(name, list(shape), F32, kind="ExternalOutput").ap()

    x = din("x", [NT, 1024]); xs_in = din("xs", [NS, 1024]); mem = din("mem", [256, 1024])
    ck = din("ck", [NS, 256, 512]); cv = din("cv", [NS, 256, 512])
    sconv = din("sconv", [NS, 3, 1536]); sssm = din("sssm", [NS, 1024, 128])
    w_in = din("w_in", [1024, 7184]); w_mk = din("w_mk", [1024, 512]); w_mv = din("w_mv", [1024, 512])
    w_bg = din("w_bg", [512, 1024]); w_bs = din("w_bs", [1024, 1024]); w_bx = din("w_bx", [512, 1024])
    w_o = din("w_o", [1024, 1024]); w_up = din("w_up", [1024, 4096]); w_dn = din("w_dn", [4096, 1024])
    consts = din("consts", [128, K_END]); vecs = din("vecs", [128, V_END])
    gm_ws = din("gm_ws", [4, 128, 128])
    r_lnw = din("r_lnw", [1, 512]); r_lnb = din("r_lnb", [1, 512]); r_bs = din("r_bs", [1, 512])
    r_dtb = din("r_dtb", [1, 16]); r_alog = din("r_alog", [1, 16]); r_dsk = din("r_dsk", [1, 16])
    r_nfin = din("r_nfin", [1, 1024]); r_cw = din("r_cw", [1, 4 * 1536]); r_cb = din("r_cb", [1, 1536])
    r_ws00 = din("r_ws00", [1, 4]); r_bs0 = din("r_bs0", [1, 4])

    y = dout("y", [NT, 1024]); ys = dout("ys", [NS, 1024]); mk = dout("mk", [256, 512]); mv = dout("mv", [256, 512])
    convp = dout("convp", [3, 1536]); ssmp = dout("ssmp", [1024, 128])
    convs = dout("convs", [NS, 3, 1536]); ssms = dout("ssms", [NS, 1024, 128]); gv = dout("gv", [NS, 512])

    st = ExitStack()
    def finish():
        S.emit(st)
        st.close()
        DEBUG['peak'] = AR.peak
        return nc
    if True:
        NBY = 207 * 1024
        At = st.enter_context(nc.sbuf_tensor("arena", [128, NBY // 4], F32))
        PS = st.enter_context(nc.psum_tensor("ps", [128, 8, 512], F32))
        S = Sched(nc)
        AR = Arena(At, NBY)
        pbi = [0]

        pb_banks = [list(range(8))]

        def pb():
            bl = pb_banks[0]
            b = bl[pbi[0] % len(bl)]; pbi[0] += 1
            return PS[:, b, :]

        def isap(v):
            return not isinstance(v, (int, float)) and v is not None

        def act(out, in_, func, bias=None, scale=None, accum=None):
            rd = [in_] + [v for v in (bias, scale) if isap(v)]
            wr = [out] + ([accum] if accum is not None else [])
            kw = {}
            if bias is not None: kw["bias"] = bias
            if scale is not None: kw["scale"] = scale
            if accum is not None: kw["accum_out"] = accum
            S.add("act", lambda e: e.activation(out=out, in_=in_, func=func, **kw), reads=rd, writes=wr)

        def tt(eng, out, in0, in1, op):
            S.add(eng, lambda e: e.tensor_tensor(out=out, in0=in0, in1=in1, op=op), reads=[in0, in1], writes=[out])

        def ts(eng, out, in0, s1, s2, op0, op1=None):
            rd = [in0] + [v for v in (s1, s2) if isap(v)]
            if op1 is None:
                S.add(eng, lambda e: e.tensor_scalar(out=out, in0=in0, scalar1=s1, scalar2=None, op0=op0), reads=rd, writes=[out])
            else:
                S.add(eng, lambda e: e.tensor_scalar(out=out, in0=in0, scalar1=s1, scalar2=s2, op0=op0, op1=op1), reads=rd, writes=[out])

        def stt(eng, out, in0, sc, in1, op0, op1):
            rd = [in0, in1] + ([sc] if isap(sc) else [])
            S.add(eng, lambda e: e.scalar_tensor_tensor(out=out, in0=in0, scalar=sc, in1=in1, op0=op0, op1=op1), reads=rd, writes=[out])

        def cp(eng, out, in_):
            if eng == "act":
                act(out, in_, AF.Copy)
            else:
                S.add(eng, lambda e: e.tensor_copy(out=out, in_=in_), reads=[in_], writes=[out])

        def red(eng, out, in_, op):
            S.add(eng, lambda e: e.tensor_reduce(out=out, in_=in_, axis=AX.X, op=op), reads=[in_], writes=[out])

        def memset(eng, out, val):
            S.add(eng, lambda e: e.memset(out, val), writes=[out])

        def mm(out, pairs):
            def fn(e):
                n = len(pairs); ins = None
                for i, (l, r) in enumerate(pairs):
                    ins = e.matmul(out, l, r, start=(i == 0), stop=(i == n - 1))
                return ins
            S.add("pe", fn, reads=[a for p in pairs for a in p], writes=[out])

        def tr(out, in_, ident):
            S.add("pe", lambda e: e.transpose(out, in_, ident), reads=[in_, ident], writes=[out])

        def dma(q, out, in_, slow=False):
            if slow:
                S.add(q, lambda e: e.dma_start(out=out, in_=in_, allow_slow_non_contiguous=True), reads=[in_], writes=[out], dma=True)
            else:
                S.add(q, lambda e: e.dma_start(out=out, in_=in_), reads=[in_], writes=[out], dma=True)

        def load_w(dst, src, K, N):
            srcv = src.rearrange("(k p) n -> p k n", p=128)
            for c0 in range(0, N, 2048):
                c1 = min(N, c0 + 2048)
                dma("pool", dst[:, :, c0:c1], srcv[:, :, c0:c1])

        def bc3(ap, n):
            return ap.unsqueeze(2).to_broadcast([ap.shape[0], ap.shape[1], n])

        def bc_mid(ap, n):
            return ap.unsqueeze(1).to_broadcast([ap.shape[0], n, ap.shape[1]])

        XN = AR.alloc([128, 8, NTOK], BF16)
        BO = AR.alloc([128, 8, NTOK], BF16)
        CONST = AR.alloc([128, K_E], F32)
        VEC = AR.alloc([128, V_END], F32)
        IDB = AR.alloc([128, 128], BF16)
        NEGH = AR.alloc([128, 8], F32)
        SM = Rot(AR, [128, 8], F32, 12)
        ID = CONST[:, K_ID:K_ID + 128]; TRI = CONST[:, K_TRI:K_TRI + 128]; LST = CONST[:, K_L:K_L + 128]
        TRIL = CONST[:, K_TRIL:K_TRIL + 128]; ONES = CONST[:, K_ONES:K_ONES + 128]
        dma("sp", CONST, consts[:, 0:K_E]); dma("sp", VEC, vecs)
        cp("dve", IDB, ID)
        CB16 = AR.alloc([128, 3, 128], BF16)
        TRIb = CB16[:, 0, :]; ONESb = CB16[:, 1, :]; LSTb = CB16[:, 2, :]
        cp("dve", TRIb, TRI); cp("dve", ONESb, ONES); cp("dve", LSTb, LST)
        memset("pool", NEGH, -0.5)

        tiles4 = [(i * 512, 512) for i in range(4)] + [(NT, NS)]
        tiles17 = [(i * 128, 128) for i in range(16)] + [(NT, NS)]

        def rstd_of(ss, n, count, eps_scale=1.0):
            s = SM.next()
            ts("dve", s[0:n, 0:1], ss, 1.0 / count, EPS, ALU.mult, ALU.add)
            tt("pool", s[0:n, 1:2], s[0:n, 0:1], NEGH[0:n, 0:1], ALU.pow)
            return s[0:n, 1:2]

        def norm_to_T(src, n, t0, nwcol, dstT, XB, JK):
            s = SM.next()
            jk = JK.next()
            act(jk[0:n], src, AF.Square, accum=s[0:n, 0:1])
            r = rstd_of(s[0:n, 0:1], n, 1024.0)
            xb = XB.next()
            ts("dve", xb[0:n], src, r, None, ALU.mult)
            bank = pb().bitcast(BF16).rearrange("p (a b) -> p a b", a=8)
            for kc in range(8):
                tr(bank[:, kc, 0:n], xb[0:n, kc * 128:(kc + 1) * 128], IDB[0:n, 0:n])
            tt("dve", dstT[:, :, t0:t0 + n], bank[:, :, 0:n], bc3(VEC[:, nwcol:nwcol + 8], n), ALU.mult)

        m0 = AR.mark()
        XIN = Rot(AR, [128, 1024], F32, 3)
        XBr = Rot(AR, [128, 1024], BF16, 2)
        JKr = Rot(AR, [128, 1024], BF16, 1)
        mP = AR.mark()
        for (t0, n) in tiles17:
            xi = XIN.next()
            dma("sp", xi[0:n], x[t0:t0 + n, :] if t0 < NT else xs_in[:, :])
            norm_to_T(xi[0:n], n, t0, V_NMIX, XN, XBr, JKr)

        if stage <= 0:
            return finish()
        AR.release(m0)
        Wxbc = AR.alloc([128, 8, 1536], BF16); Wz = AR.alloc([128, 8, 1024], BF16); Wdt = AR.alloc([128, 8, 16], BF16)
        load_w(Wxbc, w_in[:, C_XBC:C_XBC + 1536], 1024, 1536)
        load_w(Wdt, w_in[:, C_DT:C_DT + 16], 1024, 16)
        load_w(Wz, w_in[:, C_Z:C_Z + 1024], 1024, 1024)
        if stage <= 0.5:
            return finish()
        RB = AR.alloc([128, 64], F32)
        DTB = RB[:, 0:16]; ALOG = RB[:, 16:32]; DSK = RB[:, 32:48]; ANEG = RB[:, 48:64]
        dma("sp", DTB, r_dtb.partition_broadcast(128)); dma("sp", ALOG, r_alog.partition_broadcast(128))
        dma("sp", DSK, r_dsk.partition_broadcast(128))
        act(ANEG, ALOG, AF.Exp)
        ts("dve", ANEG, ANEG, -1.0, None, ALU.mult)
        if stage <= 0.7:
            return finish()
        PRE_S = AR.alloc([128, 8, 256], F32)
        DT = PRE_S[:, 0, :]; AA = PRE_S[:, 1, :]; EAC = PRE_S[:, 2, :]; CD = PRE_S[:, 3, :]
        DEND = PRE_S[:, 4, :]; DTD = PRE_S[:, 5, :]; T6 = PRE_S[:, 6, :]; T7 = PRE_S[:, 7, :]
        bk = pb()
        for c in range(NCH):
            mm(bk[:, c * 16:(c + 1) * 16], [(XN[:, kc, c * 128:(c + 1) * 128], Wdt[:, kc, :]) for kc in range(8)])
        v3 = lambda ap: ap.rearrange("p (c h) -> p c h", c=16)
        tt("dve", v3(T6), v3(bk[:, 0:256]), bc_mid(DTB, 16), ALU.add)
        act(T7, T6, AF.Exp)
        ts("dve", T7, T7, 1.0, None, ALU.add)
        act(DT, T7, AF.Ln)
        if stage <= 0.8:
            return finish()
        tt("dve", v3(AA), v3(DT), bc_mid(ANEG, 16), ALU.mult)
        AHL = AR.alloc([128, 2, 256], BF16)
        AAh = AHL[:, 0, :]; AAl = AHL[:, 1, :]
        cp("dve", AAh, AA)
        tt("dve", T6, AA, AAh, ALU.subtract)
        cp("dve", AAl, T6)
        AALf = AR.alloc([128, 256], F32)
        cp("dve", AALf, AAl)
        bk2 = pb()
        mm(bk2[:, 0:256], [(TRIb, AAh), (TRIb, AAl)])
        mm(bk2[:, 256:512], [(ONESb, AAh), (ONESb, AAl)])
        act(EAC, bk2[:, 0:256], AF.Exp)
        act(CD, bk2[:, 256:512], AF.Exp)
        act(T6, bk2[:, 0:256], AF.Copy)
        act(T7, bk2[:, 256:512], AF.Copy)
        tt("dve", T7, T7, T6, ALU.subtract)
        act(DEND, T7, AF.Exp)
        tt("dve", DTD, DT, DEND, ALU.mult)

        if stage <= 1:
            return finish()
        mS = AR.mark()
        HALO = AR.alloc([128, 12, 4], F32)
        memset("pool", HALO, 0.0)
        HW = 256
        PREc = Rot(AR, [128, HW + 4], F32, 2)
        XC = AR.alloc([128, 12, HW], F32)
        BCT2 = [AR.alloc([128, 4, HW], BF16), AR.alloc([128, 4, HW], BF16)]
        CTr = Rot(AR, [128, HW], F32, 1)
        XSr = Rot(AR, [128, 1024], F32, 2)
        XDTr = Rot(AR, [128, 16, 64], BF16, 2)
        XDTDr = Rot(AR, [128, 16, 64], BF16, 2)
        BTKr = Rot(AR, [128, 2, 128], BF16, 2)
        CBMr = Rot(AR, [128, 2, 128], F32, 2)
        RHr = Rot(AR, [128, 2, 4, 128], BF16, 2)
        ESr = Rot(AR, [128, 4, 128], F32, 2)
        MTr = Rot(AR, [128, 16, 128], BF16, 1)
        Y1r = Rot(AR, [128, 1024], F32, 1)
        XSDr = Rot(AR, [128, 1024], F32, 1)
        Zr = Rot(AR, [128, 1024], F32, 2)
        THr = Rot(AR, [128, 1024], F32, 1)
        YNr = Rot(AR, [128, 1024], BF16, 1)
        HT = AR.alloc([128, 1024], F32)
        HTb = AR.alloc([128, 1024], BF16)
        memset("pool", HT, 0.0)
        memset("pool", HTb, 0.0)
        JK2 = Rot(AR, [128, 512], BF16, 2)
        def conv_gen(T, bankf):
            t0 = T * HW
            for cc in range(12):
                if cc:
                    yield
                bank = bankf()
                mm(bank[:, 0:HW], [(Wxbc[:, kc, cc * 128:(cc + 1) * 128], XN[:, kc, t0:t0 + HW]) for kc in range(8)])
                pre = PREc.next()
                act(pre[:, 0:3], HALO[:, cc, 0:3], AF.Copy)
                act(pre[:, 3:HW + 3], bank[:, 0:HW], AF.Copy)
                cw = VEC[:, V_CW + cc * 4:V_CW + cc * 4 + 4]
                act(XC[:, cc, :], pre[:, 0:HW], AF.Identity, bias=VEC[:, V_CB + cc:V_CB + cc + 1], scale=cw[:, 0:1])
                for k in range(1, 4):
                    if True:
                        stt("dve", XC[:, cc, :], pre[:, k:k + HW], cw[:, k:k + 1], XC[:, cc, :], ALU.mult, ALU.add)
                    else:
                        ct = CTr.next()
                        ts("pool", ct, pre[:, k:k + HW], cw[:, k:k + 1], None, ALU.mult)
                        tt("pool", XC[:, cc, :], XC[:, cc, :], ct, ALU.add)
                act(HALO[:, cc, 0:3], pre[:, HW:HW + 3], AF.Copy)
                if t0 + HW == NT:
                    dma("sp", convp[:, cc * 128:(cc + 1) * 128].rearrange("k c -> c k"), pre[:, HW:HW + 3], slow=True)
            yield
            for cc in range(12):
                act(XC[:, cc, :], XC[:, cc, :], AF.Silu)
            for j in range(4):
                cp("act", BCT2[T % 2][:, j, :], XC[:, 8 + j, :])

        CH = {}
        pfi = [0]; pti = [0]

        def pbf():
            b = 6 + pfi[0] % 2; pfi[0] += 1
            return PS[:, b, :]

        def pbt():
            b = 4 + pti[0] % 2; pti[0] += 1
            return PS[:, b, :]

        def front(c):
            q = c % 2
            t0 = c * 128
            cs = slice(c * 16, (c + 1) * 16)
            qs = slice(q * 128, (q + 1) * 128)
            d = {}
            CH[c] = d
            BCT = BCT2[(c // 2) % 2]
            d["BCT"] = BCT
            XS = XSr.next()
            b0 = pbf(); b1 = pbf()
            for f in range(8):
                bank = b0 if f < 4 else b1
                tr(bank[:, (f % 4) * 128:(f % 4 + 1) * 128], XC[:, f, qs], ID)
            yield
            act(XS[:, 0:512], b0, AF.Copy)
            act(XS[:, 512:1024], b1, AF.Copy)
            b2 = pbf()
            for g in range(2):
                tr(b2[:, g * 128:(g + 1) * 128], XC[:, 8 + g, qs], ID)
            yield
            BTK = BTKr.next()
            cp("dve", BTK, b2[:, 0:256].rearrange("p (g n) -> p g n", g=2))
            yield
            XS3 = XS.rearrange("p (h d) -> p h d", h=16)
            XDT = XDTr.next(); XDTD = XDTDr.next()
            tt("dve", XDT, XS3, bc3(DT[:, cs], 64), ALU.mult)
            tt("pool", XDTD, XS3, bc3(DTD[:, cs], 64), ALU.mult)
            d.update(XS=XS, XS3=XS3, BTK=BTK, XDTD=XDTD, qs=qs, cs=cs, t0=t0)
            yield
            b3 = pbf()
            for g in range(2):
                mm(b3[:, g * 128:(g + 1) * 128], [(BCT[:, g, qs], BCT[:, 2 + g, qs])])
            CBM = CBMr.next()
            tt("dve", CBM, b3[:, 0:256].rearrange("p (g n) -> p g n", g=2), bc_mid(TRI, 2), ALU.mult)
            yield
            Z = Zr.next(); TH = THr.next()
            for hf in range(2):
                bz = pbf()
                mm(bz, [(XN[:, kc, t0:t0 + 128], Wz[:, kc, hf * 512:(hf + 1) * 512]) for kc in range(8)])
                act(Z[:, hf * 512:(hf + 1) * 512], bz, AF.Copy)
                act(TH[:, hf * 512:(hf + 1) * 512], bz, AF.Tanh, scale=0.5)
                yield
            stt("dve", Z, TH, 1.0, Z, ALU.add, ALU.mult)
            d["Z"] = Z
            yield
            MT = MTr.next()
            yd = [PS[:, 2 * (c % 2), :], PS[:, 2 * (c % 2) + 1, :]]
            d["yd"] = yd
            def mk_rh(hq):
                RH = RHr.next()
                hsl = slice(c * 16 + hq * 4, c * 16 + hq * 4 + 4)
                tt("dve", RH[:, 0], bc_mid(TRIb, 4), bc3(AAh[:, hsl], 128), ALU.mult)
                for hh in range(4):
                    hcol = c * 16 + hq * 4 + hh
                    act(RH[:, 1, hh, :], TRIb, AF.Copy, scale=AALf[:, hcol:hcol + 1])
                return RH

            def seg_mm(RH):
                bs = pbf()
                mm(bs, [(LSTb, RH[:, 0].rearrange("p a b -> p (a b)")), (LSTb, RH[:, 1].rearrange("p a b -> p (a b)"))])
                return bs

            segs = {}
            for hq in range(2):
                segs[hq] = seg_mm(mk_rh(hq))
                yield
            for hq in range(4):
                g = hq // 2
                ES = ESr.next()
                act(ES.rearrange("p a b -> p (a b)"), segs.pop(hq), AF.Exp)
                tt("dve", MT[:, hq * 4:hq * 4 + 4, :], ES, bc_mid(CBM[:, g, :], 4), ALU.mult)
                yield
                if hq + 2 < 4:
                    segs[hq + 2] = seg_mm(mk_rh(hq + 2))
                for hh in range(4):
                    h = hq * 4 + hh
                    mm(yd[h // 8][:, (h % 8) * 64:(h % 8 + 1) * 64], [(MT[:, h, :], XDT[:, h, :])])
                yield

        def tail(c):
            d = CH.pop(c)
            XS3 = d["XS3"]; BTK = d["BTK"]; XDTD = d["XDTD"]; qs = d["qs"]; cs = d["cs"]; t0 = d["t0"]; yd = d["yd"]; Z = d["Z"]
            BCT = d["BCT"]
            XSD = XSDr.next()
            tt("pool", XSD.rearrange("p (h d) -> p h d", h=16), XS3, bc3(DSK, 64), ALU.mult)
            Y1 = Y1r.next()
            if c > 0:
                yo = [pbt(), pbt()]
                for g in range(2):
                    mm(yo[g], [(BCT[:, 2 + g, qs], HTb[:, g * 512:(g + 1) * 512])])
                for g in range(2):
                    tt("dve", Y1[:, g * 512:(g + 1) * 512].rearrange("p (h d) -> p h d", h=8),
                       yo[g].rearrange("p (h d) -> p h d", h=8), bc3(EAC[:, c * 16 + g * 8:c * 16 + g * 8 + 8], 64), ALU.mult)
            yield
            stb = [pbt(), pbt()]
            for g in range(2):
                mm(stb[g], [(BTK[:, g, :], XDTD[:, g * 8:(g + 1) * 8, :].rearrange("p h d -> p (h d)"))])
            tt("dve", HT.rearrange("p (h d) -> p h d", h=16), HT.rearrange("p (h d) -> p h d", h=16), bc3(CD[:, cs], 64), ALU.mult)
            for g in range(2):
                tt("dve", HT[:, g * 512:(g + 1) * 512], stb[g], HT[:, g * 512:(g + 1) * 512], ALU.add)
            cp("act", HTb, HT)
            yield
            if c > 0:
                tt("dve", Y1, Y1, XSD, ALU.add)
                for g in range(2):
                    tt("dve", Y1[:, g * 512:(g + 1) * 512], yd[g], Y1[:, g * 512:(g + 1) * 512], ALU.add)
            else:
                for g in range(2):
                    tt("dve", Y1[:, g * 512:(g + 1) * 512], yd[g], XSD[:, g * 512:(g + 1) * 512], ALU.add)
            yield
            tt("dve", Y1, Y1, Z, ALU.mult)
            yield
            finish_ssd_tokens(Y1, 128, t0)
            yield

        def finish_ssd_tokens(Y2, n, t0):
            s = SM.next()
            for g in range(2):
                jk = JK2.next()
                act(jk[0:n], Y2[0:n, g * 512:(g + 1) * 512], AF.Square, accum=s[0:n, g:g + 1])
            s2 = SM.next()
            ts("dve", s2[0:n, 0:2], s[0:n, 0:2], 1.0 / (4 * 512), EPS, ALU.mult, ALU.add)
            tt("pool", s2[0:n, 2:4], s2[0:n, 0:2], NEGH[0:n, 0:2], ALU.pow)
            YN = YNr.next()
            ts("dve", s2[0:n, 4:6], s2[0:n, 2:4], 0.5, None, ALU.mult)
            for g in range(2):
                act(YN[0:n, g * 512:(g + 1) * 512], Y2[0:n, g * 512:(g + 1) * 512], AF.Copy, scale=s2[0:n, 4 + g:5 + g])
            bank = pbt().bitcast(BF16).rearrange("p (a b) -> p a b", a=8)
            for fc in range(8):
                tr(bank[:, fc, 0:n], YN[0:n, fc * 128:(fc + 1) * 128], IDB[0:n, 0:n])
            tt("dve", BO[:, :, t0:t0 + n], bank[:, :, 0:n], bc3(VEC[:, V_NSSD:V_NSSD + 8], n), ALU.mult)

        def drain(g):
            for _ in g:
                pass

        def interleave(g1, g2):
            d1 = d2 = False
            while not (d1 and d2):
                if not d1:
                    try:
                        next(g1)
                    except StopIteration:
                        d1 = True
                if not d2:
                    try:
                        next(g2)
                    except StopIteration:
                        d2 = True

        def interleave_all(gens):
            gens = list(gens)
            while gens:
                for g in list(gens):
                    try:
                        next(g)
                    except StopIteration:
                        gens.remove(g)

        drain(conv_gen(0, pb))
        if stage <= 2:
            return finish()
        drain(front(0))
        pb_banks[0] = [4, 5, 6, 7]
        for c in range(NCH):
            if c % 2 == 1 and c + 1 < NCH:
                drain(conv_gen((c + 1) // 2, pb))
            gens = [tail(c)]
            if c + 1 < NCH:
                gens.append(front(c + 1))
            interleave_all(gens)
        pb_banks[0] = list(range(8))
        for f in range(8):
            bank = pb()
            tr(bank[:, 0:128], HT[:, f * 128:(f + 1) * 128], ID)
            o = ESr.next()
            cp("act", o[:, 0, :], bank[:, 0:128])
            dma("sp", ssmp[f * 128:(f + 1) * 128, :], o[:, 0, :])

        if stage <= 3 and stage != 3.5:
            return finish()
        AR.release(mS)
        SB = slice(NT, NTOK)
        EM = AR.alloc([16, 1024], F32)
        dma("sp", EM, consts[0:16, K_E:K_E + 1024])
        SC48 = AR.alloc([48, 1536], F32)
        SCT = AR.alloc([128, 12, 48], F32)
        XBS = AR.alloc([16, 1536], F32)
        XBT = AR.alloc([128, 12, 16], F32)
        XCT = AR.alloc_top([128, 12, 16], F32)
        SMALL = AR.alloc([128, 16, 16], F32)
        DTE = AR.alloc([128, 8, 16], F32); DAE = AR.alloc_top([128, 8, 16], F32); DSE = AR.alloc_top([128, 8], F32)
        DTX = AR.alloc_top([128, 8, 16], F32); YST = AR.alloc_top([128, 8, 16], F32); ZST = AR.alloc_top([128, 8, 16], F32)
        ZSI = AR.alloc_top([128, 8, 16], F32)
        dma("sp", SC48, sconv.rearrange("b k c -> (b k) c"))
        dma("sp", convs[:, 0:2, :], sconv[:, 1:3, :])
        for cc in range(12):
            bank = pb()
            tr(bank[:, 0:48], SC48[:, cc * 128:(cc + 1) * 128], ID[0:48, 0:48])
            cp("act", SCT[:, cc, :], bank[:, 0:48])
        for j in range(3):
            bank = pb()
            mm(bank[0:16], [(XN[:, kc, SB], Wxbc[:, kc, j * 512:(j + 1) * 512]) for kc in range(8)])
            act(XBS[:, j * 512:(j + 1) * 512], bank[0:16], AF.Copy)
        dma("sp", convs[:, 2, :], XBS)
        bank = pb()
        for cc in range(12):
            mm(bank[:, cc * 16:(cc + 1) * 16], [(Wxbc[:, kc, cc * 128:(cc + 1) * 128], XN[:, kc, SB]) for kc in range(8)])
        act(XBT.rearrange("p a b -> p (a b)"), bank[:, 0:192], AF.Copy)
        for cc in range(12):
            cw = VEC[:, V_CW + cc * 4:V_CW + cc * 4 + 4]
            sv = SCT[:, cc, :].rearrange("p (b k) -> p b k", k=3)
            ts("dve", XCT[:, cc, :], XBT[:, cc, :], cw[:, 3:4], VEC[:, V_CB + cc:V_CB + cc + 1], ALU.mult, ALU.add)
            for k in range(3):
                stt("dve", XCT[:, cc, :], sv[:, :, k], cw[:, k:k + 1], XCT[:, cc, :], ALU.mult, ALU.add)
        act(XCT.rearrange("p a b -> p (a b)"), XCT.rearrange("p a b -> p (a b)"), AF.Silu)
        bank = pb()
        mm(bank[0:16, 0:16], [(Wdt[:, kc, :], XN[:, kc, SB]) for kc in range(8)])
        S0 = SMALL[0:16, 0, :]; S1 = SMALL[0:16, 1, :]; S2 = SMALL[0:16, 2, :]; S3 = SMALL[0:16, 3, :]; S4 = SMALL[0:16, 4, :]
        ts("dve", S0, bank[0:16, 0:16], VEC[0:16, V_DTB:V_DTB + 1], None, ALU.add)
        act(S1, S0, AF.Exp)
        ts("dve", S1, S1, 1.0, None, ALU.add)
        act(S2, S1, AF.Ln)
        act(S3[:, 0:1], VEC[0:16, V_ALOG:V_ALOG + 1], AF.Exp)
        ts("dve", S3[:, 1:2], S3[:, 0:1], -1.0, None, ALU.mult)
        ts("dve", S4, S2, S3[:, 1:2], None, ALU.mult)
        act(S4, S4, AF.Exp)
        bank = pb()
        for j in range(8):
            mm(bank[:, j * 16:(j + 1) * 16], [(EM[:, j * 128:(j + 1) * 128], S2)])
            mm(bank[:, 128 + j * 16:128 + (j + 1) * 16], [(EM[:, j * 128:(j + 1) * 128], S4)])
            mm(bank[:, 256 + j:256 + j + 1], [(EM[:, j * 128:(j + 1) * 128], VEC[0:16, V_DSK:V_DSK + 1])])
        act(DTE.rearrange("p a b -> p (a b)"), bank[:, 0:128], AF.Copy)
        act(DAE.rearrange("p a b -> p (a b)"), bank[:, 128:256], AF.Copy)
        act(DSE, bank[:, 256:264], AF.Copy)
        tt("dve", DTX, XCT[:, 0:8, :], DTE, ALU.mult)
        bank = pb()
        for j in range(8):
            mm(bank[:, j * 16:(j + 1) * 16], [(Wz[:, kc, j * 128:(j + 1) * 128], XN[:, kc, SB]) for kc in range(8)])
        act(ZSI.rearrange("p a b -> p (a b)"), bank[:, 0:128], AF.Silu)
        if stage <= 3.5:
            return finish()
        SAMPLE = True
        tilesF = tiles4 if SAMPLE else tiles4[:4]
        tilesT = tiles17 if SAMPLE else tiles17[:16]
        AR.release(m0)
        MACC = AR.alloc([128, 8, NTOK], BF16)
        mW = AR.mark()
        SGr = Rot(AR, [128, 512], F32, 2)
        TMr = Rot(AR, [128, 512], F32, 2)
        JK2 = Rot(AR, [128, 512], BF16, 2)
        mPh = AR.mark()
        Wu = AR.alloc([128, 8, 512], BF16); Wv = AR.alloc([128, 8, 512], BF16)
        mPh2 = AR.mark()

        def branch_out_g(Wbr, nk, SRC, Wg, mode, tiles, bankf=None):
            bankf = bankf or pb
            for (t0, n) in tiles:
                for fc in range(8):
                    yield
                    bb = bankf(); bg = bankf()
                    mm(bb[:, 0:n], [(Wbr[:, kc, fc * 128:(fc + 1) * 128], SRC[:, kc, t0:t0 + n]) for kc in range(nk)])
                    mm(bg[:, 0:n], [(Wg[:, kc, fc * 128:(fc + 1) * 128], XN[:, kc, t0:t0 + n]) for kc in range(8)])
                    sg = SGr.next()
                    act(sg[:, 0:n], bg[:, 0:n], AF.Sigmoid)
                    dst = MACC[:, fc, t0:t0 + n]
                    if mode == "first":
                        tt("dve", dst, bb[:, 0:n], sg[:, 0:n], ALU.mult)
                    else:
                        tm = TMr.next()
                        tt("dve", tm[:, 0:n], bb[:, 0:n], sg[:, 0:n], ALU.mult)
                        if mode == "mid":
                            tt("dve", dst, dst, tm[:, 0:n], ALU.add)
                        else:
                            tt("dve", BO[:, fc, t0:t0 + n], dst, tm[:, 0:n], ALU.add)

        def branch_out(Wbr, nk, SRC, Wg, mode):
            drain(branch_out_g(Wbr, nk, SRC, Wg, mode, tilesF))

        Wbs = AR.alloc([128, 8, 1024], BF16); Wg1 = AR.alloc([128, 8, 1024], BF16)
        load_w(Wbs, w_bs, 1024, 1024)
        load_w(Wg1, w_in[:, C_G + 1024:C_G + 2048], 1024, 1024)
        load_w(Wu, w_in[:, C_U:C_U + 512], 1024, 512); load_w(Wv, w_in[:, C_V:C_V + 512], 1024, 512)
        mSL = AR.mark()
        Hr = Rot(AR, [128, 8, 128], F32, 3); T2r = Rot(AR, [128, 8, 128], F32, 2); DGr = Rot(AR, [128, 4, 128], F32, 2)
        SML2 = AR.alloc([128, 4, 16], F32)

        def sample_state_loop():
            Hs = {}

            def load_state(b):
                Hs[b] = Hr.next()
                dma("sp", Hs[b], sssm[b].rearrange("(j p) n -> p j n", p=128))

            load_state(0)
            for b in range(16):
                if b + 1 < 16:
                    load_state(b + 1)
                DG = DGr.next()
                tt("pool", DG, bc_mid(ID, 4), bc3(XCT[:, 8:12, b], 128), ALU.mult)
                bcb = pb()
                mm(bcb, [(ONES, DG.rearrange("p a b -> p (a b)"))])
                H = Hs.pop(b); T2 = T2r.next()
                yield
                for j in range(8):
                    act(H[:, j, :], H[:, j, :], AF.Copy, scale=DAE[:, j, b:b + 1])
                Bv = bcb[:, 0:256].rearrange("p (g n) -> p g n", g=2).unsqueeze(2).to_broadcast([128, 2, 4, 128])
                Cv = bcb[:, 256:512].rearrange("p (g n) -> p g n", g=2).unsqueeze(2).to_broadcast([128, 2, 4, 128])
                T24 = T2.rearrange("p (g j) n -> p g j n", g=2)
                dxb = DTX[:, :, b].rearrange("p (g j) -> p g j", g=2).unsqueeze(3).to_broadcast([128, 2, 4, 128])
                tt("dve", T24, Bv, dxb, ALU.mult)
                yield
                tt("dve", H, H, T2, ALU.add)
                dma("sp", ssms[b].rearrange("(j p) n -> p j n", p=128), H)
                tt("dve", T24, Cv, H.rearrange("p (g j) n -> p g j n", g=2), ALU.mult)
                red("dve", YST[:, :, b], T2, ALU.add)
                yield

        if SAMPLE:
            def twice(g):
                while True:
                    try:
                        next(g); next(g)
                    except StopIteration:
                        return
                    yield
            interleave(twice(sample_state_loop()), branch_out_g(Wbs, 8, BO, Wg1, "first", tiles4[:4]))
            tt("dve", ZST, XCT[:, 0:8, :], bc3(DSE, 16), ALU.mult)
            tt("dve", YST, YST, ZST, ALU.add)
            tt("dve", YST, YST, ZSI, ALU.mult)
            tt("dve", ZST, YST, YST, ALU.mult)
            bank = pb()
            for g in range(2):
                mm(bank[:, g * 16:(g + 1) * 16], [(ONES, ZST[:, 4 * g + j, :]) for j in range(4)])
            RS2 = SML2[:, 0:2, :]; NH32 = SML2[:, 2:4, :]
            ts("dve", RS2.rearrange("p a b -> p (a b)"), bank[:, 0:32], 1.0 / 512, EPS, ALU.mult, ALU.add)
            memset("pool", NH32, -0.5)
            tt("pool", RS2, RS2, NH32, ALU.pow)
            for g in range(2):
                tt("dve", YST[:, 4 * g:4 * g + 4, :], YST[:, 4 * g:4 * g + 4, :], bc_mid(RS2[:, g, :], 4), ALU.mult)
            tt("dve", BO[:, :, SB], YST, bc3(VEC[:, V_NSSD:V_NSSD + 8], 16), ALU.mult)
            drain(branch_out_g(Wbs, 8, BO, Wg1, "first", tiles4[4:]))
        else:
            branch_out(Wbs, 8, BO, Wg1, "first")
        AR.release(mSL)
        if stage <= 4:
            return finish()
        AR.release(mPh2)
        Wg0 = AR.alloc([128, 8, 1024], BF16); Wbg = AR.alloc([128, 4, 1024], BF16)
        load_w(Wg0, w_in[:, C_G:C_G + 1024], 1024, 1024); load_w(Wbg, w_bg, 512, 1024)
        GM = AR.alloc([128, 4, NTOK], BF16)
        mP3b = AR.mark()
        WST = AR.alloc([128, 4, 128], BF16)
        LNW = AR.alloc([128, 512], F32); LNB = AR.alloc([128, 512], F32); BSB = AR.alloc([128, 512], F32)
        dma("sp", LNW, r_lnw.partition_broadcast(128)); dma("sp", LNB, r_lnb.partition_broadcast(128))
        dma("sp", BSB, r_bs.partition_broadcast(128))
        for g in range(4):
            wt_ = TMr.next()
            dma("sp", wt_[:, 0:128], gm_ws[g])
            tt("pool", wt_[:, 0:128], wt_[:, 0:128], TRIL, ALU.mult)
            bank = pb()
            tr(bank[:, 0:128], wt_[:, 0:128], ID)
            cp("act", WST[:, g, :], bank[:, 0:128])
        UT = AR.alloc([128, 4, 512], F32)
        Vr = Rot(AR, [128, 512], F32, 2); VNr = Rot(AR, [128, 512], F32, 2); VBr = Rot(AR, [128, 512], BF16, 4)
        def p3_uproj(T):
            t0 = T * 512
            for g in range(4):
                bank = pb()
                mm(bank, [(Wu[:, kc, g * 128:(g + 1) * 128], XN[:, kc, t0:t0 + 512]) for kc in range(8)])
                act(UT[:, g, :], bank, AF.Gelu)

        P3V = {}

        def p3_A(c, bk):
            c0 = c * 128
            bank = PS[:, bk, :]
            mm(bank, [(XN[:, kc, c0:c0 + 128], Wv[:, kc, :]) for kc in range(8)])
            yield
            V = Vr.next(); s = SM.next()
            act(V, bank, AF.Gelu, accum=s[:, 0:1])
            jk = JK2.next()
            act(jk, V, AF.Square, accum=s[:, 1:2])
            yield
            s2 = SM.next()
            ts("dve", s2[:, 0:1], s[:, 0:1], 1.0 / 512, None, ALU.mult)
            tt("dve", s2[:, 1:2], s2[:, 0:1], s2[:, 0:1], ALU.mult)
            yield
            stt("dve", s2[:, 2:3], s[:, 1:2], 1.0 / 512, s2[:, 1:2], ALU.mult, ALU.subtract)
            ts("dve", s2[:, 3:4], s2[:, 2:3], EPS, None, ALU.add)
            yield
            tt("pool", s2[:, 4:5], s2[:, 3:4], NEGH[:, 0:1], ALU.pow)
            yield
            VN = VNr.next()
            ts("dve", VN, V, s2[:, 0:1], s2[:, 4:5], ALU.subtract, ALU.mult)
            yield
            tt("dve", VN, VN, LNW, ALU.mult)
            yield
            VB = VBr.next()
            tt("dve", VB, VN, LNB, ALU.add)
            P3V[c] = VB

        def p3_B(c, bk):
            c0 = c * 128; q = c % 4
            VB = P3V.pop(c)
            bank2 = PS[:, bk, :]
            for g in range(4):
                mm(bank2[:, g * 128:(g + 1) * 128], [(VB[:, g * 128:(g + 1) * 128], WST[:, g, :])])
            yield
            tm = TMr.next()
            tt("dve", tm, bank2, BSB, ALU.add)
            yield
            tt("dve", GM[:, :, c0:c0 + 128], tm.rearrange("p (g t) -> p g t", g=4), UT[:, :, q * 128:(q + 1) * 128], ALU.mult)

        pb_banks[0] = [4, 5, 6, 7]
        p3_uproj(0)
        interleave_all([p3_A(0, 0), p3_A(1, 1)])
        for k in range(8):
            c = 2 * k
            gens = [p3_B(c, 2), p3_B(c + 1, 3)]
            if c + 2 < 16:
                gens += [p3_A(c + 2, 0), p3_A(c + 3, 1)]
            interleave_all(gens)
            if (c + 2) % 4 == 0 and c + 2 < 16:
                p3_uproj((c + 2) // 4)
        pb_banks[0] = list(range(8))
        if SAMPLE:
            WS0 = AR.alloc([16, 8], F32)
            dma("sp", WS0[:, 0:4], r_ws00.partition_broadcast(16)); dma("sp", WS0[:, 4:8], r_bs0.partition_broadcast(16))
            bu = pb()
            mm(bu[0:16], [(XN[:, kc, SB], Wu[:, kc, :]) for kc in range(8)])
            US = Vr.next()
            act(US[0:16], bu[0:16], AF.Gelu)
            bv = pb()
            mm(bv[0:16], [(XN[:, kc, SB], Wv[:, kc, :]) for kc in range(8)])
            V = Vr.next(); s = SM.next()
            act(V[0:16], bv[0:16], AF.Gelu, accum=s[0:16, 0:1])
            jk = JK2.next()
            act(jk[0:16], V[0:16], AF.Square, accum=s[0:16, 1:2])
            s2 = SM.next()
            ts("dve", s2[0:16, 0:1], s[0:16, 0:1], 1.0 / 512, None, ALU.mult)
            tt("dve", s2[0:16, 1:2], s2[0:16, 0:1], s2[0:16, 0:1], ALU.mult)
            stt("dve", s2[0:16, 2:3], s[0:16, 1:2], 1.0 / 512, s2[0:16, 1:2], ALU.mult, ALU.subtract)
            ts("dve", s2[0:16, 3:4], s2[0:16, 2:3], EPS, None, ALU.add)
            tt("pool", s2[0:16, 4:5], s2[0:16, 3:4], NEGH[0:16, 0:1], ALU.pow)
            VN = VNr.next()
            ts("dve", VN[0:16], V[0:16], s2[0:16, 0:1], s2[0:16, 4:5], ALU.subtract, ALU.mult)
            tt("pool", VN[0:16], VN[0:16], LNW[0:16], ALU.mult)
            tt("pool", VN[0:16], VN[0:16], LNB[0:16], ALU.add)
            dma("sp", gv, VN[0:16])
            MX = VNr.next()
            for g in range(4):
                ts("dve", MX[0:16, g * 128:(g + 1) * 128], VN[0:16, g * 128:(g + 1) * 128], WS0[:, g:g + 1], WS0[:, 4 + g:5 + g], ALU.mult, ALU.add)
            VB = VBr.next()
            tt("pool", VB[0:16], MX[0:16], US[0:16], ALU.mult)
            bank = pb().bitcast(BF16).rearrange("p (a b) -> p a b", a=8)
            for g in range(4):
                tr(bank[:, g, 0:16], VB[0:16, g * 128:(g + 1) * 128], IDB[0:16, 0:16])
            cp("act", GM[:, :, SB], bank[:, 0:4, 0:16])
        AR.release(mP3b)
        Wmk = AR.alloc([128, 8, 512], BF16); Wmv = AR.alloc([128, 8, 512], BF16)
        load_w(Wmk, w_mk, 1024, 512); load_w(Wmv, w_mv, 1024, 512)
        mP3c = AR.mark()
        branch_out(Wbg, 4, GM, Wg0, "mid")
        if stage <= 5:
            return finish()
        AR.release(mPh)
        XA = AR.alloc([128, 4, NTOK], BF16)
        KT = AR.alloc([128, 4, 256], BF16); VBm = AR.alloc([128, 2, 512], BF16)
        mC = AR.mark()
        assert mC <= mP3b, (mC, mP3b)
        AR.release(mP3c)
        MNT = AR.alloc([128, 8, 256], BF16)
        MEMr = Rot(AR, [128, 1024], F32, 1); XB4 = Rot(AR, [128, 1024], BF16, 1); JK4 = Rot(AR, [128, 1024], BF16, 1)
        for mt in range(2):
            mi = MEMr.next()
            dma("sp", mi, mem[mt * 128:(mt + 1) * 128, :])
            norm_to_T(mi, 128, mt * 128, V_NMEM, MNT, XB4, JK4)
        for mt in range(2):
            for (W_, o_, isv) in ((Wmk, mk, False), (Wmv, mv, True)):
                bank = pb()
                mm(bank, [(MNT[:, kc, mt * 128:(mt + 1) * 128], W_[:, kc, :]) for kc in range(8)])
                ko = TMr.next()
                act(ko, bank, AF.Copy)
                dma("sp", o_[mt * 128:(mt + 1) * 128, :], ko)
                if isv:
                    cp("act", VBm[:, mt, :], ko)
        for h in range(4):
            bank = pb()
            mm(bank[:, 0:256], [(Wmk[:, kc, h * 128:(h + 1) * 128], MNT[:, kc, :]) for kc in range(8)])
            cp("act", KT[:, h, :], bank[:, 0:256])
        AR.release(mC)
        Wq = AR.alloc([128, 8, 512], BF16); Wg2 = AR.alloc([128, 8, 1024], BF16); Wbx = AR.alloc([128, 4, 1024], BF16)
        load_w(Wq, w_in[:, C_Q:C_Q + 512], 1024, 512)
        load_w(Wg2, w_in[:, C_G + 2048:C_G + 3072], 1024, 1024); load_w(Wbx, w_bx, 512, 1024)
        mQ = AR.mark()
        QT = AR.alloc([128, 4, 512], BF16)
        Pr = Rot(AR, [128, 4, 256], BF16, 2); PNr = Rot(AR, [128, 4, 256], BF16, 4); PTr = Rot(AR, [128, 8, 128], BF16, 2)
        def p4_qproj(T):
            t0 = T * 512
            for h in range(4):
                bank = pb()
                mm(bank, [(Wq[:, kc, h * 128:(h + 1) * 128], XN[:, kc, t0:t0 + 512]) for kc in range(8)])
                act(QT[:, h, :], bank, AF.Copy, scale=float(128 ** -0.5))

        P4N = {}

        def p4_A(c, banks):
            q = c % 4
            sc = [PS[:, banks[0], :], PS[:, banks[1], :]]
            for h in range(4):
                mm(sc[h // 2][:, (h % 2) * 256:(h % 2 + 1) * 256], [(QT[:, h, q * 128:(q + 1) * 128], KT[:, h, :])])
            yield
            s = SM.next(); s2 = SM.next(); s3 = SM.next()
            for j in range(2):
                red("dve", s[:, 2 * j:2 * j + 2], sc[j].rearrange("p (h m) -> p h m", h=2), ALU.max)
            yield
            ts("dve", s2[:, 0:4], s[:, 0:4], -1.0, None, ALU.mult)
            yield
            P = Pr.next()
            for h in range(4):
                act(P[:, h, :], sc[h // 2][:, (h % 2) * 256:(h % 2 + 1) * 256], AF.Exp, bias=s2[:, h:h + 1], accum=s3[:, h:h + 1])
            yield
            S.add("dve", lambda e, o=s3[:, 4:8], i=s3[:, 0:4]: e.reciprocal(out=o, in_=i), reads=[s3[:, 0:4]], writes=[s3[:, 4:8]])
            yield
            PN = PNr.next()
            tt("dve", PN, P, bc3(s3[:, 4:8], 256), ALU.mult)
            P4N[c] = PN

        def p4_B(c, bk):
            c0 = c * 128
            PN = P4N.pop(c)
            bank = PS[:, bk, :].bitcast(BF16).rearrange("p (a b) -> p a b", a=8)
            for h in range(4):
                for mh in range(2):
                    tr(bank[:, h * 2 + mh, :], PN[:, h, mh * 128:(mh + 1) * 128], IDB)
            yield
            PT = PTr.next()
            cp("act", PT, bank)
            yield
            bo = PS[:, bk, :]
            for h in range(4):
                mm(bo[:, h * 128:(h + 1) * 128], [(VBm[:, mh, h * 128:(h + 1) * 128], PT[:, h * 2 + mh, :]) for mh in range(2)])
            yield
            cp("act", XA[:, :, c0:c0 + 128], bo.rearrange("p (h t) -> p h t", h=4))

        pb_banks[0] = [6, 7]
        p4_qproj(0)
        interleave_all([p4_A(0, (0, 1)), p4_A(1, (2, 3))])
        for k in range(8):
            c = 2 * k
            gens = [p4_B(c, 4), p4_B(c + 1, 5)]
            if (c + 2) % 4 == 0 and c + 2 < 16:
                p4_qproj((c + 2) // 4)
            if c + 2 < 16:
                gens += [p4_A(c + 2, (0, 1)), p4_A(c + 3, (2, 3))]
            interleave_all(gens)
        pb_banks[0] = list(range(8))
        if SAMPLE:
            sbi = [0]; bbi = [0]

            def sbank():
                b = 4 + sbi[0] % 4; sbi[0] += 1
                return PS[:, b, :]

            def bbank():
                b = bbi[0] % 4; bbi[0] += 1
                return PS[:, b, :]

            def sample_attn():
                AR.release(mQ)
                QS = AR.alloc([16, 512], F32); SCTs = AR.alloc([128, 2, 64], F32); OH4 = AR.alloc([4, 4], BF16)
                PSs = AR.alloc([64, 256], F32); PTs = AR.alloc([128, 2, 64], BF16)
                KSr = Rot(AR, [128, 2, 512], F32, 3); PRr = Rot(AR, [128, 2, 512], F32, 1); SELr = Rot(AR, [16, 128], F32, 2)
                VBsr = Rot(AR, [128, 2, 512], BF16, 2); OBr = Rot(AR, [4, 512], BF16, 2)
                cp("dve", OH4, CONST[0:4, K_OH:K_OH + 4])
                bq = sbank()
                mm(bq[0:16], [(XN[:, kc, SB], Wq[:, kc, :]) for kc in range(8)])
                act(QS, bq[0:16], AF.Copy, scale=float(128 ** -0.5))
                Ks = {}

                def load_k(b):
                    Ks[b] = KSr.next()
                    dma("sp", Ks[b], ck[b].rearrange("(mh m) f -> m mh f", mh=2))

                load_k(0); load_k(1)
                for b in range(16):
                    if b + 2 < 16:
                        load_k(b + 2)
                    K_ = Ks.pop(b)
                    SEL = SELr.next()
                    cp("act", SEL, ID[0:16, b:b + 1].to_broadcast([16, 128]))
                    bqb = sbank()
                    mm(bqb, [(SEL, QS)])
                    PR = PRr.next()
                    tt("dve", PR, K_, bc_mid(bqb, 2), ALU.mult)
                    red("dve", SCTs[:, :, 4 * b:4 * b + 4], PR.rearrange("p a (h d) -> p a h d", h=4), ALU.add)
                    yield
                bs_ = sbank()
                for mh in range(2):
                    tr(bs_[0:64, mh * 128:(mh + 1) * 128], SCTs[:, mh, :], ID)
                sA = SM.next()
                red("dve", sA[0:64, 0:1], bs_[0:64, 0:256], ALU.max)
                ts("dve", sA[0:64, 1:2], sA[0:64, 0:1], -1.0, None, ALU.mult)
                act(PSs, bs_[0:64, 0:256], AF.Exp, bias=sA[0:64, 1:2], accum=sA[0:64, 2:3])
                S.add("dve", lambda e, o=sA[0:64, 3:4], i=sA[0:64, 2:3]: e.reciprocal(out=o, in_=i), reads=[sA[0:64, 2:3]], writes=[sA[0:64, 3:4]])
                ts("dve", PSs, PSs, sA[0:64, 3:4], None, ALU.mult)
                yield
                bt_ = sbank()
                for mh in range(2):
                    tr(bt_[:, mh * 64:(mh + 1) * 64], PSs[:, mh * 128:(mh + 1) * 128], ID[0:64, 0:64])
                cp("act", PTs.rearrange("p a b -> p (a b)"), bt_[:, 0:128])
                Vs = {}

                def load_v(b):
                    V_ = KSr.next()
                    dma("sp", V_, cv[b].rearrange("(mh m) f -> m mh f", mh=2))
                    Vs[b] = VBsr.next()
                    cp("dve", Vs[b], V_)

                load_v(0)
                for b in range(16):
                    if b + 1 < 16:
                        load_v(b + 1)
                    VBs = Vs.pop(b)
                    bo_ = sbank()
                    mm(bo_[0:4], [(PTs[:, mh, 4 * b:4 * b + 4], VBs[:, mh, :]) for mh in range(2)])
                    OB = OBr.next()
                    cp("dve", OB, bo_[0:4])
                    yield
                    bxb = sbank()
                    for h in range(4):
                        mm(bxb[:, h:h + 1], [(OB[:, h * 128:(h + 1) * 128], OH4[:, h:h + 1])])
                    cp("act", XA[:, :, NT + b:NT + b + 1], bxb[:, 0:4].rearrange("p (h o) -> p h o", o=1))
                    yield

            interleave_all([sample_attn(), branch_out_g(Wbx, 4, XA, Wg2, "last", tiles4[:4], bbank)])
            drain(branch_out_g(Wbx, 4, XA, Wg2, "last", tiles4[4:]))
        else:
            branch_out(Wbx, 4, XA, Wg2, "last")
        if stage <= 6:
            return finish()
        AR.release(m0)
        H2 = AR.alloc([128, 17, 1024], F32)
        mF = AR.mark()
        Wo = AR.alloc([128, 8, 1024], BF16)
        load_w(Wo, w_o, 1024, 1024)
        XIN2 = Rot(AR, [128, 1024], F32, 2); XB5 = Rot(AR, [128, 1024], BF16, 2); JK5 = Rot(AR, [128, 1024], BF16, 1)
        def p5_A(ti):
            t0, n = tilesT[ti]
            xi = XIN2.next()
            dma("sp", xi[0:n], x[t0:t0 + n, :] if t0 < NT else xs_in[:, :])
            for hf in range(2):
                bank = pb()
                mm(bank[0:n], [(BO[:, kc, t0:t0 + n], Wo[:, kc, hf * 512:(hf + 1) * 512]) for kc in range(8)])
                tt("dve", H2[0:n, ti, hf * 512:(hf + 1) * 512], bank[0:n], xi[0:n, hf * 512:(hf + 1) * 512], ALU.add)

        p5_A(0)
        for ti, (t0, n) in enumerate(tilesT):
            if ti + 1 < len(tilesT):
                p5_A(ti + 1)
            norm_to_T(H2[0:n, ti, :], n, t0, V_NFFN, XN, XB5, JK5)
        if stage <= 7:
            return finish()
        AR.release(mF)
        Wupr = Rot(AR, [128, 8, 512], BF16, 2); Wdnr = Rot(AR, [128, 4, 1024], BF16, 2)
        RL = Rot(AR, [128, 512], BF16, 2)
        NFW = AR.alloc([128, 1024], F32)
        dma("sp", NFW, r_nfin.partition_broadcast(128))
        YOr = Rot(AR, [128, 1024], F32, 2); JK6 = Rot(AR, [128, 1024], BF16, 1)
        wnext = (Wupr.next(), Wdnr.next())
        load_w(wnext[0], w_up[:, 0:512], 1024, 512); load_w(wnext[1], w_dn[0:512, :], 512, 1024)
        for dc in range(8):
            wu, wd = wnext
            if dc < 7:
                wnext = (Wupr.next(), Wdnr.next())
                load_w(wnext[0], w_up[:, (dc + 1) * 512:(dc + 2) * 512], 1024, 512)
                load_w(wnext[1], w_dn[(dc + 1) * 512:(dc + 2) * 512, :], 512, 1024)
            AT = BO[:, (dc % 2) * 4:(dc % 2) * 4 + 4, :]
            for (t0, n) in tilesF:
                for j in range(4):
                    bank = pb()
                    mm(bank[:, 0:n], [(wu[:, kc, j * 128:(j + 1) * 128], XN[:, kc, t0:t0 + n]) for kc in range(8)])
                    rl = RL.next()
                    act(rl[:, 0:n], bank[:, 0:n], AF.Relu)
                    tt("pool", AT[:, j, t0:t0 + n], rl[:, 0:n], rl[:, 0:n], ALU.mult)
            for ti, (t0, n) in enumerate(tilesT):
                for hf in range(2):
                    bank = pb()
                    mm(bank[0:n], [(AT[:, j, t0:t0 + n], wd[:, j, hf * 512:(hf + 1) * 512]) for j in range(4)])
                    hs = H2[0:n, ti, hf * 512:(hf + 1) * 512]
                    tt("dve", hs, bank[0:n], hs, ALU.add)
        for ti, (t0, n) in enumerate(tilesT):
            s = SM.next(); jk = JK6.next()
            act(jk[0:n], H2[0:n, ti, :], AF.Square, accum=s[0:n, 0:1])
            r = rstd_of(s[0:n, 0:1], n, 1024.0)
            yo = YOr.next()
            stt("dve", yo[0:n], H2[0:n, ti, :], r, NFW[0:n], ALU.mult, ALU.mult)
            dma("sp", y[t0:t0 + n, :] if t0 < NT else ys[:, :], yo[0:n])
        return finish()


def make_consts():
    c = np.zeros((128, K_END), np.float32)
    i = np.arange(128)
    c[:, K_ID:K_ID + 128] = np.eye(128)
    c[:, K_TRI:K_TRI + 128] = (i[:, None] <= i[None, :])
    c[:, K_L:K_L + 128] = (i[:, None] > i[None, :])
    c[:, K_TRIL:K_TRIL + 128] = (i[None, :] <= i[:, None])
    c[:, K_ONES:K_ONES + 128] = 1.0
    for h in range(4):
        c[h, K_OH + h] = 1.0
    for h in range(16):
        c[h, K_E + h * 64:K_E + (h + 1) * 64] = 1.0
    return c


def kernel(x_prompt, x_sample, mem_prompt, cache_mem_k, cache_mem_v, state_conv, state_ssm,
           norm_mix_w, w_in, gm_ln_w, gm_ln_b, gm_ws, gm_bs, conv_w, conv_b, dt_bias, a_log,
           d_skip, ssd_norm_w, mem_norm_w, w_mem_k, w_mem_v, w_br_gm, w_br_ssd, w_br_xa, w_out,
           norm_ffn_w, w_up, w_down, norm_final_w, _stage=99):
    f = lambda a: np.ascontiguousarray(np.asarray(a, dtype=np.float32))
    vec = np.zeros((128, V_END), np.float32)
    pl = lambda v: f(v).reshape(-1, 128).T
    vec[:, V_NMIX:V_NMIX + 8] = pl(norm_mix_w[0]); vec[:, V_NMEM:V_NMEM + 8] = pl(mem_norm_w[0])
    vec[:, V_NSSD:V_NSSD + 8] = pl(ssd_norm_w[0]); vec[:, V_NFFN:V_NFFN + 8] = pl(norm_ffn_w[0])
    cw = f(conv_w[0])
    vec[:, V_CW:V_CW + 48] = cw.reshape(4, 12, 128).transpose(2, 1, 0).reshape(128, 48)
    vec[:, V_CB:V_CB + 12] = pl(conv_b[0])
    vec[0:16, V_DTB] = f(dt_bias[0]); vec[0:16, V_ALOG] = f(a_log[0]); vec[0:16, V_DSK] = f(d_skip[0])
    shared = {
        "w_in": f(w_in[0]), "w_mk": f(w_mem_k[0]), "w_mv": f(w_mem_v[0]), "w_bg": f(w_br_gm[0]),
        "w_bs": f(w_br_ssd[0]), "w_bx": f(w_br_xa[0]), "w_o": f(w_out[0]), "w_up": f(w_up[0]), "w_dn": f(w_down[0]),
        "consts": make_consts(), "vecs": vec, "gm_ws": f(gm_ws[0]),
        "r_lnw": f(gm_ln_w[0]).reshape(1, 512), "r_lnb": f(gm_ln_b[0]).reshape(1, 512), "r_bs": f(gm_bs[0]).reshape(1, 512),
        "r_dtb": f(dt_bias[0]).reshape(1, 16), "r_alog": f(a_log[0]).reshape(1, 16), "r_dsk": f(d_skip[0]).reshape(1, 16),
        "r_nfin": f(norm_final_w).reshape(1, 1024), "r_cw": f(conv_w[0]).reshape(1, 4 * 1536), "r_cb": f(conv_b[0]).reshape(1, 1536),
        "r_ws00": f(gm_ws[0][:, 0, 0]).reshape(1, 4), "r_bs0": f(gm_bs[0][:, 0]).reshape(1, 4),
    }
    in_maps = []
    for c in range(8):
        m = dict(shared)
        sl = slice(16 * c, 16 * c + 16)
        m["x"] = f(x_prompt[c]); m["xs"] = f(x_sample[sl, 0]); m["mem"] = f(mem_prompt[c])
        m["ck"] = f(cache_mem_k[0, sl]).reshape(16, 256, 512); m["cv"] = f(cache_mem_v[0, sl]).reshape(16, 256, 512)
        m["sconv"] = f(state_conv[0, sl]); m["sssm"] = f(state_ssm[0, sl]).reshape(16, 1024, 128)
        in_maps.append(m)
    nc = build(_stage)
    res = run_bass_kernel_spmd(nc, in_maps, core_ids=list(range(8)))
    DEBUG["exec_ns"] = getattr(res, "exec_time_ns", None)
    R = res.results
    g = lambda k: np.stack([np.asarray(R[c][k], dtype=np.float32) for c in range(8)])
    y_prompt = g("y")
    y_sample = g("ys").reshape(128, 1, 1024)
    mem_k = g("mk").reshape(1, 8, 256, 4, 128)
    mem_v = g("mv").reshape(1, 8, 256, 4, 128)
    conv_p = g("convp").reshape(1, 8, 3, 1536)
    ssm_p = g("ssmp").reshape(1, 8, 16, 64, 128)
    conv_s = g("convs").reshape(1, 128, 3, 1536)
    ssm_s = g("ssms").reshape(1, 128, 16, 64, 128)
    gv_s = g("gv").reshape(1, 128, 1, 512)
    return (y_prompt, y_sample, mem_k, mem_v, conv_p, ssm_p, conv_s, ssm_s, gv_s)
```

```python
import numpy as np
from contextlib import ExitStack
import concourse.bass as bass
import concourse.mybir as mybir
from concourse.bass_utils import run_bass_kernel_spmd

F32 = mybir.dt.float32
BF16 = mybir.dt.bfloat16
AF = mybir.ActivationFunctionType
ALU = mybir.AluOpType
AX = mybir.AxisListType
ESZ = {F32: 4, BF16: 2}
PAGE = 256
COMPUTE = ("pe", "act", "dve", "pool")
EPS = 1e-6
NT = 2048
NS = 16
NTOK = NT + NS
NCH = 16


class Op:
    __slots__ = ("eng", "fn", "deps", "signal", "is_dma", "sem", "val", "q")

    def __init__(self, eng, fn, is_dma):
        self.eng = eng; self.fn = fn; self.deps = set(); self.signal = False
        self.is_dma = is_dma; self.sem = None; self.val = 0; self.q = None


class Sched:
    def __init__(self, nc, ndma_sems=24):
        self.nc = nc
        self.ops = {e: [] for e in ("pe", "act", "dve", "pool", "sp")}
        self.last_w = {}
        self.rd = {}
        self.ndma = ndma_sems
        self.dma_count = {"sp": 0, "act": 0, "pool": 0}
        self.dma_ops = []

    @staticmethod
    def pages(ap):
        sp = str(ap.space).upper()
        if "DRAM" in sp or "HBM" in sp:
            return ()
        a = ap.ap
        es = ESZ.get(ap.dtype, 4)
        pstep = a[0][0]
        off = int(ap.offset)
        lo = off % pstep if pstep > 0 else off
        hi = lo
        for st, cnt in a[1:]:
            hi += st * (cnt - 1)
        hi += 1
        name = ap.tensor.name
        pg = 2048 if name == "ps" else PAGE
        return [(name, p) for p in range((lo * es) // pg, (hi * es - 1) // pg + 1)]

    def add(self, eng, fn, reads=(), writes=(), dma=False):
        op = Op(eng, fn, dma)
        deps = set()
        rpages = []
        for ap in reads:
            rpages.extend(self.pages(ap))
        wpages = []
        for ap in writes:
            wpages.extend(self.pages(ap))
        for pg in rpages:
            w = self.last_w.get(pg)
            if w is not None:
                if not (w.eng == eng and eng == "pe" and not w.is_dma and not dma):
                    deps.add(w)
        for pg in wpages:
            w = self.last_w.get(pg)
            if w is not None:
                if w.is_dma or dma or w.eng != eng or eng != "pe":
                    deps.add(w)
            r = self.rd.get(pg)
            if r:
                for o in r.values():
                    if o.is_dma or dma or o.eng != eng or eng != "pe":
                        deps.add(o)
        for pg in rpages:
            r = self.rd.get(pg)
            if r is None:
                r = {}; self.rd[pg] = r
            if dma:
                r[("dma", id(op))] = op
            else:
                r[eng] = op
        for pg in wpages:
            self.last_w[pg] = op
            self.rd[pg] = {}
        deps.discard(op)
        op.deps = deps
        for d in deps:
            d.signal = True
        self.ops[eng].append(op)
        if dma:
            k = self.dma_count[eng]
            self.dma_count[eng] += 1
            op.q = (eng, k)
            self.dma_ops.append(op)
        return op

    def emit(self, stack):
        nc = self.nc
        sems = {e: stack.enter_context(nc.semaphore("s_" + e)) for e in COMPUTE}
        dsems = {}
        for q, cnt in self.dma_count.items():
            if cnt:
                dsems[q] = [stack.enter_context(nc.semaphore("d_%s_%d" % (q, i))) for i in range(min(cnt, self.ndma))]
        for e in COMPUTE:
            c = 0
            for op in self.ops[e]:
                if op.is_dma:
                    continue
                if op.signal:
                    c += 1
                    op.sem = sems[e]; op.val = c
        for op in self.dma_ops:
            q, k = op.q
            op.sem = dsems[q][k % self.ndma]
            op.val = 16 * (k // self.ndma + 1)
        final_waits = {}
        for op in self.dma_ops:
            final_waits[op.sem] = max(final_waits.get(op.sem, 0), op.val)

        def run(name, e):
            seen = {}
            for op in self.ops[name]:
                waits = {}
                for d in op.deps:
                    if waits.get(d.sem, 0) < d.val:
                        waits[d.sem] = d.val
                if op.is_dma:
                    q, k = op.q
                    if k >= self.ndma:
                        v = op.val - 16
                        if waits.get(op.sem, 0) < v:
                            waits[op.sem] = v
                for s, v in waits.items():
                    if seen.get(s, 0) < v:
                        e.wait_ge(s, v)
                        seen[s] = v
                inst = op.fn(e)
                if op.is_dma:
                    inst.then_inc(op.sem, 16)
                elif op.signal:
                    inst.then_inc(op.sem, 1)
            if name == "sp":
                for s, v in final_waits.items():
                    if seen.get(s, 0) < v:
                        e.wait_ge(s, v)

        block = stack.enter_context(nc.Block())

        @block.sync
        def _(e):
            run("sp", e)

        @block.scalar
        def _(e):
            run("act", e)

        @block.vector
        def _(e):
            run("dve", e)

        @block.gpsimd
        def _(e):
            run("pool", e)

        @block.tensor
        def _(e):
            run("pe", e)


class Arena:
    def __init__(self, tens, nbytes):
        self.t = tens; self.n = nbytes; self.off = 0; self.peak = 0

    def alloc(self, shape, dt):
        n = int(np.prod(shape[1:])) * ESZ[dt]
        ap = self.t[:, self.off // 4:(self.off + n) // 4]
        if dt != F32:
            ap = ap.bitcast(dt)
        if len(shape) == 3:
            ap = ap.rearrange("p (a b) -> p a b", a=shape[1])
        elif len(shape) == 4:
            ap = ap.rearrange("p (a b c) -> p a b c", a=shape[1], b=shape[2])
        if shape[0] < 128:
            ap = ap[0:shape[0]]
        self.off += (n + PAGE - 1) // PAGE * PAGE
        self.peak = max(self.peak, self.off)
        assert self.off <= self.n, ("arena overflow", self.off, self.n)
        return ap

    def alloc_top(self, shape, dt):
        n = int(np.prod(shape[1:])) * ESZ[dt]
        self.n -= (n + PAGE - 1) // PAGE * PAGE
        assert self.n >= self.peak, ("arena top overflow", self.n, self.peak)
        ap = self.t[:, self.n // 4:(self.n + n) // 4]
        if dt != F32:
            ap = ap.bitcast(dt)
        if len(shape) == 3:
            ap = ap.rearrange("p (a b) -> p a b", a=shape[1])
        return ap

    def mark(self):
        return self.off

    def release(self, m):
        self.off = m


class Rot:
    def __init__(self, ar, shape, dt, n):
        self.t = [ar.alloc(shape, dt) for _ in range(n)]; self.i = 0

    def next(self):
        r = self.t[self.i % len(self.t)]; self.i += 1
        return r


C_U, C_V, C_Z, C_XBC, C_DT, C_Q, C_G = 0, 512, 1024, 2048, 3584, 3600, 4112
K_ID, K_TRI, K_L, K_TRIL, K_ONES, K_OH, K_E, K_END = 0, 128, 256, 384, 512, 640, 644, 644 + 1024
V_NMIX, V_NMEM, V_NSSD, V_NFFN, V_CW, V_CB, V_DTB, V_ALOG, V_DSK, V_END = 0, 8, 16, 24, 32, 80, 92, 93, 94, 95

DEBUG = {}


def build(stage=99):
    nc = bass.Bass("TRN2", target_bir_lowering=False)

    def din(name, shape):
        return nc.dram_tensor(name, list(shape), F32, kind="ExternalInput").ap()

    def dout(name, shape):
        return nc.dram_tensor(name, list(shape), F32, kind="ExternalOutput").ap()

    x = din("x", [NT, 1024]); xs_in = din("xs", [NS, 1024]); mem = din("mem", [256, 1024])
    ck = din("ck", [NS, 256, 512]); cv = din("cv", [NS, 256, 512])
    sconv = din("sconv", [NS, 3, 1536]); sssm = din("sssm", [NS, 1024, 128])
    w_in = din("w_in", [1024, 7184]); w_mk = din("w_mk", [1024, 512]); w_mv = din("w_mv", [1024, 512])
    w_bg = din("w_bg", [512, 1024]); w_bs = din("w_bs", [1024, 1024]); w_bx = din("w_bx", [512, 1024])
    w_o = din("w_o", [1024, 1024]); w_up = din("w_up", [1024, 4096]); w_dn = din("w_dn", [4096, 1024])
    consts = din("consts", [128, K_END]); vecs = din("vecs", [128, V_END])
    gm_ws = din("gm_ws", [4, 128, 128])
    r_lnw = din("r_lnw", [1, 512]); r_lnb = din("r_lnb", [1, 512]); r_bs = din("r_bs", [1, 512])
    r_dtb = din("r_dtb", [1, 16]); r_alog = din("r_alog", [1, 16]); r_dsk = din("r_dsk", [1, 16])
    r_nfin = din("r_nfin", [1, 1024]); r_cw = din("r_cw", [1, 4 * 1536]); r_cb = din("r_cb", [1, 1536])
    r_ws00 = din("r_ws00", [1, 4]); r_bs0 = din("r_bs0", [1, 4])

    y = dout("y", [NT, 1024]); ys = dout("ys", [NS, 1024]); mk = dout("mk", [256, 512]); mv = dout("mv", [256, 512])
    convp = dout("convp", [3, 1536]); ssmp = dout("ssmp", [1024, 128])
    convs = dout("convs", [NS, 3, 1536]); ssms = dout("ssms", [NS, 1024, 128]); gv = dout("gv", [NS, 512])

    st = ExitStack()
    def finish():
        S.emit(st)
        st.close()
        DEBUG['peak'] = AR.peak
        return nc
    if True:
        NBY = 207 * 1024
        At = st.enter_context(nc.sbuf_tensor("arena", [128, NBY // 4], F32))
        PS = st.enter_context(nc.psum_tensor("ps", [128, 8, 512], F32))
        S = Sched(nc)
        AR = Arena(At, NBY)
        pbi = [0]

        pb_banks = [list(range(8))]

        def pb():
            bl = pb_banks[0]
            b = bl[pbi[0] % len(bl)]; pbi[0] += 1
            return PS[:, b, :]

        def isap(v):
            return not isinstance(v, (int, float)) and v is not None

        def act(out, in_, func, bias=None, scale=None, accum=None):
            rd = [in_] + [v for v in (bias, scale) if isap(v)]
            wr = [out] + ([accum] if accum is not None else [])
            kw = {}
            if bias is not None: kw["bias"] = bias
            if scale is not None: kw["scale"] = scale
            if accum is not None: kw["accum_out"] = accum
            S.add("act", lambda e: e.activation(out=out, in_=in_, func=func, **kw), reads=rd, writes=wr)

        def tt(eng, out, in0, in1, op):
            S.add(eng, lambda e: e.tensor_tensor(out=out, in0=in0, in1=in1, op=op), reads=[in0, in1], writes=[out])

        def ts(eng, out, in0, s1, s2, op0, op1=None):
            rd = [in0] + [v for v in (s1, s2) if isap(v)]
            if op1 is None:
                S.add(eng, lambda e: e.tensor_scalar(out=out, in0=in0, scalar1=s1, scalar2=None, op0=op0), reads=rd, writes=[out])
            else:
                S.add(eng, lambda e: e.tensor_scalar(out=out, in0=in0, scalar1=s1, scalar2=s2, op0=op0, op1=op1), reads=rd, writes=[out])

        def stt(eng, out, in0, sc, in1, op0, op1):
            rd = [in0, in1] + ([sc] if isap(sc) else [])
            S.add(eng, lambda e: e.scalar_tensor_tensor(out=out, in0=in0, scalar=sc, in1=in1, op0=op0, op1=op1), reads=rd, writes=[out])

        def cp(eng, out, in_):
            if eng == "act":
                act(out, in_, AF.Copy)
            else:
                S.add(eng, lambda e: e.tensor_copy(out=out, in_=in_), reads=[in_], writes=[out])

        def red(eng, out, in_, op):
            S.add(eng, lambda e: e.tensor_reduce(out=out, in_=in_, axis=AX.X, op=op), reads=[in_], writes=[out])

        def memset(eng, out, val):
            S.add(eng, lambda e: e.memset(out, val), writes=[out])

        def mm(out, pairs):
            def fn(e):
                n = len(pairs); ins = None
                for i, (l, r) in enumerate(pairs):
                    ins = e.matmul(out, l, r, start=(i == 0), stop=(i == n - 1))
                return ins
            S.add("pe", fn, reads=[a for p in pairs for a in p], writes=[out])

        def tr(out, in_, ident):
            S.add("pe", lambda e: e.transpose(out, in_, ident), reads=[in_, ident], writes=[out])

        def dma(q, out, in_, slow=False):
            if slow:
                S.add(q, lambda e: e.dma_start(out=out, in_=in_, allow_slow_non_contiguous=True), reads=[in_], writes=[out], dma=True)
            else:
                S.add(q, lambda e: e.dma_start(out=out, in_=in_), reads=[in_], writes=[out], dma=True)

        def load_w(dst, src, K, N):
            srcv = src.rearrange("(k p) n -> p k n", p=128)
            for c0 in range(0, N, 2048):
                c1 = min(N, c0 + 2048)
                dma("pool", dst[:, :, c0:c1], srcv[:, :, c0:c1])

        def bc3(ap, n):
            return ap.unsqueeze(2).to_broadcast([ap.shape[0], ap.shape[1], n])

        def bc_mid(ap, n):
            return ap.unsqueeze(1).to_broadcast([ap.shape[0], n, ap.shape[1]])

        XN = AR.alloc([128, 8, NTOK], BF16)
        BO = AR.alloc([128, 8, NTOK], BF16)
        CONST = AR.alloc([128, K_E], F32)
        VEC = AR.alloc([128, V_END], F32)
        IDB = AR.alloc([128, 128], BF16)
        NEGH = AR.alloc([128, 8], F32)
        SM = Rot(AR, [128, 8], F32, 12)
        ID = CONST[:, K_ID:K_ID + 128]; TRI = CONST[:, K_TRI:K_TRI + 128]; LST = CONST[:, K_L:K_L + 128]
        TRIL = CONST[:, K_TRIL:K_TRIL + 128]; ONES = CONST[:, K_ONES:K_ONES + 128]
        dma("sp", CONST, consts[:, 0:K_E]); dma("sp", VEC, vecs)
        cp("dve", IDB, ID)
        CB16 = AR.alloc([128, 3, 128], BF16)
        TRIb = CB16[:, 0, :]; ONESb = CB16[:, 1, :]; LSTb = CB16[:, 2, :]
        cp("dve", TRIb, TRI); cp("dve", ONESb, ONES); cp("dve", LSTb, LST)
        memset("pool", NEGH, -0.5)

        tiles4 = [(i * 512, 512) for i in range(4)] + [(NT, NS)]
        tiles17 = [(i * 128, 128) for i in range(16)] + [(NT, NS)]

        def rstd_of(ss, n, count, eps_scale=1.0):
            s = SM.next()
            ts("dve", s[0:n, 0:1], ss, 1.0 / count, EPS, ALU.mult, ALU.add)
            tt("pool", s[0:n, 1:2], s[0:n, 0:1], NEGH[0:n, 0:1], ALU.pow)
            return s[0:n, 1:2]

        def norm_to_T(src, n, t0, nwcol, dstT, XB, JK):
            s = SM.next()
            jk = JK.next()
            act(jk[0:n], src, AF.Square, accum=s[0:n, 0:1])
            r = rstd_of(s[0:n, 0:1], n, 1024.0)
            xb = XB.next()
            ts("dve", xb[0:n], src, r, None, ALU.mult)
            bank = pb().bitcast(BF16).rearrange("p (a b) -> p a b", a=8)
            for kc in range(8):
                tr(bank[:, kc, 0:n], xb[0:n, kc * 128:(kc + 1) * 128], IDB[0:n, 0:n])
            tt("dve", dstT[:, :, t0:t0 + n], bank[:, :, 0:n], bc3(VEC[:, nwcol:nwcol + 8], n), ALU.mult)

        m0 = AR.mark()
        XIN = Rot(AR, [128, 1024], F32, 3)
        XBr = Rot(AR, [128, 1024], BF16, 2)
        JKr = Rot(AR, [128, 1024], BF16, 1)
        mP = AR.mark()
        for (t0, n) in tiles17:
            xi = XIN.next()
            dma("sp", xi[0:n], x[t0:t0 + n, :] if t0 < NT else xs_in[:, :])
            norm_to_T(xi[0:n], n, t0, V_NMIX, XN, XBr, JKr)

        if stage <= 0:
            return finish()
        AR.release(m0)
        Wxbc = AR.alloc([128, 8, 1536], BF16); Wz = AR.alloc([128, 8, 1024], BF16); Wdt = AR.alloc([128, 8, 16], BF16)
        load_w(Wxbc, w_in[:, C_XBC:C_XBC + 1536], 1024, 1536)
        load_w(Wdt, w_in[:, C_DT:C_DT + 16], 1024, 16)
        load_w(Wz, w_in[:, C_Z:C_Z + 1024], 1024, 1024)
        if stage <= 0.5:
            return finish()
        RB = AR.alloc([128, 64], F32)
        DTB = RB[:, 0:16]; ALOG = RB[:, 16:32]; DSK = RB[:, 32:48]; ANEG = RB[:, 48:64]
        dma("sp", DTB, r_dtb.partition_broadcast(128)); dma("sp", ALOG, r_alog.partition_broadcast(128))
        dma("sp", DSK, r_dsk.partition_broadcast(128))
        act(ANEG, ALOG, AF.Exp)
        ts("dve", ANEG, ANEG, -1.0, None, ALU.mult)
        if stage <= 0.7:
            return finish()
        PRE_S = AR.alloc([128, 8, 256], F32)
        DT = PRE_S[:, 0, :]; AA = PRE_S[:, 1, :]; EAC = PRE_S[:, 2, :]; CD = PRE_S[:, 3, :]
        DEND = PRE_S[:, 4, :]; DTD = PRE_S[:, 5, :]; T6 = PRE_S[:, 6, :]; T7 = PRE_S[:, 7, :]
        bk = pb()
        for c in range(NCH):
            mm(bk[:, c * 16:(c + 1) * 16], [(XN[:, kc, c * 128:(c + 1) * 128], Wdt[:, kc, :]) for kc in range(8)])
        v3 = lambda ap: ap.rearrange("p (c h) -> p c h", c=16)
        tt("dve", v3(T6), v3(bk[:, 0:256]), bc_mid(DTB, 16), ALU.add)
        act(T7, T6, AF.Exp)
        ts("dve", T7, T7, 1.0, None, ALU.add)
        act(DT, T7, AF.Ln)
        if stage <= 0.8:
            return finish()
        tt("dve", v3(AA), v3(DT), bc_mid(ANEG, 16), ALU.mult)
        AHL = AR.alloc([128, 2, 256], BF16)
        AAh = AHL[:, 0, :]; AAl = AHL[:, 1, :]
        cp("dve", AAh, AA)
        tt("dve", T6, AA, AAh, ALU.subtract)
        cp("dve", AAl, T6)
        AALf = AR.alloc([128, 256], F32)
        cp("dve", AALf, AAl)
        bk2 = pb()
        mm(bk2[:, 0:256], [(TRIb, AAh), (TRIb, AAl)])
        mm(bk2[:, 256:512], [(ONESb, AAh), (ONESb, AAl)])
        act(EAC, bk2[:, 0:256], AF.Exp)
        act(CD, bk2[:, 256:512], AF.Exp)
        act(T6, bk2[:, 0:256], AF.Copy)
        act(T7, bk2[:, 256:512], AF.Copy)
        tt("dve", T7, T7, T6, ALU.subtract)
        act(DEND, T7, AF.Exp)
        tt("dve", DTD, DT, DEND, ALU.mult)

        if stage <= 1:
            return finish()
        mS = AR.mark()
        HALO = AR.alloc([128, 12, 4], F32)
        memset("pool", HALO, 0.0)
        HW = 256
        PREc = Rot(AR, [128, HW + 4], F32, 2)
        XC = AR.alloc([128, 12, HW], F32)
        BCT2 = [AR.alloc([128, 4, HW], BF16), AR.alloc([128, 4, HW], BF16)]
        CTr = Rot(AR, [128, HW], F32, 1)
        XSr = Rot(AR, [128, 1024], F32, 2)
        XDTr = Rot(AR, [128, 16, 64], BF16, 2)
        XDTDr = Rot(AR, [128, 16, 64], BF16, 2)
        BTKr = Rot(AR, [128, 2, 128], BF16, 2)
        CBMr = Rot(AR, [128, 2, 128], F32, 2)
        RHr = Rot(AR, [128, 2, 4, 128], BF16, 2)
        ESr = Rot(AR, [128, 4, 128], F32, 2)
        MTr = Rot(AR, [128, 16, 128], BF16, 1)
        Y1r = Rot(AR, [128, 1024], F32, 1)
        XSDr = Rot(AR, [128, 1024], F32, 1)
        Zr = Rot(AR, [128, 1024], F32, 2)
        THr = Rot(AR, [128, 1024], F32, 1)
        YNr = Rot(AR, [128, 1024], BF16, 1)
        HT = AR.alloc([128, 1024], F32)
        HTb = AR.alloc([128, 1024], BF16)
        memset("pool", HT, 0.0)
        memset("pool", HTb, 0.0)
        JK2 = Rot(AR, [128, 512], BF16, 2)
        def conv_gen(T, bankf):
            t0 = T * HW
            for cc in range(12):
                if cc:
                    yield
                bank = bankf()
                mm(bank[:, 0:HW], [(Wxbc[:, kc, cc * 128:(cc + 1) * 128], XN[:, kc, t0:t0 + HW]) for kc in range(8)])
                pre = PREc.next()
                act(pre[:, 0:3], HALO[:, cc, 0:3], AF.Copy)
                act(pre[:, 3:HW + 3], bank[:, 0:HW], AF.Copy)
                cw = VEC[:, V_CW + cc * 4:V_CW + cc * 4 + 4]
                act(XC[:, cc, :], pre[:, 0:HW], AF.Identity, bias=VEC[:, V_CB + cc:V_CB + cc + 1], scale=cw[:, 0:1])
                for k in range(1, 4):
                    if True:
                        stt("dve", XC[:, cc, :], pre[:, k:k + HW], cw[:, k:k + 1], XC[:, cc, :], ALU.mult, ALU.add)
                    else:
                        ct = CTr.next()
                        ts("pool", ct, pre[:, k:k + HW], cw[:, k:k + 1], None, ALU.mult)
                        tt("pool", XC[:, cc, :], XC[:, cc, :], ct, ALU.add)
                act(HALO[:, cc, 0:3], pre[:, HW:HW + 3], AF.Copy)
                if t0 + HW == NT:
                    dma("sp", convp[:, cc * 128:(cc + 1) * 128].rearrange("k c -> c k"), pre[:, HW:HW + 3], slow=True)
            yield
            for cc in range(12):
                act(XC[:, cc, :], XC[:, cc, :], AF.Silu)
            for j in range(4):
                cp("act", BCT2[T % 2][:, j, :], XC[:, 8 + j, :])

        CH = {}
        pfi = [0]; pti = [0]

        def pbf():
            b = 6 + pfi[0] % 2; pfi[0] += 1
            return PS[:, b, :]

        def pbt():
            b = 4 + pti[0] % 2; pti[0] += 1
            return PS[:, b, :]

        def front(c):
            q = c % 2
            t0 = c * 128
            cs = slice(c * 16, (c + 1) * 16)
            qs = slice(q * 128, (q + 1) * 128)
            d = {}
            CH[c] = d
            BCT = BCT2[(c // 2) % 2]
            d["BCT"] = BCT
            XS = XSr.next()
            b0 = pbf(); b1 = pbf()
            for f in range(8):
                bank = b0 if f < 4 else b1
                tr(bank[:, (f % 4) * 128:(f % 4 + 1) * 128], XC[:, f, qs], ID)
            yield
            act(XS[:, 0:512], b0, AF.Copy)
            act(XS[:, 512:1024], b1, AF.Copy)
            b2 = pbf()
            for g in range(2):
                tr(b2[:, g * 128:(g + 1) * 128], XC[:, 8 + g, qs], ID)
            yield
            BTK = BTKr.next()
            cp("dve", BTK, b2[:, 0:256].rearrange("p (g n) -> p g n", g=2))
            yield
            XS3 = XS.rearrange("p (h d) -> p h d", h=16)
            XDT = XDTr.next(); XDTD = XDTDr.next()
            tt("dve", XDT, XS3, bc3(DT[:, cs], 64), ALU.mult)
            tt("pool", XDTD, XS3, bc3(DTD[:, cs], 64), ALU.mult)
            d.update(XS=XS, XS3=XS3, BTK=BTK, XDTD=XDTD, qs=qs, cs=cs, t0=t0)
            yield
            b3 = pbf()
            for g in range(2):
                mm(b3[:, g * 128:(g + 1) * 128], [(BCT[:, g, qs], BCT[:, 2 + g, qs])])
            CBM = CBMr.next()
            tt("dve", CBM, b3[:, 0:256].rearrange("p (g n) -> p g n", g=2), bc_mid(TRI, 2), ALU.mult)
            yield
            Z = Zr.next(); TH = THr.next()
            for hf in range(2):
                bz = pbf()
                mm(bz, [(XN[:, kc, t0:t0 + 128], Wz[:, kc, hf * 512:(hf + 1) * 512]) for kc in range(8)])
                act(Z[:, hf * 512:(hf + 1) * 512], bz, AF.Copy)
                act(TH[:, hf * 512:(hf + 1) * 512], bz, AF.Tanh, scale=0.5)
                yield
            stt("dve", Z, TH, 1.0, Z, ALU.add, ALU.mult)
            d["Z"] = Z
            yield
            MT = MTr.next()
            yd = [PS[:, 2 * (c % 2), :], PS[:, 2 * (c % 2) + 1, :]]
            d["yd"] = yd
            def mk_rh(hq):
                RH = RHr.next()
                hsl = slice(c * 16 + hq * 4, c * 16 + hq * 4 + 4)
                tt("dve", RH[:, 0], bc_mid(TRIb, 4), bc3(AAh[:, hsl], 128), ALU.mult)
                for hh in range(4):
                    hcol = c * 16 + hq * 4 + hh
                    act(RH[:, 1, hh, :], TRIb, AF.Copy, scale=AALf[:, hcol:hcol + 1])
                return RH

            def seg_mm(RH):
                bs = pbf()
                mm(bs, [(LSTb, RH[:, 0].rearrange("p a b -> p (a b)")), (LSTb, RH[:, 1].rearrange("p a b -> p (a b)"))])
                return bs

            segs = {}
            for hq in range(2):
                segs[hq] = seg_mm(mk_rh(hq))
                yield
            for hq in range(4):
                g = hq // 2
                ES = ESr.next()
                act(ES.rearrange("p a b -> p (a b)"), segs.pop(hq), AF.Exp)
                tt("dve", MT[:, hq * 4:hq * 4 + 4, :], ES, bc_mid(CBM[:, g, :], 4), ALU.mult)
                yield
                if hq + 2 < 4:
                    segs[hq + 2] = seg_mm(mk_rh(hq + 2))
                for hh in range(4):
                    h = hq * 4 + hh
                    mm(yd[h // 8][:, (h % 8) * 64:(h % 8 + 1) * 64], [(MT[:, h, :], XDT[:, h, :])])
                yield

        def tail(c):
            d = CH.pop(c)
            XS3 = d["XS3"]; BTK = d["BTK"]; XDTD = d["XDTD"]; qs = d["qs"]; cs = d["cs"]; t0 = d["t0"]; yd = d["yd"]; Z = d["Z"]
            BCT = d["BCT"]
            XSD = XSDr.next()
            tt("pool", XSD.rearrange("p (h d) -> p h d", h=16), XS3, bc3(DSK, 64), ALU.mult)
            Y1 = Y1r.next()
            if c > 0:
                yo = [pbt(), pbt()]
                for g in range(2):
                    mm(yo[g], [(BCT[:, 2 + g, qs], HTb[:, g * 512:(g + 1) * 512])])
                for g in range(2):
                    tt("dve", Y1[:, g * 512:(g + 1) * 512].rearrange("p (h d) -> p h d", h=8),
                       yo[g].rearrange("p (h d) -> p h d", h=8), bc3(EAC[:, c * 16 + g * 8:c * 16 + g * 8 + 8], 64), ALU.mult)
            yield
            stb = [pbt(), pbt()]
            for g in range(2):
                mm(stb[g], [(BTK[:, g, :], XDTD[:, g * 8:(g + 1) * 8, :].rearrange("p h d -> p (h d)"))])
            tt("dve", HT.rearrange("p (h d) -> p h d", h=16), HT.rearrange("p (h d) -> p h d", h=16), bc3(CD[:, cs], 64), ALU.mult)
            for g in range(2):
                tt("dve", HT[:, g * 512:(g + 1) * 512], stb[g], HT[:, g * 512:(g + 1) * 512], ALU.add)
            cp("act", HTb, HT)
            yield
            if c > 0:
                tt("dve", Y1, Y1, XSD, ALU.add)
                for g in range(2):
                    tt("dve", Y1[:, g * 512:(g + 1) * 512], yd[g], Y1[:, g * 512:(g + 1) * 512], ALU.add)
            else:
                for g in range(2):
                    tt("dve", Y1[:, g * 512:(g + 1) * 512], yd[g], XSD[:, g * 512:(g + 1) * 512], ALU.add)
            yield
            tt("dve", Y1, Y1, Z, ALU.mult)
            yield
            finish_ssd_tokens(Y1, 128, t0)
            yield

        def finish_ssd_tokens(Y2, n, t0):
            s = SM.next()
            for g in range(2):
                jk = JK2.next()
                act(jk[0:n], Y2[0:n, g * 512:(g + 1) * 512], AF.Square, accum=s[0:n, g:g + 1])
            s2 = SM.next()
            ts("dve", s2[0:n, 0:2], s[0:n, 0:2], 1.0 / (4 * 512), EPS, ALU.mult, ALU.add)
            tt("pool", s2[0:n, 2:4], s2[0:n, 0:2], NEGH[0:n, 0:2], ALU.pow)
            YN = YNr.next()
            ts("dve", s2[0:n, 4:6], s2[0:n, 2:4], 0.5, None, ALU.mult)
            for g in range(2):
                act(YN[0:n, g * 512:(g + 1) * 512], Y2[0:n, g * 512:(g + 1) * 512], AF.Copy, scale=s2[0:n, 4 + g:5 + g])
            bank = pbt().bitcast(BF16).rearrange("p (a b) -> p a b", a=8)
            for fc in range(8):
                tr(bank[:, fc, 0:n], YN[0:n, fc * 128:(fc + 1) * 128], IDB[0:n, 0:n])
            tt("dve", BO[:, :, t0:t0 + n], bank[:, :, 0:n], bc3(VEC[:, V_NSSD:V_NSSD + 8], n), ALU.mult)

        def drain(g):
            for _ in g:
                pass

        def interleave(g1, g2):
            d1 = d2 = False
            while not (d1 and d2):
                if not d1:
                    try:
                        next(g1)
                    except StopIteration:
                        d1 = True
                if not d2:
                    try:
                        next(g2)
                    except StopIteration:
                        d2 = True

        def interleave_all(gens):
            gens = list(gens)
            while gens:
                for g in list(gens):
                    try:
                        next(g)
                    except StopIteration:
                        gens.remove(g)

        drain(conv_gen(0, pb))
        if stage <= 2:
            return finish()
        drain(front(0))
        pb_banks[0] = [4, 5, 6, 7]
        for c in range(NCH):
            if c % 2 == 1 and c + 1 < NCH:
                drain(conv_gen((c + 1) // 2, pb))
            gens = [tail(c)]
            if c + 1 < NCH:
                gens.append(front(c + 1))
            interleave_all(gens)
        pb_banks[0] = list(range(8))
        for f in range(8):
            bank = pb()
            tr(bank[:, 0:128], HT[:, f * 128:(f + 1) * 128], ID)
            o = ESr.next()
            cp("act", o[:, 0, :], bank[:, 0:128])
            dma("sp", ssmp[f * 128:(f + 1) * 128, :], o[:, 0, :])

        if stage <= 3 and stage != 3.5:
            return finish()
        AR.release(mS)
        SB = slice(NT, NTOK)
        EM = AR.alloc([16, 1024], F32)
        dma("sp", EM, consts[0:16, K_E:K_E + 1024])
        SC48 = AR.alloc([48, 1536], F32)
        SCT = AR.alloc([128, 12, 48], F32)
        XBS = AR.alloc([16, 1536], F32)
        XBT = AR.alloc([128, 12, 16], F32)
        XCT = AR.alloc_top([128, 12, 16], F32)
        SMALL = AR.alloc([128, 16, 16], F32)
        DTE = AR.alloc([128, 8, 16], F32); DAE = AR.alloc_top([128, 8, 16], F32); DSE = AR.alloc_top([128, 8], F32)
        DTX = AR.alloc_top([128, 8, 16], F32); YST = AR.alloc_top([128, 8, 16], F32); ZST = AR.alloc_top([128, 8, 16], F32)
        ZSI = AR.alloc_top([128, 8, 16], F32)
        dma("sp", SC48, sconv.rearrange("b k c -> (b k) c"))
        dma("sp", convs[:, 0:2, :], sconv[:, 1:3, :])
        for cc in range(12):
            bank = pb()
            tr(bank[:, 0:48], SC48[:, cc * 128:(cc + 1) * 128], ID[0:48, 0:48])
            cp("act", SCT[:, cc, :], bank[:, 0:48])
        for j in range(3):
            bank = pb()
            mm(bank[0:16], [(XN[:, kc, SB], Wxbc[:, kc, j * 512:(j + 1) * 512]) for kc in range(8)])
            act(XBS[:, j * 512:(j + 1) * 512], bank[0:16], AF.Copy)
        dma("sp", convs[:, 2, :], XBS)
        bank = pb()
        for cc in range(12):
            mm(bank[:, cc * 16:(cc + 1) * 16], [(Wxbc[:, kc, cc * 128:(cc + 1) * 128], XN[:, kc, SB]) for kc in range(8)])
        act(XBT.rearrange("p a b -> p (a b)"), bank[:, 0:192], AF.Copy)
        for cc in range(12):
            cw = VEC[:, V_CW + cc * 4:V_CW + cc * 4 + 4]
            sv = SCT[:, cc, :].rearrange("p (b k) -> p b k", k=3)
            ts("dve", XCT[:, cc, :], XBT[:, cc, :], cw[:, 3:4], VEC[:, V_CB + cc:V_CB + cc + 1], ALU.mult, ALU.add)
            for k in range(3):
                stt("dve", XCT[:, cc, :], sv[:, :, k], cw[:, k:k + 1], XCT[:, cc, :], ALU.mult, ALU.add)
        act(XCT.rearrange("p a b -> p (a b)"), XCT.rearrange("p a b -> p (a b)"), AF.Silu)
        bank = pb()
        mm(bank[0:16, 0:16], [(Wdt[:, kc, :], XN[:, kc, SB]) for kc in range(8)])
        S0 = SMALL[0:16, 0, :]; S1 = SMALL[0:16, 1, :]; S2 = SMALL[0:16, 2, :]; S3 = SMALL[0:16, 3, :]; S4 = SMALL[0:16, 4, :]
        ts("dve", S0, bank[0:16, 0:16], VEC[0:16, V_DTB:V_DTB + 1], None, ALU.add)
        act(S1, S0, AF.Exp)
        ts("dve", S1, S1, 1.0, None, ALU.add)
        act(S2, S1, AF.Ln)
        act(S3[:, 0:1], VEC[0:16, V_ALOG:V_ALOG + 1], AF.Exp)
        ts("dve", S3[:, 1:2], S3[:, 0:1], -1.0, None, ALU.mult)
        ts("dve", S4, S2, S3[:, 1:2], None, ALU.mult)
        act(S4, S4, AF.Exp)
        bank = pb()
        for j in range(8):
            mm(bank[:, j * 16:(j + 1) * 16], [(EM[:, j * 128:(j + 1) * 128], S2)])
            mm(bank[:, 128 + j * 16:128 + (j + 1) * 16], [(EM[:, j * 128:(j + 1) * 128], S4)])
            mm(bank[:, 256 + j:256 + j + 1], [(EM[:, j * 128:(j + 1) * 128], VEC[0:16, V_DSK:V_DSK + 1])])
        act(DTE.rearrange("p a b -> p (a b)"), bank[:, 0:128], AF.Copy)
        act(DAE.rearrange("p a b -> p (a b)"), bank[:, 128:256], AF.Copy)
        act(DSE, bank[:, 256:264], AF.Copy)
        tt("dve", DTX, XCT[:, 0:8, :], DTE, ALU.mult)
        bank = pb()
        for j in range(8):
            mm(bank[:, j * 16:(j + 1) * 16], [(Wz[:, kc, j * 128:(j + 1) * 128], XN[:, kc, SB]) for kc in range(8)])
        act(ZSI.rearrange("p a b -> p (a b)"), bank[:, 0:128], AF.Silu)
        if stage <= 3.5:
            return finish()
        SAMPLE = True
        tilesF = tiles4 if SAMPLE else tiles4[:4]
        tilesT = tiles17 if SAMPLE else tiles17[:16]
        AR.release(m0)
        MACC = AR.alloc([128, 8, NTOK], BF16)
        mW = AR.mark()
        SGr = Rot(AR, [128, 512], F32, 2)
        TMr = Rot(AR, [128, 512], F32, 2)
        JK2 = Rot(AR, [128, 512], BF16, 2)
        mPh = AR.mark()
        Wu = AR.alloc([128, 8, 512], BF16); Wv = AR.alloc([128, 8, 512], BF16)
        mPh2 = AR.mark()

        def branch_out_g(Wbr, nk, SRC, Wg, mode, tiles, bankf=None):
            bankf = bankf or pb
            for (t0, n) in tiles:
                for fc in range(8):
                    yield
                    bb = bankf(); bg = bankf()
                    mm(bb[:, 0:n], [(Wbr[:, kc, fc * 128:(fc + 1) * 128], SRC[:, kc, t0:t0 + n]) for kc in range(nk)])
                    mm(bg[:, 0:n], [(Wg[:, kc, fc * 128:(fc + 1) * 128], XN[:, kc, t0:t0 + n]) for kc in range(8)])
                    sg = SGr.next()
                    act(sg[:, 0:n], bg[:, 0:n], AF.Sigmoid)
                    dst = MACC[:, fc, t0:t0 + n]
                    if mode == "first":
                        tt("dve", dst, bb[:, 0:n], sg[:, 0:n], ALU.mult)
                    else:
                        tm = TMr.next()
                        tt("dve", tm[:, 0:n], bb[:, 0:n], sg[:, 0:n], ALU.mult)
                        if mode == "mid":
                            tt("dve", dst, dst, tm[:, 0:n], ALU.add)
                        else:
                            tt("dve", BO[:, fc, t0:t0 + n], dst, tm[:, 0:n], ALU.add)

        def branch_out(Wbr, nk, SRC, Wg, mode):
            drain(branch_out_g(Wbr, nk, SRC, Wg, mode, tilesF))

        Wbs = AR.alloc([128, 8, 1024], BF16); Wg1 = AR.alloc([128, 8, 1024], BF16)
        load_w(Wbs, w_bs, 1024, 1024)
        load_w(Wg1, w_in[:, C_G + 1024:C_G + 2048], 1024, 1024)
        load_w(Wu, w_in[:, C_U:C_U + 512], 1024, 512); load_w(Wv, w_in[:, C_V:C_V + 512], 1024, 512)
        mSL = AR.mark()
        Hr = Rot(AR, [128, 8, 128], F32, 3); T2r = Rot(AR, [128, 8, 128], F32, 2); DGr = Rot(AR, [128, 4, 128], F32, 2)
        SML2 = AR.alloc([128, 4, 16], F32)

        def sample_state_loop():
            Hs = {}

            def load_state(b):
                Hs[b] = Hr.next()
                dma("sp", Hs[b], sssm[b].rearrange("(j p) n -> p j n", p=128))

            load_state(0)
            for b in range(16):
                if b + 1 < 16:
                    load_state(b + 1)
                DG = DGr.next()
                tt("pool", DG, bc_mid(ID, 4), bc3(XCT[:, 8:12, b], 128), ALU.mult)
                bcb = pb()
                mm(bcb, [(ONES, DG.rearrange("p a b -> p (a b)"))])
                H = Hs.pop(b); T2 = T2r.next()
                yield
                for j in range(8):
                    act(H[:, j, :], H[:, j, :], AF.Copy, scale=DAE[:, j, b:b + 1])
                Bv = bcb[:, 0:256].rearrange("p (g n) -> p g n", g=2).unsqueeze(2).to_broadcast([128, 2, 4, 128])
                Cv = bcb[:, 256:512].rearrange("p (g n) -> p g n", g=2).unsqueeze(2).to_broadcast([128, 2, 4, 128])
                T24 = T2.rearrange("p (g j) n -> p g j n", g=2)
                dxb = DTX[:, :, b].rearrange("p (g j) -> p g j", g=2).unsqueeze(3).to_broadcast([128, 2, 4, 128])
                tt("dve", T24, Bv, dxb, ALU.mult)
                yield
                tt("dve", H, H, T2, ALU.add)
                dma("sp", ssms[b].rearrange("(j p) n -> p j n", p=128), H)
                tt("dve", T24, Cv, H.rearrange("p (g j) n -> p g j n", g=2), ALU.mult)
                red("dve", YST[:, :, b], T2, ALU.add)
                yield

        if SAMPLE:
            def twice(g):
                while True:
                    try:
                        next(g); next(g)
                    except StopIteration:
                        return
                    yield
            interleave(twice(sample_state_loop()), branch_out_g(Wbs, 8, BO, Wg1, "first", tiles4[:4]))
            tt("dve", ZST, XCT[:, 0:8, :], bc3(DSE, 16), ALU.mult)
            tt("dve", YST, YST, ZST, ALU.add)
            tt("dve", YST, YST, ZSI, ALU.mult)
            tt("dve", ZST, YST, YST, ALU.mult)
            bank = pb()
            for g in range(2):
                mm(bank[:, g * 16:(g + 1) * 16], [(ONES, ZST[:, 4 * g + j, :]) for j in range(4)])
            RS2 = SML2[:, 0:2, :]; NH32 = SML2[:, 2:4, :]
            ts("dve", RS2.rearrange("p a b -> p (a b)"), bank[:, 0:32], 1.0 / 512, EPS, ALU.mult, ALU.add)
            memset("pool", NH32, -0.5)
            tt("pool", RS2, RS2, NH32, ALU.pow)
            for g in range(2):
                tt("dve", YST[:, 4 * g:4 * g + 4, :], YST[:, 4 * g:4 * g + 4, :], bc_mid(RS2[:, g, :], 4), ALU.mult)
            tt("dve", BO[:, :, SB], YST, bc3(VEC[:, V_NSSD:V_NSSD + 8], 16), ALU.mult)
            drain(branch_out_g(Wbs, 8, BO, Wg1, "first", tiles4[4:]))
        else:
            branch_out(Wbs, 8, BO, Wg1, "first")
        AR.release(mSL)
        if stage <= 4:
            return finish()
        AR.release(mPh2)
        Wg0 = AR.alloc([128, 8, 1024], BF16); Wbg = AR.alloc([128, 4, 1024], BF16)
        load_w(Wg0, w_in[:, C_G:C_G + 1024], 1024, 1024); load_w(Wbg, w_bg, 512, 1024)
        GM = AR.alloc([128, 4, NTOK], BF16)
        mP3b = AR.mark()
        WST = AR.alloc([128, 4, 128], BF16)
        LNW = AR.alloc([128, 512], F32); LNB = AR.alloc([128, 512], F32); BSB = AR.alloc([128, 512], F32)
        dma("sp", LNW, r_lnw.partition_broadcast(128)); dma("sp", LNB, r_lnb.partition_broadcast(128))
        dma("sp", BSB, r_bs.partition_broadcast(128))
        for g in range(4):
            wt_ = TMr.next()
            dma("sp", wt_[:, 0:128], gm_ws[g])
            tt("pool", wt_[:, 0:128], wt_[:, 0:128], TRIL, ALU.mult)
            bank = pb()
            tr(bank[:, 0:128], wt_[:, 0:128], ID)
            cp("act", WST[:, g, :], bank[:, 0:128])
        UT = AR.alloc([128, 4, 512], F32)
        Vr = Rot(AR, [128, 512], F32, 2); VNr = Rot(AR, [128, 512], F32, 2); VBr = Rot(AR, [128, 512], BF16, 4)
        def p3_uproj(T):
            t0 = T * 512
            for g in range(4):
                bank = pb()
                mm(bank, [(Wu[:, kc, g * 128:(g + 1) * 128], XN[:, kc, t0:t0 + 512]) for kc in range(8)])
                act(UT[:, g, :], bank, AF.Gelu)

        P3V = {}

        def p3_A(c, bk):
            c0 = c * 128
            bank = PS[:, bk, :]
            mm(bank, [(XN[:, kc, c0:c0 + 128], Wv[:, kc, :]) for kc in range(8)])
            yield
            V = Vr.next(); s = SM.next()
            act(V, bank, AF.Gelu, accum=s[:, 0:1])
            jk = JK2.next()
            act(jk, V, AF.Square, accum=s[:, 1:2])
            yield
            s2 = SM.next()
            ts("dve", s2[:, 0:1], s[:, 0:1], 1.0 / 512, None, ALU.mult)
            tt("dve", s2[:, 1:2], s2[:, 0:1], s2[:, 0:1], ALU.mult)
            yield
            stt("dve", s2[:, 2:3], s[:, 1:2], 1.0 / 512, s2[:, 1:2], ALU.mult, ALU.subtract)
            ts("dve", s2[:, 3:4], s2[:, 2:3], EPS, None, ALU.add)
            yield
            tt("pool", s2[:, 4:5], s2[:, 3:4], NEGH[:, 0:1], ALU.pow)
            yield
            VN = VNr.next()
            ts("dve", VN, V, s2[:, 0:1], s2[:, 4:5], ALU.subtract, ALU.mult)
            yield
            tt("dve", VN, VN, LNW, ALU.mult)
            yield
            VB = VBr.next()
            tt("dve", VB, VN, LNB, ALU.add)
            P3V[c] = VB

        def p3_B(c, bk):
            c0 = c * 128; q = c % 4
            VB = P3V.pop(c)
            bank2 = PS[:, bk, :]
            for g in range(4):
                mm(bank2[:, g * 128:(g + 1) * 128], [(VB[:, g * 128:(g + 1) * 128], WST[:, g, :])])
            yield
            tm = TMr.next()
            tt("dve", tm, bank2, BSB, ALU.add)
            yield
            tt("dve", GM[:, :, c0:c0 + 128], tm.rearrange("p (g t) -> p g t", g=4), UT[:, :, q * 128:(q + 1) * 128], ALU.mult)

        pb_banks[0] = [4, 5, 6, 7]
        p3_uproj(0)
        interleave_all([p3_A(0, 0), p3_A(1, 1)])
        for k in range(8):
            c = 2 * k
            gens = [p3_B(c, 2), p3_B(c + 1, 3)]
            if c + 2 < 16:
                gens += [p3_A(c + 2, 0), p3_A(c + 3, 1)]
            interleave_all(gens)
            if (c + 2) % 4 == 0 and c + 2 < 16:
                p3_uproj((c + 2) // 4)
        pb_banks[0] = list(range(8))
        if SAMPLE:
            WS0 = AR.alloc([16, 8], F32)
            dma("sp", WS0[:, 0:4], r_ws00.partition_broadcast(16)); dma("sp", WS0[:, 4:8], r_bs0.partition_broadcast(16))
            bu = pb()
            mm(bu[0:16], [(XN[:, kc, SB], Wu[:, kc, :]) for kc in range(8)])
            US = Vr.next()
            act(US[0:16], bu[0:16], AF.Gelu)
            bv = pb()
            mm(bv[0:16], [(XN[:, kc, SB], Wv[:, kc, :]) for kc in range(8)])
            V = Vr.next(); s = SM.next()
            act(V[0:16], bv[0:16], AF.Gelu, accum=s[0:16, 0:1])
            jk = JK2.next()
            act(jk[0:16], V[0:16], AF.Square, accum=s[0:16, 1:2])
            s2 = SM.next()
            ts("dve", s2[0:16, 0:1], s[0:16, 0:1], 1.0 / 512, None, ALU.mult)
            tt("dve", s2[0:16, 1:2], s2[0:16, 0:1], s2[0:16, 0:1], ALU.mult)
            stt("dve", s2[0:16, 2:3], s[0:16, 1:2], 1.0 / 512, s2[0:16, 1:2], ALU.mult, ALU.subtract)
            ts("dve", s2[0:16, 3:4], s2[0:16, 2:3], EPS, None, ALU.add)
            tt("pool", s2[0:16, 4:5], s2[0:16, 3:4], NEGH[0:16, 0:1], ALU.pow)
            VN = VNr.next()
            ts("dve", VN[0:16], V[0:16], s2[0:16, 0:1], s2[0:16, 4:5], ALU.subtract, ALU.mult)
            tt("pool", VN[0:16], VN[0:16], LNW[0:16], ALU.mult)
            tt("pool", VN[0:16], VN[0:16], LNB[0:16], ALU.add)
            dma("sp", gv, VN[0:16])
            MX = VNr.next()
            for g in range(4):
                ts("dve", MX[0:16, g * 128:(g + 1) * 128], VN[0:16, g * 128:(g + 1) * 128], WS0[:, g:g + 1], WS0[:, 4 + g:5 + g], ALU.mult, ALU.add)
            VB = VBr.next()
            tt("pool", VB[0:16], MX[0:16], US[0:16], ALU.mult)
            bank = pb().bitcast(BF16).rearrange("p (a b) -> p a b", a=8)
            for g in range(4):
                tr(bank[:, g, 0:16], VB[0:16, g * 128:(g + 1) * 128], IDB[0:16, 0:16])
            cp("act", GM[:, :, SB], bank[:, 0:4, 0:16])
        AR.release(mP3b)
        Wmk = AR.alloc([128, 8, 512], BF16); Wmv = AR.alloc([128, 8, 512], BF16)
        load_w(Wmk, w_mk, 1024, 512); load_w(Wmv, w_mv, 1024, 512)
        mP3c = AR.mark()
        branch_out(Wbg, 4, GM, Wg0, "mid")
        if stage <= 5:
            return finish()
        AR.release(mPh)
        XA = AR.alloc([128, 4, NTOK], BF16)
        KT = AR.alloc([128, 4, 256], BF16); VBm = AR.alloc([128, 2, 512], BF16)
        mC = AR.mark()
        assert mC <= mP3b, (mC, mP3b)
        AR.release(mP3c)
        MNT = AR.alloc([128, 8, 256], BF16)
        MEMr = Rot(AR, [128, 1024], F32, 1); XB4 = Rot(AR, [128, 1024], BF16, 1); JK4 = Rot(AR, [128, 1024], BF16, 1)
        for mt in range(2):
            mi = MEMr.next()
            dma("sp", mi, mem[mt * 128:(mt + 1) * 128, :])
            norm_to_T(mi, 128, mt * 128, V_NMEM, MNT, XB4, JK4)
        for mt in range(2):
            for (W_, o_, isv) in ((Wmk, mk, False), (Wmv, mv, True)):
                bank = pb()
                mm(bank, [(MNT[:, kc, mt * 128:(mt + 1) * 128], W_[:, kc, :]) for kc in range(8)])
                ko = TMr.next()
                act(ko, bank, AF.Copy)
                dma("sp", o_[mt * 128:(mt + 1) * 128, :], ko)
                if isv:
                    cp("act", VBm[:, mt, :], ko)
        for h in range(4):
            bank = pb()
            mm(bank[:, 0:256], [(Wmk[:, kc, h * 128:(h + 1) * 128], MNT[:, kc, :]) for kc in range(8)])
            cp("act", KT[:, h, :], bank[:, 0:256])
        AR.release(mC)
        Wq = AR.alloc([128, 8, 512], BF16); Wg2 = AR.alloc([128, 8, 1024], BF16); Wbx = AR.alloc([128, 4, 1024], BF16)
        load_w(Wq, w_in[:, C_Q:C_Q + 512], 1024, 512)
        load_w(Wg2, w_in[:, C_G + 2048:C_G + 3072], 1024, 1024); load_w(Wbx, w_bx, 512, 1024)
        mQ = AR.mark()
        QT = AR.alloc([128, 4, 512], BF16)
        Pr = Rot(AR, [128, 4, 256], BF16, 2); PNr = Rot(AR, [128, 4, 256], BF16, 4); PTr = Rot(AR, [128, 8, 128], BF16, 2)
        def p4_qproj(T):
            t0 = T * 512
            for h in range(4):
                bank = pb()
                mm(bank, [(Wq[:, kc, h * 128:(h + 1) * 128], XN[:, kc, t0:t0 + 512]) for kc in range(8)])
                act(QT[:, h, :], bank, AF.Copy, scale=float(128 ** -0.5))

        P4N = {}

        def p4_A(c, banks):
            q = c % 4
            sc = [PS[:, banks[0], :], PS[:, banks[1], :]]
            for h in range(4):
                mm(sc[h // 2][:, (h % 2) * 256:(h % 2 + 1) * 256], [(QT[:, h, q * 128:(q + 1) * 128], KT[:, h, :])])
            yield
            s = SM.next(); s2 = SM.next(); s3 = SM.next()
            for j in range(2):
                red("dve", s[:, 2 * j:2 * j + 2], sc[j].rearrange("p (h m) -> p h m", h=2), ALU.max)
            yield
            ts("dve", s2[:, 0:4], s[:, 0:4], -1.0, None, ALU.mult)
            yield
            P = Pr.next()
            for h in range(4):
                act(P[:, h, :], sc[h // 2][:, (h % 2) * 256:(h % 2 + 1) * 256], AF.Exp, bias=s2[:, h:h + 1], accum=s3[:, h:h + 1])
            yield
            S.add("dve", lambda e, o=s3[:, 4:8], i=s3[:, 0:4]: e.reciprocal(out=o, in_=i), reads=[s3[:, 0:4]], writes=[s3[:, 4:8]])
            yield
            PN = PNr.next()
            tt("dve", PN, P, bc3(s3[:, 4:8], 256), ALU.mult)
            P4N[c] = PN

        def p4_B(c, bk):
            c0 = c * 128
            PN = P4N.pop(c)
            bank = PS[:, bk, :].bitcast(BF16).rearrange("p (a b) -> p a b", a=8)
            for h in range(4):
                for mh in range(2):
                    tr(bank[:, h * 2 + mh, :], PN[:, h, mh * 128:(mh + 1) * 128], IDB)
            yield
            PT = PTr.next()
            cp("act", PT, bank)
            yield
            bo = PS[:, bk, :]
            for h in range(4):
                mm(bo[:, h * 128:(h + 1) * 128], [(VBm[:, mh, h * 128:(h + 1) * 128], PT[:, h * 2 + mh, :]) for mh in range(2)])
            yield
            cp("act", XA[:, :, c0:c0 + 128], bo.rearrange("p (h t) -> p h t", h=4))

        pb_banks[0] = [6, 7]
        p4_qproj(0)
        interleave_all([p4_A(0, (0, 1)), p4_A(1, (2, 3))])
        for k in range(8):
            c = 2 * k
            gens = [p4_B(c, 4), p4_B(c + 1, 5)]
            if (c + 2) % 4 == 0 and c + 2 < 16:
                p4_qproj((c + 2) // 4)
            if c + 2 < 16:
                gens += [p4_A(c + 2, (0, 1)), p4_A(c + 3, (2, 3))]
            interleave_all(gens)
        pb_banks[0] = list(range(8))
        if SAMPLE:
            sbi = [0]; bbi = [0]

            def sbank():
                b = 4 + sbi[0] % 4; sbi[0] += 1
                return PS[:, b, :]

            def bbank():
                b = bbi[0] % 4; bbi[0] += 1
                return PS[:, b, :]

            def sample_attn():
                AR.release(mQ)
                QS = AR.alloc([16, 512], F32); SCTs = AR.alloc([128, 2, 64], F32); OH4 = AR.alloc([4, 4], BF16)
                PSs = AR.alloc([64, 256], F32); PTs = AR.alloc([128, 2, 64], BF16)
                KSr = Rot(AR, [128, 2, 512], F32, 3); PRr = Rot(AR, [128, 2, 512], F32, 1); SELr = Rot(AR, [16, 128], F32, 2)
                VBsr = Rot(AR, [128, 2, 512], BF16, 2); OBr = Rot(AR, [4, 512], BF16, 2)
                cp("dve", OH4, CONST[0:4, K_OH:K_OH + 4])
                bq = sbank()
                mm(bq[0:16], [(XN[:, kc, SB], Wq[:, kc, :]) for kc in range(8)])
                act(QS, bq[0:16], AF.Copy, scale=float(128 ** -0.5))
                Ks = {}

                def load_k(b):
                    Ks[b] = KSr.next()
                    dma("sp", Ks[b], ck[b].rearrange("(mh m) f -> m mh f", mh=2))

                load_k(0); load_k(1)
                for b in range(16):
                    if b + 2 < 16:
                        load_k(b + 2)
                    K_ = Ks.pop(b)
                    SEL = SELr.next()
                    cp("act", SEL, ID[0:16, b:b + 1].to_broadcast([16, 128]))
                    bqb = sbank()
                    mm(bqb, [(SEL, QS)])
                    PR = PRr.next()
                    tt("dve", PR, K_, bc_mid(bqb, 2), ALU.mult)
                    red("dve", SCTs[:, :, 4 * b:4 * b + 4], PR.rearrange("p a (h d) -> p a h d", h=4), ALU.add)
                    yield
                bs_ = sbank()
                for mh in range(2):
                    tr(bs_[0:64, mh * 128:(mh + 1) * 128], SCTs[:, mh, :], ID)
                sA = SM.next()
                red("dve", sA[0:64, 0:1], bs_[0:64, 0:256], ALU.max)
                ts("dve", sA[0:64, 1:2], sA[0:64, 0:1], -1.0, None, ALU.mult)
                act(PSs, bs_[0:64, 0:256], AF.Exp, bias=sA[0:64, 1:2], accum=sA[0:64, 2:3])
                S.add("dve", lambda e, o=sA[0:64, 3:4], i=sA[0:64, 2:3]: e.reciprocal(out=o, in_=i), reads=[sA[0:64, 2:3]], writes=[sA[0:64, 3:4]])
                ts("dve", PSs, PSs, sA[0:64, 3:4], None, ALU.mult)
                yield
                bt_ = sbank()
                for mh in range(2):
                    tr(bt_[:, mh * 64:(mh + 1) * 64], PSs[:, mh * 128:(mh + 1) * 128], ID[0:64, 0:64])
                cp("act", PTs.rearrange("p a b -> p (a b)"), bt_[:, 0:128])
                Vs = {}

                def load_v(b):
                    V_ = KSr.next()
                    dma("sp", V_, cv[b].rearrange("(mh m) f -> m mh f", mh=2))
                    Vs[b] = VBsr.next()
                    cp("dve", Vs[b], V_)

                load_v(0)
                for b in range(16):
                    if b + 1 < 16:
                        load_v(b + 1)
                    VBs = Vs.pop(b)
                    bo_ = sbank()
                    mm(bo_[0:4], [(PTs[:, mh, 4 * b:4 * b + 4], VBs[:, mh, :]) for mh in range(2)])
                    OB = OBr.next()
                    cp("dve", OB, bo_[0:4])
                    yield
                    bxb = sbank()
                    for h in range(4):
                        mm(bxb[:, h:h + 1], [(OB[:, h * 128:(h + 1) * 128], OH4[:, h:h + 1])])
                    cp("act", XA[:, :, NT + b:NT + b + 1], bxb[:, 0:4].rearrange("p (h o) -> p h o", o=1))
                    yield

            interleave_all([sample_attn(), branch_out_g(Wbx, 4, XA, Wg2, "last", tiles4[:4], bbank)])
            drain(branch_out_g(Wbx, 4, XA, Wg2, "last", tiles4[4:]))
        else:
            branch_out(Wbx, 4, XA, Wg2, "last")
        if stage <= 6:
            return finish()
        AR.release(m0)
        H2 = AR.alloc([128, 17, 1024], F32)
        mF = AR.mark()
        Wo = AR.alloc([128, 8, 1024], BF16)
        load_w(Wo, w_o, 1024, 1024)
        XIN2 = Rot(AR, [128, 1024], F32, 2); XB5 = Rot(AR, [128, 1024], BF16, 2); JK5 = Rot(AR, [128, 1024], BF16, 1)
        def p5_A(ti):
            t0, n = tilesT[ti]
            xi = XIN2.next()
            dma("sp", xi[0:n], x[t0:t0 + n, :] if t0 < NT else xs_in[:, :])
            for hf in range(2):
                bank = pb()
                mm(bank[0:n], [(BO[:, kc, t0:t0 + n], Wo[:, kc, hf * 512:(hf + 1) * 512]) for kc in range(8)])
                tt("dve", H2[0:n, ti, hf * 512:(hf + 1) * 512], bank[0:n], xi[0:n, hf * 512:(hf + 1) * 512], ALU.add)

        p5_A(0)
        for ti, (t0, n) in enumerate(tilesT):
            if ti + 1 < len(tilesT):
                p5_A(ti + 1)
            norm_to_T(H2[0:n, ti, :], n, t0, V_NFFN, XN, XB5, JK5)
        if stage <= 7:
            return finish()
        AR.release(mF)
        Wupr = Rot(AR, [128, 8, 512], BF16, 2); Wdnr = Rot(AR, [128, 4, 1024], BF16, 2)
        RL = Rot(AR, [128, 512], BF16, 2)
        NFW = AR.alloc([128, 1024], F32)
        dma("sp", NFW, r_nfin.partition_broadcast(128))
        YOr = Rot(AR, [128, 1024], F32, 2); JK6 = Rot(AR, [128, 1024], BF16, 1)
        FLAG = 2
        FIN = {}

        def fin_A(ti):
            t0, n = tilesT[ti]
            s = SM.next(); jk = JK6.next()
            act(jk[0:n], H2[0:n, ti, :], AF.Square, accum=s[0:n, 0:1])
            FIN[ti] = rstd_of(s[0:n, 0:1], n, 1024.0)

        def fin_B(ti):
            t0, n = tilesT[ti]
            r = FIN.pop(ti)
            yo = YOr.next()
            act(yo[0:n], H2[0:n, ti, :], AF.Copy, scale=r)
            tt("pool", yo[0:n], yo[0:n], NFW[0:n], ALU.mult)
            dma("sp", y[t0:t0 + n, :] if t0 < NT else ys[:, :], yo[0:n])

        wnext = (Wupr.next(), Wdnr.next())
        load_w(wnext[0], w_up[:, 0:512], 1024, 512); load_w(wnext[1], w_dn[0:512, :], 512, 1024)
        for dc in range(8):
            wu, wd = wnext
            if dc < 7:
                wnext = (Wupr.next(), Wdnr.next())
                load_w(wnext[0], w_up[:, (dc + 1) * 512:(dc + 2) * 512], 1024, 512)
                load_w(wnext[1], w_dn[(dc + 1) * 512:(dc + 2) * 512, :], 512, 1024)
            AT = BO[:, (dc % 2) * 4:(dc % 2) * 4 + 4, :]
            for (t0, n) in tilesF:
                for j in range(4):
                    bank = pb()
                    mm(bank[:, 0:n], [(wu[:, kc, j * 128:(j + 1) * 128], XN[:, kc, t0:t0 + n]) for kc in range(8)])
                    rl = RL.next()
                    act(rl[:, 0:n], bank[:, 0:n], AF.Relu)
                    tt("pool", AT[:, j, t0:t0 + n], rl[:, 0:n], rl[:, 0:n], ALU.mult)
            for ti, (t0, n) in enumerate(tilesT):
                for hf in range(2):
                    bank = pb()
                    mm(bank[0:n], [(AT[:, j, t0:t0 + n], wd[:, j, hf * 512:(hf + 1) * 512]) for j in range(4)])
                    hs = H2[0:n, ti, hf * 512:(hf + 1) * 512]
                    tt("dve", hs, bank[0:n], hs, ALU.add)
                if dc == 7:
                    fin_A(ti)
                    if ti >= FLAG:
                        fin_B(ti - FLAG)
        for ti in range(len(tilesT) - FLAG, len(tilesT)):
            fin_B(ti)
        return finish()


def make_consts():
    c = np.zeros((128, K_END), np.float32)
    i = np.arange(128)
    c[:, K_ID:K_ID + 128] = np.eye(128)
    c[:, K_TRI:K_TRI + 128] = (i[:, None] <= i[None, :])
    c[:, K_L:K_L + 128] = (i[:, None] > i[None, :])
    c[:, K_TRIL:K_TRIL + 128] = (i[None, :] <= i[:, None])
    c[:, K_ONES:K_ONES + 128] = 1.0
    for h in range(4):
        c[h, K_OH + h] = 1.0
    for h in range(16):
        c[h, K_E + h * 64:K_E + (h + 1) * 64] = 1.0
    return c


def kernel(x_prompt, x_sample, mem_prompt, cache_mem_k, cache_mem_v, state_conv, state_ssm,
           norm_mix_w, w_in, gm_ln_w, gm_ln_b, gm_ws, gm_bs, conv_w, conv_b, dt_bias, a_log,
           d_skip, ssd_norm_w, mem_norm_w, w_mem_k, w_mem_v, w_br_gm, w_br_ssd, w_br_xa, w_out,
           norm_ffn_w, w_up, w_down, norm_final_w, _stage=99):
    f = lambda a: np.ascontiguousarray(np.asarray(a, dtype=np.float32))
    vec = np.zeros((128, V_END), np.float32)
    pl = lambda v: f(v).reshape(-1, 128).T
    vec[:, V_NMIX:V_NMIX + 8] = pl(norm_mix_w[0]); vec[:, V_NMEM:V_NMEM + 8] = pl(mem_norm_w[0])
    vec[:, V_NSSD:V_NSSD + 8] = pl(ssd_norm_w[0]); vec[:, V_NFFN:V_NFFN + 8] = pl(norm_ffn_w[0])
    cw = f(conv_w[0])
    vec[:, V_CW:V_CW + 48] = cw.reshape(4, 12, 128).transpose(2, 1, 0).reshape(128, 48)
    vec[:, V_CB:V_CB + 12] = pl(conv_b[0])
    vec[0:16, V_DTB] = f(dt_bias[0]); vec[0:16, V_ALOG] = f(a_log[0]); vec[0:16, V_DSK] = f(d_skip[0])
    shared = {
        "w_in": f(w_in[0]), "w_mk": f(w_mem_k[0]), "w_mv": f(w_mem_v[0]), "w_bg": f(w_br_gm[0]),
        "w_bs": f(w_br_ssd[0]), "w_bx": f(w_br_xa[0]), "w_o": f(w_out[0]), "w_up": f(w_up[0]), "w_dn": f(w_down[0]),
        "consts": make_consts(), "vecs": vec, "gm_ws": f(gm_ws[0]),
        "r_lnw": f(gm_ln_w[0]).reshape(1, 512), "r_lnb": f(gm_ln_b[0]).reshape(1, 512), "r_bs": f(gm_bs[0]).reshape(1, 512),
        "r_dtb": f(dt_bias[0]).reshape(1, 16), "r_alog": f(a_log[0]).reshape(1, 16), "r_dsk": f(d_skip[0]).reshape(1, 16),
        "r_nfin": f(norm_final_w).reshape(1, 1024), "r_cw": f(conv_w[0]).reshape(1, 4 * 1536), "r_cb": f(conv_b[0]).reshape(1, 1536),
        "r_ws00": f(gm_ws[0][:, 0, 0]).reshape(1, 4), "r_bs0": f(gm_bs[0][:, 0]).reshape(1, 4),
    }
    in_maps = []
    for c in range(8):
        m = dict(shared)
        sl = slice(16 * c, 16 * c + 16)
        m["x"] = f(x_prompt[c]); m["xs"] = f(x_sample[sl, 0]); m["mem"] = f(mem_prompt[c])
        m["ck"] = f(cache_mem_k[0, sl]).reshape(16, 256, 512); m["cv"] = f(cache_mem_v[0, sl]).reshape(16, 256, 512)
        m["sconv"] = f(state_conv[0, sl]); m["sssm"] = f(state_ssm[0, sl]).reshape(16, 1024, 128)
        in_maps.append(m)
    nc = build(_stage)
    res = run_bass_kernel_spmd(nc, in_maps, core_ids=list(range(8)))
    DEBUG["exec_ns"] = getattr(res, "exec_time_ns", None)
    R = res.results
    g = lambda k: np.stack([np.asarray(R[c][k], dtype=np.float32) for c in range(8)])
    y_prompt = g("y")
    y_sample = g("ys").reshape(128, 1, 1024)
    mem_k = g("mk").reshape(1, 8, 256, 4, 128)
    mem_v = g("mv").reshape(1, 8, 256, 4, 128)
    conv_p = g("convp").reshape(1, 8, 3, 1536)
    ssm_p = g("ssmp").reshape(1, 8, 16, 64, 128)
    conv_s = g("convs").reshape(1, 128, 3, 1536)
    ssm_s = g("ssms").reshape(1, 128, 16, 64, 128)
    gv_s = g("gv").reshape(1, 128, 1, 512)
    return (y_prompt, y_sample, mem_k, mem_v, conv_p, ssm_p, conv_s, ssm_s, gv_s)
```
